# Optimizing a Trainium2 kernel written in Bass

```python
import jax, jax.numpy as jnp
from jax import lax
import numpy as np

D_MODEL = 1024
BATCH = 16
SEQ = 256
DEPTH = 2
DEC_BATCH = 2
DEC_SEQ = 1024
PAST_LEN = 256

GRID_W = 64
W_A = D_MODEL // 2
N_POOL_GROUPS = 4
POOL_GROUP = W_A // N_POOL_GROUPS
POOL_HALF = (1, 2, 4, 8)
N_HEADS_B = 8
HEAD_DIM_B = (D_MODEL // 2) // N_HEADS_B
W_B = N_HEADS_B * HEAD_DIM_B
WIN_H = 8
WIN_W = 16
W_C = D_MODEL // 2
W_D = D_MODEL // 2
CONV_C = 3
CONV_D = 31
N_EVEN = (DEPTH + 1) // 2
N_ODD = DEPTH // 2
IN_EVEN = 2 * W_A + 4 * W_B
IN_ODD = 4 * W_C + 3 * W_D
EPS = 1e-6

kernel_name = "hybrid_pool_natten_conv_dit_step"


def rmsnorm(x, g):
    xf = x.astype(jnp.float32)
    y = xf * lax.rsqrt(jnp.mean(xf * xf, axis=-1, keepdims=True) + EPS)
    return (y * g.astype(jnp.float32)).astype(x.dtype)


def layernorm(x, g, b):
    xf = x.astype(jnp.float32)
    mu = jnp.mean(xf, axis=-1, keepdims=True)
    var = jnp.mean(jnp.square(xf - mu), axis=-1, keepdims=True)
    y = (xf - mu) * lax.rsqrt(var + EPS)
    return (y * g.astype(jnp.float32) + b.astype(jnp.float32)).astype(x.dtype)


def modulated_norm(x, cond, norm_g, w_mod, b_mod):
    m = jax.nn.silu(cond) @ w_mod + b_mod
    shift, scale, gate = jnp.split(m, 3, axis=-1)
    return rmsnorm(x, norm_g) * (1 + scale) + shift, gate


def depthwise_conv(x, w, width):
    return lax.conv_general_dilated(x, w[:, None, :], window_strides=(1,),
                                    padding=[(width // 2, width // 2)],
                                    dimension_numbers=('NWC', 'WIO', 'NWC'),
                                    feature_group_count=x.shape[-1])


def pool_mixer(u, w_pool, pool_scale):
    bsz, t, _ = u.shape
    uf = u.astype(jnp.float32).reshape(bsz, t, N_POOL_GROUPS, POOL_GROUP)
    cs = jnp.concatenate([jnp.zeros((bsz, 1, N_POOL_GROUPS, POOL_GROUP), jnp.float32),
                          jnp.cumsum(uf, axis=1)], axis=1)
    pos = jnp.arange(t)[:, None]
    half = jnp.array(POOL_HALF, dtype=jnp.int32)[None, :]
    lo = jnp.clip(pos - half, 0, t)
    hi = jnp.clip(pos + half, 0, t)
    gidx = jnp.arange(N_POOL_GROUPS)[None, :]
    win_sum = cs[:, hi, gidx] - cs[:, lo, gidx]
    mean = win_sum / (hi - lo).astype(jnp.float32)[None, :, :, None]
    p = (mean - uf).astype(u.dtype)
    y = jnp.einsum('btgc,gcd->btgd', p, w_pool).reshape(bsz, t, W_A)
    return y * pool_scale


def context_attention(q, k, v):
    bsz, l = q.shape[:2]
    s = jnp.einsum('bqhd,bkhd->bhqk', q, k).astype(jnp.float32) * (HEAD_DIM_B ** -0.5)
    p = jax.nn.softmax(s, axis=-1).astype(v.dtype)
    return jnp.einsum('bhqk,bkhd->bqhd', p, v).reshape(bsz, l, W_B)


def neighbourhood_attention(q, k, v, kc, vc, rpb):
    bsz, t = q.shape[:2]
    rows = t // GRID_W
    kh = min(WIN_H, rows)
    kw = min(WIN_W, GRID_W)
    qg = q.reshape(bsz, rows, GRID_W, N_HEADS_B, HEAD_DIM_B)
    kg = k.reshape(bsz, rows, GRID_W, N_HEADS_B, HEAD_DIM_B)
    vg = v.reshape(bsz, rows, GRID_W, N_HEADS_B, HEAD_DIM_B)
    r = jnp.arange(rows)
    row_idx = jnp.clip(r - kh // 2, 0, rows - kh)[:, None] + jnp.arange(kh)[None, :]
    k_band = kg[:, row_idx]
    v_band = vg[:, row_idx]
    col = jnp.arange(GRID_W)
    col_start = jnp.clip(col - kw // 2, 0, GRID_W - kw)
    col_ok = (col[None, :] >= col_start[:, None]) & (col[None, :] < col_start[:, None] + kw)
    dr = row_idx - r[:, None] + (WIN_H - 1)
    dc = jnp.clip(col[None, :] - col[:, None], -(WIN_W - 1), WIN_W - 1) + (WIN_W - 1)
    bias = rpb[:, dr[:, None, :, None], dc[None, :, None, :]].astype(jnp.float32)
    scale = HEAD_DIM_B ** -0.5
    s_loc = jnp.einsum('brqhd,brkwhd->bhrqkw', qg, k_band).astype(jnp.float32) * scale + bias
    s_loc = jnp.where(col_ok[:, None, :], s_loc, -jnp.inf)
    s_ctx = jnp.einsum('brqhd,bhld->bhrql', qg, kc).astype(jnp.float32) * scale
    n_loc = kh * GRID_W
    s = jnp.concatenate([s_loc.reshape(bsz, N_HEADS_B, rows, GRID_W, n_loc), s_ctx], axis=-1)
    p = jax.nn.softmax(s, axis=-1).astype(v.dtype)
    p_loc = p[..., :n_loc].reshape(bsz, N_HEADS_B, rows, GRID_W, kh, GRID_W)
    p_ctx = p[..., n_loc:]
    o = (jnp.einsum('bhrqkw,brkwhd->brqhd', p_loc, v_band)
         + jnp.einsum('bhrql,bhld->brqhd', p_ctx, vc))
    return o.reshape(bsz, t, W_B)


def even_layer(x, cond, ctx_kv, norm_g, w_mod, b_mod, w_in, w_pool, pool_scale, rpb, w_out):
    bsz, t, _ = x.shape
    h, gate = modulated_norm(x, cond, norm_g, w_mod, b_mod)
    proj = h @ w_in
    u_a, g_a, q, k, v, g_b = jnp.split(
        proj, [W_A, 2 * W_A, 2 * W_A + W_B, 2 * W_A + 2 * W_B, 2 * W_A + 3 * W_B], axis=-1)
    q = q.reshape(bsz, t, N_HEADS_B, HEAD_DIM_B)
    k = k.reshape(bsz, t, N_HEADS_B, HEAD_DIM_B)
    v = v.reshape(bsz, t, N_HEADS_B, HEAD_DIM_B)
    a_out = pool_mixer(u_a, w_pool, pool_scale) * jax.nn.silu(g_a)
    if ctx_kv is None:
        att = context_attention(q, k, v)
        kv = (k.transpose(0, 2, 1, 3), v.transpose(0, 2, 1, 3))
    else:
        att = neighbourhood_attention(q, k, v, ctx_kv[0], ctx_kv[1], rpb)
        kv = None
    b_out = att * jax.nn.silu(g_b)
    y = jnp.concatenate([a_out, b_out], axis=-1) @ w_out
    return x + gate * y, kv


def odd_layer(x, cond, norm_g, w_mod, b_mod, w_in, conv_c, conv_d, conv_d_b, ln_g, ln_b, w_out):
    h, gate = modulated_norm(x, cond, norm_g, w_mod, b_mod)
    proj = h @ w_in
    b_c, c_c, x_c, g_c, a_d, b_d, g_d = jnp.split(
        proj, [W_C, 2 * W_C, 3 * W_C, 4 * W_C, 4 * W_C + W_D, 4 * W_C + 2 * W_D], axis=-1)
    c_out = b_c * depthwise_conv(c_c * x_c, conv_c, CONV_C) * jax.nn.silu(g_c)
    z = depthwise_conv(a_d * jax.nn.sigmoid(b_d), conv_d, CONV_D) + conv_d_b
    z = jax.nn.silu(layernorm(z, ln_g, ln_b))
    d_out = z * jax.nn.silu(g_d)
    y = jnp.concatenate([c_out, d_out], axis=-1) @ w_out
    return x + gate * y


def setup_inputs(seed: int = 0) -> dict:
    key = jax.random.key(seed)
    ks = jax.random.split(key, 24)
    nrm = lambda k, s: jax.random.normal(k, s, jnp.float32)
    D = D_MODEL
    return {
        "x_prompt": nrm(ks[0], (BATCH, SEQ, D)),
        "x_sample": nrm(ks[1], (DEC_BATCH, DEC_SEQ, D)),
        "cache_k": nrm(ks[2], (DEC_BATCH, N_EVEN, N_HEADS_B, PAST_LEN, HEAD_DIM_B)),
        "cache_v": nrm(ks[3], (DEC_BATCH, N_EVEN, N_HEADS_B, PAST_LEN, HEAD_DIM_B)),
        "c": nrm(ks[4], (DEC_BATCH, D)),
        "c_ctx": nrm(ks[5], (D,)),
        "norm_g": 1.0 + 0.02 * nrm(ks[6], (DEPTH, D)),
        "w_mod": 0.5 * D ** -0.5 * nrm(ks[7], (DEPTH, D, 3 * D)),
        "b_mod": 0.02 * nrm(ks[8], (DEPTH, 3 * D)),
        "w_in_even": D ** -0.5 * nrm(ks[9], (N_EVEN, D, IN_EVEN)),
        "w_pool": POOL_GROUP ** -0.5 * nrm(ks[10], (N_EVEN, N_POOL_GROUPS, POOL_GROUP, POOL_GROUP)),
        "pool_scale": 1.0 + 0.1 * nrm(ks[11], (N_EVEN, W_A)),
        "rpb": 0.1 * nrm(ks[12], (N_EVEN, N_HEADS_B, 2 * WIN_H - 1, 2 * WIN_W - 1)),
        "w_out_even": (W_A + W_B) ** -0.5 * nrm(ks[13], (N_EVEN, W_A + W_B, D)),
        "w_in_odd": D ** -0.5 * nrm(ks[14], (N_ODD, D, IN_ODD)),
        "conv_c": CONV_C ** -0.5 * nrm(ks[15], (N_ODD, CONV_C, W_C)),
        "conv_d": CONV_D ** -0.5 * nrm(ks[16], (N_ODD, CONV_D, W_D)),
        "conv_d_b": 0.02 * nrm(ks[17], (N_ODD, W_D)),
        "ln_g": 1.0 + 0.02 * nrm(ks[18], (N_ODD, W_D)),
        "ln_b": 0.02 * nrm(ks[19], (N_ODD, W_D)),
        "w_out_odd": (W_C + W_D) ** -0.5 * nrm(ks[20], (N_ODD, W_C + W_D, D)),
        "final_g": 1.0 + 0.02 * nrm(ks[21], (D,)),
    }


def reference(x_prompt, x_sample, cache_k, cache_v, c, c_ctx, norm_g, w_mod, b_mod,
              w_in_even, w_pool, pool_scale, rpb, w_out_even, w_in_odd, conv_c, conv_d,
              conv_d_b, ln_g, ln_b, w_out_odd, final_g):
    cond_ctx = c_ctx[None, None, :]
    cond_lat = c[:, None, :]
    xp, xs = x_prompt, x_sample
    new_k, new_v = [], []
    for layer in range(DEPTH):
        i = layer // 2
        common = (norm_g[layer], w_mod[layer], b_mod[layer])
        if layer % 2 == 0:
            ew = (w_in_even[i], w_pool[i], pool_scale[i], rpb[i], w_out_even[i])
            xp, kv = even_layer(xp, cond_ctx, None, *common, *ew)
            new_k.append(kv[0])
            new_v.append(kv[1])
            xs, _ = even_layer(xs, cond_lat, (cache_k[:, i], cache_v[:, i]), *common, *ew)
        else:
            ow = (w_in_odd[i], conv_c[i], conv_d[i], conv_d_b[i], ln_g[i], ln_b[i], w_out_odd[i])
            xp = odd_layer(xp, cond_ctx, *common, *ow)
            xs = odd_layer(xs, cond_lat, *common, *ow)
    y_prompt = rmsnorm(xp, final_g)
    y_sample = rmsnorm(xs, final_g)
    new_cache_k = jnp.stack(new_k, axis=1)
    new_cache_v = jnp.stack(new_v, axis=1)
    return (y_prompt, y_sample, new_cache_k, new_cache_v)
```

```python
import contextlib
import numpy as np
import concourse.bass as bass
import concourse.mybir as mybir
from concourse.bass_utils import run_bass_kernel_spmd

F32 = mybir.dt.float32
BF16 = mybir.dt.bfloat16
AF = mybir.ActivationFunctionType
ALU = mybir.AluOpType

NEG = -30000.0
EPS = 1e-6
TP, TS, TM, TW = 512, 320, 832, 896
EXT0 = 352
OWN0 = 32
SLOT = 896
NF, NB = 18, 30
PC_COND, PC_G, PC_BMOD, PC_PSC, PC_CC, PC_CD, PC_CDB, PC_LNG, PC_LNB, PC_FG = 0, 16, 48, 144, 148, 160, 284, 288, 292, 296
NPAR = 305
QR = {0: (16, 32), 1: (16, 288), 2: (16, 288), 3: (16, 304), 4: (16, 304), 5: (32, 304), 6: (32, 304)}
QOFF = {}
_o = 0
for _kb in range(7):
    QOFF[_kb] = _o
    _o += QR[_kb][1] - QR[_kb][0]
QTOT = _o
PU_SEQ = (8, 272, 544)
PUW = 872
PG_SEQ = (15, 286, 557)
PGW = 892


class Sched:
    def __init__(self, nc, es):
        self.nc = nc
        self.E = {'pe': nc.tensor, 'act': nc.scalar, 'dve': nc.vector, 'pool': nc.gpsimd, 'sp': nc.sync}
        self.sems = {}
        for e in ('pe', 'act', 'dve', 'pool'):
            self.sems[e] = es.enter_context(nc.semaphore('s_' + e))
        self.cnt = {e: 0 for e in ('pe', 'act', 'dve', 'pool')}
        self.dma_pool = {}
        for q, n in (('sp', 40), ('pool', 40)):
            lst = []
            for i in range(n):
                nm = 'd_%s%d' % (q, i)
                self.sems[nm] = es.enter_context(nc.semaphore(nm))
                lst.append([nm, 0])
            self.dma_pool[q] = lst
        self.dma_rr = {q: 0 for q in self.dma_pool}
        self.waited = {}
        self.lastw = {}
        self.readers = {}

    @staticmethod
    def _is_ps(k):
        return isinstance(k, tuple) and k[0] == 'ps'

    def _deps(self, reads, writes, eng=None):
        d = {}

        def add(tok):
            if tok is None:
                return
            n, v = tok
            if d.get(n, 0) < v:
                d[n] = v
        for k in reads:
            add(self.lastw.get(k))
            if self._is_ps(k):
                for n, v in self.readers.get(k, {}).items():
                    if n != eng:
                        add((n, v))
        for k in writes:
            add(self.lastw.get(k))
            for n, v in self.readers.get(k, {}).items():
                add((n, v))
        return d

    def _wait(self, eng, d):
        for n, v in d.items():
            if self.waited.get((eng, n), 0) < v:
                self.E[eng].wait_ge(self.sems[n], v)
                self.waited[(eng, n)] = v

    def _commit(self, tok, reads, writes):
        for k in writes:
            self.lastw[k] = tok
            self.readers[k] = {}
        for k in reads:
            r = self.readers.setdefault(k, {})
            if r.get(tok[0], 0) < tok[1]:
                r[tok[0]] = tok[1]

    def op(self, eng, fn, reads=(), writes=()):
        self._wait(eng, self._deps(reads, writes, eng))
        ins = fn()
        self.cnt[eng] += 1
        tok = (eng, self.cnt[eng])
        ins.then_inc(self.sems[eng], 1)
        self._commit(tok, reads, writes)

    def group(self, eng, fns, reads=(), writes=()):
        self._wait(eng, self._deps(reads, writes, eng))
        ins = None
        for fn in fns:
            ins = fn()
        self.cnt[eng] += 1
        tok = (eng, self.cnt[eng])
        ins.then_inc(self.sems[eng], 1)
        self._commit(tok, reads, writes)

    def dma(self, q, out, in_, reads=(), writes=(), after=(), **kw):
        d = self._deps(reads, writes)
        for k in after:
            tok = self.lastw.get(k)
            if tok is not None and d.get(tok[0], 0) < tok[1]:
                d[tok[0]] = tok[1]
        self._wait(q, d)
        pool = self.dma_pool[q]
        i = self.dma_rr[q]
        self.dma_rr[q] = (i + 1) % len(pool)
        ent = pool[i]
        if ent[1] > 0 and self.waited.get((q, ent[0]), 0) < ent[1]:
            self.E[q].wait_ge(self.sems[ent[0]], ent[1])
            self.waited[(q, ent[0])] = ent[1]
        ins = self.E[q].dma_start(out=out, in_=in_, **kw)
        ent[1] += 16
        ins.then_inc(self.sems[ent[0]], 16)
        self._commit((ent[0], ent[1]), reads, writes)

    def finish(self):
        for q, pool in self.dma_pool.items():
            for nm, v in pool:
                if v > 0 and self.waited.get(('sp', nm), 0) < v:
                    self.E['sp'].wait_ge(self.sems[nm], v)
                    self.waited[('sp', nm)] = v


class SlotPool:
    def __init__(self, name, n):
        self.name = name
        self.free = list(range(n))
        self.peak = 0
        self.n = n

    def alloc(self, k=1, consecutive=False):
        if consecutive:
            fs = sorted(self.free)
            for i in range(len(fs) - k + 1):
                if fs[i + k - 1] - fs[i] == k - 1:
                    got = fs[i:i + k]
                    for g in got:
                        self.free.remove(g)
                    self.peak = max(self.peak, self.n - len(self.free))
                    return got
            raise RuntimeError('no consecutive slots in ' + self.name)
        assert len(self.free) >= k, 'out of slots in %s' % self.name
        got = [self.free.pop(0) for _ in range(k)]
        self.peak = max(self.peak, self.n - len(self.free))
        return got if k > 1 else got[0]

    def release(self, s):
        if isinstance(s, (list, tuple)):
            for x in s:
                self.release(x)
        else:
            assert s not in self.free
            self.free.append(s)
            self.free.sort()


def build_program(debug=None, stop=None):
    nc = bass.Bass("TRN2", target_bir_lowering=False)
    dt_in = lambda n, s: nc.dram_tensor(n, list(s), F32, kind="ExternalInput").ap()
    dt_out = lambda n, s: nc.dram_tensor(n, list(s), F32, kind="ExternalOutput").ap()
    xpT_d = dt_in("xpT", (1024, 512))
    xwT_d = dt_in("xwT", (1024, 896))
    ck_d = dt_in("ck", (512, 256))
    cv_d = dt_in("cv", (2, 256, 512))
    par_d = dt_in("params", (128, NPAR))
    ident_d = dt_in("ident", (128, 128))
    vmask_d = dt_in("vmask", (128, 320))
    invc_d = dt_in("invcnt", (128, 4 * PUW))
    t2r_d = dt_in("t2r", (128, 8, QTOT))
    fgb_d = dt_in("fgb", (128, 1024))
    wmod_d = dt_in("w_mod", (2, 1024, 3072))
    wine_d = dt_in("w_in_even", (1024, 3072))
    wpool_d = dt_in("w_pool", (4, 128, 128))
    woe_d = dt_in("w_out_even", (1024, 1024))
    wino_d = dt_in("w_in_odd", (1024, 3584))
    woo_d = dt_in("w_out_odd", (1024, 1024))
    yp_d = dt_out("yp", (512, 1024))
    ys_d = dt_out("ys", (256, 1024))
    nk_d = dt_out("nk", (512, 512))
    nv_d = dt_out("nv", (2, 256, 512))
    dbg_d = {}
    if debug:
        for nm, shp in debug.items():
            dbg_d[nm] = dt_out("dbg_" + nm, shp)

    with contextlib.ExitStack() as es:
        sb = lambda n, s, d: es.enter_context(nc.sbuf_tensor("sb_" + n, list(s), d))
        S = Sched(nc, es)
        poolf = sb("poolf", (128, NF, SLOT), F32)
        poolb = sb("poolb", (128, NB * SLOT), BF16)
        hT = sb("hT", (128, 8, 1408), BF16)
        ring = sb("ring", (128, 3, 8, 512), BF16)
        kT_p = sb("kT_p", (128, 4, 512), BF16)
        v_p = sb("v_p", (128, 2, 4, 512), BF16)
        v_w = sb("v_w", (128, 2, 7, 512), BF16)
        kcT = sb("kcT", (128, 4, 256), BF16)
        cvb = sb("cvb", (128, 2, 2, 512), BF16)
        oh = sb("oh", (128, 2, 128), BF16)
        t2rb = sb("t2rb", (128, 2, QTOT), BF16)
        wpool_b = sb("wpool_b", (128, 4, 128), BF16)
        par = sb("par", (128, NPAR), F32)
        mT = sb("mT", (128, 2, 48), F32)
        gs = sb("gs", (128, 2, 16), F32)
        scond = sb("scond", (128, 16), BF16)
        stt = sb("stt", (128, 24), F32)
        dummy = sb("dummy", (128, 4), F32)
        sst = sb("sst", (128, 44), F32)
        ones_f = sb("ones_f", (128, 128), F32)
        ident_f = sb("ident_f", (128, 128), F32)
        ident_b = sb("ident_b", (128, 128), BF16)
        ones_b = sb("ones_b", (128, 128), BF16)
        vmask = sb("vmask", (128, 320), F32)
        ps = es.enter_context(nc.psum_tensor("ps", [128, 4096], F32))
        FP = SlotPool('F', NF)
        BP = SlotPool('B', NB)

        pe, act, dve, gp, sp = nc.tensor, nc.scalar, nc.vector, nc.gpsimd, nc.sync

        def fs(s, a=0, b=SLOT):
            return poolf[:, s, a:b]

        def bs(s, a=0, b=SLOT):
            return poolb[:, s * SLOT + a:s * SLOT + b]

        def bsp(pr, s, a, b):
            return poolb[pr, s * SLOT + a:s * SLOT + b]

        def bank(b, a=0, n=512):
            return ps[:, b * 512 + a: b * 512 + a + n]

        def pair(b, a=0, n=1024):
            return ps[:, b * 512 + a: b * 512 + a + n]

        def pcol(off, n=1):
            return par[:, off:off + n]

        S.dma('sp', par[:, :], par_d[:, :], writes=['par'])
        S.dma('sp', ident_f[:, :], ident_d[:, :], writes=['ident_f'])
        S.dma('sp', vmask[:, :], vmask_d[:, :], writes=['vmask'])
        S.op('dve', lambda: dve.memset(ones_b[:, :], 1.0), writes=['ones_b'])
        S.op('dve', lambda: dve.memset(ps[:, 7 * 512 + 96:8 * 512], 0.0), writes=[('ps', 7)])
        S.op('dve', lambda: dve.memset(ones_f[:, :], 1.0), writes=['ones_f'])
        S.op('dve', lambda: dve.tensor_copy(out=ident_b[:, :], in_=ident_f[:, :]), reads=['ident_f'], writes=['ident_b'])
        S.op('act', lambda: act.activation(out=scond[:, :], in_=par[:, PC_COND:PC_COND + 16], func=AF.Silu),
             reads=['par'], writes=['scond'])

        def act_preload(func, col):
            S.op('act', lambda: act.activation(out=dummy[:, col:col + 1], in_=par[:, NPAR - 1:NPAR], func=func),
                 reads=['par'], writes=[('dummy', col)])

        act_preload(AF.Ln, 0)
        pieces = []

        def wpiece(w2d, c0):
            return w2d[:, c0:c0 + 512].rearrange("(c p) n -> p c n", p=128)
        for pc in range(4):
            pieces.append(('wm0', pc, wpiece(wmod_d[0], pc * 512)))
        for pc in range(6):
            pieces.append(('wie', pc, wpiece(wine_d, pc * 512)))
            if pc in (3, 4):
                pieces.append(('wm0', pc + 1, wpiece(wmod_d[0], (pc + 1) * 512)))
        for pc in range(4):
            pieces.append(('wm1', pc, wpiece(wmod_d[1], pc * 512)))
        for pc in range(2):
            pieces.append(('wm1', 4 + pc, wpiece(wmod_d[1], (4 + pc) * 512)))
        for pc in range(2):
            pieces.append(('woe', pc, wpiece(woe_d, pc * 512)))
        for pc in (1, 2, 0, 3, 4, 5, 6):
            pieces.append(('wio', pc, wpiece(wino_d, pc * 512)))
        for pc in range(2):
            pieces.append(('woo', pc, wpiece(woo_d, pc * 512)))
        piece_idx = {(a, b): i for i, (a, b, _) in enumerate(pieces)}
        issued = [0]
        xtra = BP.alloc(5, consecutive=True)
        xtra_ap = poolb[:, xtra[0] * SLOT:xtra[0] * SLOT + 4096].rearrange("p (k n) -> p k n", k=8)
        XK = [('B', x_) for x_ in xtra]

        def slot_of(i):
            return i if i < 3 else ('X' if i == 3 else (i - 1) % 3)

        ring_limit = [None]

        def ring_issue_upto(i, after=()):
            if ring_limit[0] is not None:
                i = min(i, ring_limit[0])
            while issued[0] <= i and issued[0] < len(pieces):
                p = issued[0]
                sl = slot_of(p)
                if sl == 'X':
                    S.dma('pool', xtra_ap, pieces[p][2], writes=XK, after=list(after))
                else:
                    S.dma('pool', ring[:, sl, :, :], pieces[p][2], writes=[('ring', sl)], after=list(after))
                issued[0] += 1

        def ring_get(kind, pc):
            i = piece_idx[(kind, pc)]
            ring_issue_upto(i + 2)
            return slot_of(i)

        ring_issue_upto(0)
        S.op('pool', lambda: gp.memset(v_p[:, :, :, :], 0.0), writes=['v_p'])
        S.op('pool', lambda: gp.memset(v_w[:, :, :, :], 0.0), writes=[('v_w', wb) for wb in range(7)])
        S.op('pool', lambda: gp.memset(oh[:, :, :], 0.0), writes=['oh'])
        S.op('pool', lambda: gp.memset(oh[:, 0, 0:64], 1.0), writes=['oh'])
        S.op('pool', lambda: gp.memset(oh[:, 1, 64:128], 1.0), writes=['oh'])

        MODB = 7

        def mod_piece(l, pc):
            slot = ring_get('wm%d' % l, pc)
            fns = []
            for n4 in range(4):
                n = pc * 4 + n4
                for k in range(8):
                    wsrc = xtra_ap if slot == 'X' else ring[:, slot, :, :]
                    fns.append(lambda n=n, n4=n4, k=k, wsrc=wsrc: pe.matmul(
                        bank(MODB, l * 48 + n * 2, 2), lhsT=wsrc[:, k, n4 * 128:(n4 + 1) * 128],
                        rhs=scond[:, k * 2:k * 2 + 2], start=(k == 0), stop=(k == 7)))
            S.group('pe', fns, reads=(XK if slot == 'X' else [('ring', slot)]) + ['scond'], writes=[('ps', MODB)])
            if slot == 'X':
                BP.release(xtra)

        def mod_finish(l):
            S.op('dve', lambda: dve.tensor_tensor(out=mT[:, l, 0:32], in0=bank(MODB, l * 48, 32),
                                                  in1=par[:, PC_BMOD + l * 48:PC_BMOD + l * 48 + 32], op=ALU.add),
                 reads=[('ps', MODB), 'par'], writes=[('mT', l)])
            S.op('dve', lambda: dve.scalar_tensor_tensor(out=gs[:, l, :], in0=mT[:, l, 16:32], scalar=1.0,
                                                         in1=par[:, PC_G + l * 16:PC_G + (l + 1) * 16],
                                                         op0=ALU.add, op1=ALU.mult),
                 reads=[('mT', l), 'par'], writes=[('gs', l)])

        def mod_finish_gate(l):
            S.op('dve', lambda: dve.tensor_tensor(out=mT[:, l, 32:48], in0=bank(MODB, l * 48 + 32, 16),
                                                  in1=par[:, PC_BMOD + l * 48 + 32:PC_BMOD + (l + 1) * 48], op=ALU.add),
                 reads=[('ps', MODB), 'par'], writes=[('mTg', l)])

        def shift_col(l, c, q):
            return mT[:, l, c * 2 + q:c * 2 + q + 1]

        def gs_col(l, c, q):
            return gs[:, l, c * 2 + q:c * 2 + q + 1]

        def gate_col(l, c, q):
            return mT[:, l, 32 + c * 2 + q:32 + c * 2 + q + 1]

        if stop == 0:
            S.finish()
            return nc
        xm = FP.alloc(8, consecutive=True)
        xw = FP.alloc(8, consecutive=True)
        XM = [('F', s) for s in xm]
        XW = [('F', s) for s in xw]
        poolbf = poolb.bitcast(F32)
        for c in range(8):
            S.dma('sp', fs(xm[c], 0, 512), xpT_d[c * 128:(c + 1) * 128, :], writes=[XM[c]])
        for c in range(8):
            S.dma('sp', fs(xw[c], 0, TW), xwT_d[c * 128:(c + 1) * 128, :], writes=[XW[c]])
            if c == 3:
                ring_issue_upto(3, after=[XW[3]])

        def rms_stats(chunks, keys, col_ranges, psbanks):
            sq = BP.alloc(2)
            for c in range(8):
                q = sq[c % 2]
                ncols = max(r[0] + r[1] for r in col_ranges)
                S.op('act', lambda c=c, q=q, ncols=ncols: act.activation(out=bs(q, 0, ncols), in_=fs(chunks[c], 0, ncols), func=AF.Square),
                     reads=[keys[c]], writes=[('B', q)])
                fns = [lambda c=c, q=q, r=r: pe.matmul(bank(r[2], r[3], r[1]), lhsT=ones_b[:, :], rhs=bs(q, r[0], r[0] + r[1]),
                                                      start=(c == 0), stop=(c == 7)) for r in col_ranges]
                S.group('pe', fns, reads=[('B', q), 'ones_b'], writes=[('ps', b) for b in psbanks])
            BP.release(sq)

        def rstd_from(bank0, ncols, scale, to_psum=False):
            r = FP.alloc()
            nb = (ncols + 511) // 512
            keys = [('ps', bank0 + i) for i in range(nb)]
            S.op('act', lambda: act.activation(out=fs(r, 0, ncols), in_=ps[:, bank0 * 512:bank0 * 512 + ncols], func=AF.Ln,
                                               bias=par[:, NPAR - 1:NPAR], scale=scale),
                 reads=keys + ['par'], writes=[('F', r)])
            if to_psum:
                S.op('act', lambda: act.activation(out=ps[:, bank0 * 512:bank0 * 512 + ncols], in_=fs(r, 0, ncols), func=AF.Exp, scale=-0.5),
                     reads=[('F', r)], writes=keys)
                FP.release(r)
                return None
            S.op('act', lambda: act.activation(out=fs(r, 0, ncols), in_=fs(r, 0, ncols), func=AF.Exp, scale=-0.5),
                 reads=[('F', r)], writes=[('F', r)])
            return r

        def make_h_mult(chunks, keys, rbank, lo, hi, tmps):
            rk = [('ps', rbank + i) for i in range(lo // 512, (hi + 511) // 512)]
            for c in range(8):
                if tmps is None:
                    oap, tk = fs(chunks[c], lo, hi), keys[c]
                else:
                    oap, tk = tmps[c][0](lo, hi), tmps[c][1]
                S.op('dve', lambda c=c, oap=oap: dve.tensor_tensor(out=oap, in0=fs(chunks[c], lo, hi),
                                                                    in1=ps[:, rbank * 512 + lo:rbank * 512 + hi], op=ALU.mult),
                     reads=[keys[c]] + rk, writes=[tk] if not isinstance(tk, list) else tk)

        def make_h_affine(l, chunks, keys, regions, tmps, eng_of):
            for c in range(8):
                for ri, (c0, n, q, h0) in enumerate(regions):
                    if tmps is None:
                        iap, tk = fs(chunks[c], c0, c0 + n), keys[c]
                    else:
                        iap, tk = tmps[c][0](c0, c0 + n), tmps[c][1]
                    rd = (tk if isinstance(tk, list) else [tk]) + [('mT', l), ('gs', l)]
                    e_ = eng_of(c, ri)
                    if e_ == 'act':
                        S.op('act', lambda c=c, iap=iap, n=n, q=q, h0=h0: act.activation(
                            out=hT[:, c, h0:h0 + n], in_=iap, func=AF.Identity, bias=shift_col(l, c, q), scale=gs_col(l, c, q)),
                            reads=rd, writes=[('hTp' if h0 < 512 else 'hTs', c)])
                    else:
                        eo = dve if e_ == 'dve' else gp
                        S.op(e_, lambda c=c, iap=iap, n=n, q=q, h0=h0, eo=eo: eo.tensor_scalar(
                            out=hT[:, c, h0:h0 + n], in0=iap, scalar1=gs_col(l, c, q), scalar2=shift_col(l, c, q),
                            op0=ALU.mult, op1=ALU.add),
                            reads=rd, writes=[('hTp' if h0 < 512 else 'hTs', c)])

        def make_h(l, chunks, keys, rbank, regions, tmp, pool_regions=()):
            lo = min(r[0] for r in regions)
            hi = max(r[0] + r[1] for r in regions)
            tmps = None if tmp is None else [((lambda a, b, t=t: fs(t, a, b)), ('F', t)) for t in tmp]
            make_h_mult(chunks, keys, rbank, lo, hi, tmps)
            make_h_affine(l, chunks, keys, regions, tmps, lambda c, ri: 'pool' if ri in pool_regions else 'act')

        if stop == 1:
            S.finish()
            return nc
        rms_stats(xm, XM, [(0, 512, 2, 0)], [2])
        rms_stats(xw, XW, [(0, 512, 3, 0), (512, 384, 4, 0)], [3, 4])
        S.op('dve', lambda: dve.tensor_copy(out=poolf[:, xm[0]:xm[0] + 8, 512:832], in_=poolf[:, xw[0]:xw[0] + 8, EXT0:EXT0 + 320]),
             reads=XW, writes=XM)
        mod_piece(0, 0)
        mod_piece(0, 1)
        rstd_from(2, 512, 1.0 / 1024, to_psum=True)
        rstd_from(3, 896, 1.0 / 1024, to_psum=True)
        mod_piece(0, 2)
        rp = FP.alloc()
        S.op('act', lambda: act.copy(out=fs(rp, 0, 512), in_=bank(2, 0, 512)), reads=[('ps', 2)], writes=[('F', rp)])
        make_h_mult(xw, XW, 3, 0, 896, None)
        tB = BP.alloc(16, consecutive=True)
        tmpsP = [((lambda a, b_, k=k: poolbf[:, (tB[0] + 2 * k) * 448 + a:(tB[0] + 2 * k) * 448 + b_]),
                  [('B', tB[2 * k]), ('B', tB[2 * k + 1])]) for k in range(8)]
        for c in range(8):
            S.op('pool', lambda c=c: gp.tensor_tensor(out=tmpsP[c][0](0, 512), in0=fs(xm[c], 0, 512), in1=fs(rp, 0, 512), op=ALU.mult),
                 reads=[XM[c], ('F', rp)], writes=tmpsP[c][1])
        mod_piece(0, 3)
        mod_finish(0)
        make_h_affine(0, xw, XW, [(0, 896, 1, 512)], None, lambda c, ri: 'act' if c < 4 else 'dve')
        make_h_affine(0, xm, XM, [(0, 512, 0, 0)], tmpsP, lambda c, ri: 'pool' if c < 4 else ('act' if c < 6 else 'dve'))
        FP.release(xw)
        FP.release(rp)
        BP.release(tB)
        HTP = [('hTp', c) for c in range(8)]
        HTS = [('hTs', c) for c in range(8)]
        HT = HTP + HTS

        if stop == 2:
            S.finish()
            return nc
        def proj_P(slot, n4, pb, fine=False):
            fns = [lambda k=k: pe.matmul(bank(pb, 0, 512), lhsT=ring[:, slot, k, n4 * 128:(n4 + 1) * 128],
                                         rhs=hT[:, k, 0:512], start=(k == 0), stop=(k == 7)) for k in range(8)]
            if fine:
                for k in range(8):
                    S.group('pe', [fns[k]], reads=[('ring', slot), HTP[k]], writes=[('ps', pb)])
            else:
                S.group('pe', fns, reads=[('ring', slot)] + HTP, writes=[('ps', pb)])

        def proj_S(slot, n4, pb, sample_cols=(864, 320), fine=False):
            s0, sn = sample_cols[0], sample_cols[1]
            so = sample_cols[2] if len(sample_cols) > 2 else 0
            fns = [lambda k=k: pe.matmul(bank(pb + 1, so, sn), lhsT=ring[:, slot, k, n4 * 128:(n4 + 1) * 128],
                                         rhs=hT[:, k, s0:s0 + sn], start=(k == 0), stop=(k == 7)) for k in range(8)]
            if fine:
                for k in range(8):
                    S.group('pe', [fns[k]], reads=[('ring', slot), HTS[k]], writes=[('ps', pb + 1)])
            else:
                S.group('pe', fns, reads=[('ring', slot)] + HTS, writes=[('ps', pb + 1)])

        def proj_chunk(slot, n4, pb, sample_cols=(864, 320), fine=False):
            proj_P(slot, n4, pb, fine=fine)
            proj_S(slot, n4, pb, sample_cols, fine=fine)

        pbs = [0, 2, 4]
        pbi = [0]

        bg_hook = [None]

        def next_pb():
            b = pbs[pbi[0] % len(pbs)]
            pbi[0] += 1
            if bg_hook[0] is not None:
                bg_hook[0]()
            return b

        upad = FP.alloc(4)
        slot = ring_get('wie', 0)
        for g in range(4):
            u = upad[g]
            S.op('pool', lambda u=u: gp.memset(fs(u, 0, PUW), 0.0), writes=[('F', u)])
            pb = next_pb()
            proj_chunk(slot, g, pb, fine=(g == 0))
            S.op('act', lambda u=u, pb=pb: act.copy(out=fs(u, 8, 8 + 528).rearrange("p (s t) -> p s t", s=2)[:, :, 0:256],
                                                    in_=bank(pb).rearrange("p (s t) -> p s t", s=2)),
                 reads=[('ps', pb)], writes=[('F', u)])
            S.op('dve', lambda u=u, pb=pb: dve.tensor_tensor(out=fs(u, 544, 864), in0=bank(pb + 1, 0, 320), in1=vmask[:, :], op=ALU.mult),
                 reads=[('ps', pb + 1), 'vmask'], writes=[('F', u)])
        abo = BP.alloc(8)

        PPS = {}

        def pool_pre_gen(g):
            u = upad[g]
            ic = FP.alloc()
            S.dma('sp', fs(ic, 0, PUW), invc_d[:, g * PUW:(g + 1) * PUW], writes=[('F', ic)])
            wa, wb_ = FP.alloc(), FP.alloc()
            S.op('dve', lambda u=u, wa=wa: dve.tensor_tensor(out=fs(wa, 1, PUW), in0=fs(u, 0, PUW - 1), in1=fs(u, 1, PUW), op=ALU.add),
                 reads=[('F', u)], writes=[('F', wa)])
            yield
            cur, nxt = wa, wb_
            lo = 1
            for step in range(g):
                sh = 1 << step
                a0, a1 = lo + sh, PUW - sh
                S.op('dve', lambda cur=cur, nxt=nxt, a0=a0, a1=a1, sh=sh: dve.tensor_tensor(
                    out=fs(nxt, a0, a1), in0=fs(cur, a0 - sh, a1 - sh), in1=fs(cur, a0 + sh, a1 + sh), op=ALU.add),
                    reads=[('F', cur)], writes=[('F', nxt)])
                yield
                cur, nxt = nxt, cur
                lo = a0
            S.op('dve', lambda cur=cur, ic=ic: dve.tensor_tensor(out=fs(cur, 8, 864), in0=fs(cur, 8, 864), in1=fs(ic, 8, 864), op=ALU.mult),
                 reads=[('F', cur), ('F', ic)], writes=[('F', cur)])
            yield
            pp = BP.alloc()
            S.op('dve', lambda cur=cur, u=u, pp=pp: dve.tensor_tensor(out=bs(pp, 8, 864), in0=fs(cur, 8, 864), in1=fs(u, 8, 864), op=ALU.subtract),
                 reads=[('F', cur), ('F', u)], writes=[('B', pp)])
            yield
            FP.release([ic, wa, wb_])
            PPS[g] = pp
            PRE_DONE.add(g)

        PRE_DONE = set()
        bg = []

        def bg_step(n=1):
            for _ in range(n):
                while bg:
                    try:
                        next(bg[0])
                        break
                    except StopIteration:
                        bg.pop(0)

        def pool_pre(g):
            bg.append(pool_pre_gen(g))
            bg_hook[0] = bg_step

        def pool_need(g):
            while g not in PRE_DONE:
                bg_step()

        def pool_post(g):
            pool_need(g)
            pp = PPS[g]
            pb = next_pb()
            fns = [lambda pp=pp, pb=pb, g=g, r=r: pe.matmul(ps[:, pb * 512 + r[1]:pb * 512 + r[1] + r[2]], lhsT=wpool_b[:, g, :],
                                                           rhs=bs(pp, r[0], r[0] + r[2]), start=True, stop=True)
                   for r in ((8, 0, 256), (272, 256, 256), (544, 512, 320))]
            S.group('pe', fns, reads=[('B', pp), 'wpool_b'], writes=[('ps', pb), ('ps', pb + 1)])
            S.op('dve', lambda g=g, pb=pb: dve.scalar_tensor_tensor(out=bs(abo[g], 0, TM), in0=pair(pb, 0, TM), scalar=pcol(PC_PSC + g),
                                                                    in1=bs(siluA[g], 0, TM), op0=ALU.mult, op1=ALU.mult),
                 reads=[('ps', pb), ('ps', pb + 1), 'par', ('B', siluA[g])], writes=[('B', abo[g])])
            BP.release(pp)

        if stop == 3:
            S.finish()
            return nc
        S.dma('pool', wpool_b[:, :, :], wpool_d.rearrange("g c d -> c g d"), writes=['wpool_b'])
        siluA = BP.alloc(4)
        slot = ring_get('wie', 1)
        for g in range(4):
            pb = next_pb()
            proj_chunk(slot, g, pb, sample_cols=(864 + 16, 288, 16))
            S.op('act', lambda g=g, pb=pb: act.activation(out=bs(siluA[g], 0, TM), in_=pair(pb, 0, TM), func=AF.Silu),
                 reads=[('ps', pb), ('ps', pb + 1)], writes=[('B', siluA[g])])
        pool_pre(0)
        pool_pre(1)
        qT = BP.alloc(8)
        slot = ring_get('wie', 2)
        for g in range(4):
            pb = next_pb()
            proj_chunk(slot, g, pb, sample_cols=(864 + 16, 288, 16))
            for hh in range(2):
                qs = qT[2 * g + hh]
                pr = slice(hh * 64, hh * 64 + 64)
                S.op('pool', lambda qs=qs: gp.memset(bs(qs, 0, TM), 0.0), writes=[('B', qs)])
                S.op('dve', lambda qs=qs, pr=pr, pb=pb: dve.tensor_scalar(out=bsp(pr, qs, 0, TM), in0=ps[pr, pb * 512:pb * 512 + TM], scalar1=0.125,
                                                                        scalar2=None, op0=ALU.mult),
                     reads=[('ps', pb), ('ps', pb + 1)], writes=[('B', qs)])
        pool_pre(2)
        pool_pre(3)


        if stop == 4:
            S.finish()
            return nc
        def t2r_load(h):
            S.dma('pool', t2rb[:, h % 2, :], t2r_d[:, h, :], writes=[('t2rb', h % 2)])
        ck_tm = []

        def attn_table_loads():
            t2r_load(0)
            for lb in range(2):
                for hh in range(2):
                    S.dma('pool', cvb[:, hh, lb, :], cv_d[hh, lb * 128:(lb + 1) * 128, :], writes=[('cvb', lb)])
            S.dma('pool', kcT[:, :, :], ck_d.rearrange("(hp p) l -> p hp l", p=128), writes=['kcT'])

        kT_w = BP.alloc(4)
        kvst = FP.alloc(3)
        ktp_pend = []
        for which, pcn in (('k', 3), ('v', 4)):
            slot = ring_get('wie', pcn)
            if which == 'k':
                attn_table_loads()
            out_d = nk_d if which == 'k' else nv_d
            if which == 'k':
                for g in range(4):
                    pb = next_pb()
                    fns = [lambda k=k, g=g, pb=pb: pe.matmul(bank(pb, 0, 512), lhsT=ring[:, slot, k, g * 128:(g + 1) * 128],
                                                             rhs=hT[:, k, 0:512], start=(k == 0), stop=(k == 7)) for k in range(8)]
                    S.group('pe', fns, reads=[('ring', slot)] + HTP, writes=[('ps', pb)])
                    st = kvst[g % 3]
                    S.op('act', lambda st=st, pb=pb: act.copy(out=fs(st, 0, 512), in_=bank(pb)), reads=[('ps', pb)], writes=[('F', st)])
                    S.dma('sp', nk_d[g * 128:(g + 1) * 128, :], fs(st, 0, 512), reads=[('F', st)])
                    S.op('dve', lambda g=g, pb=pb: dve.tensor_copy(out=kT_p[:, g, :], in_=bank(pb)), reads=[('ps', pb)], writes=['kT_p'])
            for tb in (range(4) if which == 'v' else ()):
                pb = next_pb()
                fns = [lambda k=k, tb=tb, pb=pb: pe.matmul(bank(pb, 0, 512), lhsT=hT[:, k, tb * 128:(tb + 1) * 128],
                                                           rhs=ring[:, slot, k, :], start=(k == 0), stop=(k == 7)) for k in range(8)]
                S.group('pe', fns, reads=[('ring', slot)] + HTP, writes=[('ps', pb)])
                st = kvst[tb % 3]
                S.op('act', lambda st=st, pb=pb: act.copy(out=fs(st, 0, 512), in_=bank(pb)), reads=[('ps', pb)], writes=[('F', st)])
                sq, t0 = tb // 2, (tb % 2) * 128
                S.dma('sp', out_d[sq, t0:t0 + 128, :], fs(st, 0, 512), reads=[('F', st)])
                if which == 'k':
                    def k_transp(tb=tb, st=st):
                        pb2 = 6
                        fns = [lambda c4=c4: pe.transpose(out=bank(pb2, c4 * 128, 128), in_=fs(st, c4 * 128, c4 * 128 + 128),
                                                          identity=ident_f[:, :]) for c4 in range(4)]
                        S.group('pe', fns, reads=[('F', st), 'ident_f'], writes=[('ps', pb2)])
                        S.op('dve', lambda: dve.tensor_copy(out=kT_p[:, :, tb * 128:(tb + 1) * 128],
                                                            in_=bank(pb2).rearrange("p (c t) -> p c t", c=4)),
                             reads=[('ps', pb2)], writes=['kT_p'])
                    if ktp_pend:
                        ktp_pend.pop(0)()
                    ktp_pend.append(k_transp)
                else:
                    for hh in range(2):
                        S.op('dve', lambda tb=tb, pb=pb, hh=hh: dve.tensor_copy(
                            out=v_p[:, hh, tb, :].rearrange("p (g e d) -> p g e d", g=4, e=2)[:, :, hh, :],
                            in_=bank(pb).rearrange("p (g e d) -> p g e d", g=4, e=2)[:, :, hh, :]),
                            reads=[('ps', pb)], writes=['v_p'])
            if which == 'k':
                for g in range(4):
                    pb = next_pb()
                    if g == 1 and ktp_pend:
                        ktp_pend.pop(0)()
                    fns = []
                    for k in range(8):
                        fns.append(lambda k=k, g=g, pb=pb: pe.matmul(bank(pb, 0, 512), lhsT=ring[:, slot, k, g * 128:(g + 1) * 128],
                                                                     rhs=hT[:, k, 512:1024], start=(k == 0), stop=(k == 7)))
                    for k in range(8):
                        fns.append(lambda k=k, g=g, pb=pb: pe.matmul(bank(pb + 1, 0, 384), lhsT=ring[:, slot, k, g * 128:(g + 1) * 128],
                                                                     rhs=hT[:, k, 1024:1408], start=(k == 0), stop=(k == 7)))
                    S.group('pe', fns, reads=[('ring', slot)] + HT, writes=[('ps', pb), ('ps', pb + 1)])
                    S.op('act', lambda g=g, pb=pb: act.copy(out=bs(kT_w[g], 0, TW), in_=pair(pb, 0, TW)),
                         reads=[('ps', pb), ('ps', pb + 1)], writes=[('B', kT_w[g])])
            else:
                for wb in range(7):
                    pb = next_pb()
                    fns = [lambda k=k, wb=wb, pb=pb: pe.matmul(bank(pb, 0, 512), lhsT=hT[:, k, 512 + wb * 128:512 + (wb + 1) * 128],
                                                               rhs=ring[:, slot, k, :], start=(k == 0), stop=(k == 7)) for k in range(8)]
                    S.group('pe', fns, reads=[('ring', slot)] + HT, writes=[('ps', pb)])
                    for hh in range(2):
                        oap = v_w[:, hh, wb, :].rearrange("p (g e d) -> p g e d", g=4, e=2)[:, :, hh, :]
                        iap = bank(pb).rearrange("p (g e d) -> p g e d", g=4, e=2)[:, :, hh, :]
                        if hh == 1:
                            S.op('act', lambda oap=oap, iap=iap: act.copy(out=oap, in_=iap), reads=[('ps', pb)], writes=[('v_w', wb)])
                        else:
                            S.op('dve', lambda oap=oap, iap=iap: dve.tensor_copy(out=oap, in_=iap), reads=[('ps', pb)], writes=[('v_w', wb)])
            mod_piece(0, pcn + 1)
            if pcn == 4:
                mod_finish_gate(0)
            pool_post(2 * (pcn - 3))
            pool_post(2 * (pcn - 3) + 1)
        FP.release(kvst)
        FP.release(upad)
        BP.release(siluA)

        siluB = BP.alloc(4)
        slot = ring_get('wie', 5)
        for g in range(4):
            pb = next_pb()
            proj_chunk(slot, g, pb, sample_cols=(864 + 16, 288, 16))
            S.op('act', lambda g=g, pb=pb: act.activation(out=bs(siluB[g], 0, TM), in_=pair(pb, 0, TM), func=AF.Silu),
                 reads=[('ps', pb), ('ps', pb + 1)], writes=[('B', siluB[g])])
        act_preload(AF.Exp, 3)

        if stop == 5:
            S.finish()
            return nc
        if stop == 6:
            S.finish()
            return nc
        ATT, DEN = 0, 2
        sbanks = [4, 5, 6]
        PT = BP.alloc(4)

        tasks = []
        for hp in range(4):
            for sq in range(2):
                for hh in range(2):
                    tasks.append(('p', hp, hh, sq, 0))
            for hh in range(2):
                for kb in (-1, 2, 3, 4, 5, 6, 7, 8):
                    tasks.append(('s', hp, hh, 0, kb))

        def emit_S(ti):
            kind, hp, hh, sq, kb = tasks[ti]
            h = 2 * hp + hh
            pr = slice(hh * 64, hh * 64 + 64)
            sbk = sbanks[ti % 3]
            if kind == 'p':
                S.group('pe', [lambda kb_=kb_: pe.matmul(
                    bank(sbk, kb_ * 256, 256), lhsT=kT_p[:, hp, sq * 256 + kb_ * 128: sq * 256 + (kb_ + 1) * 128],
                    rhs=bs(qT[h], sq * 256, (sq + 1) * 256), start=True, stop=True) for kb_ in range(2)],
                    reads=['kT_p', ('B', qT[h])], writes=[('ps', sbk)])
            elif kb < 7:
                if kb == -1 and h + 1 < 8:
                    t2r_load(h + 1)
                fns = []
                off_ = 0
                for kb_ in ((0, 1) if kb == -1 else (kb,)):
                    qa, qb = QR[kb_]
                    nq = qb - qa
                    fns.append(lambda kb_=kb_, off_=off_, nq=nq: pe.matmul(
                        bank(sbk, off_, nq), lhsT=ident_b[:, :], rhs=t2rb[:, h % 2, QOFF[kb_]:QOFF[kb_] + nq], start=True, stop=False))
                    fns.append(lambda kb_=kb_, off_=off_, nq=nq, qa=qa, qb=qb: pe.matmul(
                        bank(sbk, off_, nq), lhsT=bs(kT_w[hp], kb_ * 128, (kb_ + 1) * 128),
                        rhs=bs(qT[h], 512 + qa, 512 + qb), start=False, stop=True))
                    off_ += nq
                S.group('pe', fns, reads=['ident_b', ('t2rb', h % 2), ('B', kT_w[hp]), ('B', qT[h])],
                        writes=[('ps', sbk)])
            else:
                lb = kb - 7
                S.group('pe', [lambda: pe.matmul(bank(sbk, 0, 320), lhsT=kcT[:, hp, lb * 128:(lb + 1) * 128],
                                                 rhs=bs(qT[h], 512, 832), start=True, stop=True)],
                        reads=['kcT', ('B', qT[h])], writes=[('ps', sbk)])

        def emit_exp_pv(ti):
            kind, hp, hh, sq, kb = tasks[ti]
            h = 2 * hp + hh
            pr = slice(hh * 64, hh * 64 + 64)
            sbk = sbanks[ti % 3]
            p_ = PT[ti % 4]
            qa = 0
            if kind == 's' and kb == -1:
                n = (QR[0][1] - QR[0][0]) + (QR[1][1] - QR[1][0])
            elif kind == 's' and kb < 7:
                qa, qb_ = QR[kb]
                n = qb_ - qa
            else:
                n = 512 if kind == 'p' else 320
            S.op('act', lambda: act.activation(out=bs(p_, 0, n), in_=bank(sbk, 0, n), func=AF.Exp),
                 reads=[('ps', sbk)], writes=[('B', p_)])
            if kind == 'p':
                c0 = sq * 256
                fns = []
                for (bk_, lh_) in ((ATT, None), (DEN, oh[:, hh, :])):
                    for kb_ in range(2):
                        l_ = v_p[:, hh, sq * 2 + kb_, hp * 128:(hp + 1) * 128] if lh_ is None else lh_
                        fns.append(lambda bk_=bk_, kb_=kb_, l_=l_: pe.matmul(
                            ps[:, bk_ * 512 + c0:bk_ * 512 + c0 + 256], lhsT=l_, rhs=bs(p_, kb_ * 256, kb_ * 256 + 256),
                            start=(hh == 0 and kb_ == 0), stop=(hh == 1 and kb_ == 1)))
                S.group('pe', fns, reads=['v_p', 'oh', ('B', p_)], writes=[('ps', ATT), ('ps', DEN)])
                return
            elif kb == -1:
                ab, db = ATT + 1, DEN + 1
                fns = []
                off_ = 0
                for kb_ in (0, 1):
                    qa_, qb_ = QR[kb_]
                    nq = qb_ - qa_
                    for (bk_, l_) in ((ab, v_w[:, hh, kb_, hp * 128:(hp + 1) * 128]), (db, oh[:, hh, :])):
                        fns.append(lambda bk_=bk_, l_=l_, qa_=qa_, nq=nq, off_=off_, kb_=kb_: pe.matmul(
                            ps[:, bk_ * 512 + qa_:bk_ * 512 + qa_ + nq], lhsT=l_, rhs=bs(p_, off_, off_ + nq),
                            start=(hh == 0 and kb_ == 0), stop=False, skip_group_check=True))
                    off_ += nq
                S.group('pe', fns, reads=[('v_w', 0), ('v_w', 1), 'oh', ('B', p_)], writes=[('ps', ab), ('ps', db)])
                return
            elif kb < 7:
                c0, ab, db, last = 0, ATT + 1, DEN + 1, 8
                vl, vkey = v_w[:, hh, kb, hp * 128:(hp + 1) * 128], ('v_w', kb)
            else:
                c0, ab, db, last = 0, ATT + 1, DEN + 1, 8
                vl, vkey = cvb[:, hh, kb - 7, hp * 128:(hp + 1) * 128], ('cvb', kb - 7)
            st_ = False
            sp_ = (hh == 1 and kb == last)
            c0 = c0 + qa
            fns = [lambda: pe.matmul(ps[:, ab * 512 + c0:ab * 512 + c0 + n], lhsT=vl, rhs=bs(p_, 0, n), start=st_, stop=sp_,
                                     skip_group_check=(kind == 's')),
                   lambda: pe.matmul(ps[:, db * 512 + c0:db * 512 + c0 + n], lhsT=oh[:, hh, :], rhs=bs(p_, 0, n), start=st_, stop=sp_,
                                     skip_group_check=(kind == 's'))]
            S.group('pe', fns, reads=[vkey, 'oh', ('B', p_)], writes=[('ps', ab), ('ps', db)])

        def finalize_pair(hp):
            denc = FP.alloc()
            attc = FP.alloc()
            S.op('act', lambda: act.activation(out=fs(denc, 0, TM), in_=pair(DEN, 0, TM), func=AF.Ln),
                 reads=[('ps', DEN), ('ps', DEN + 1)], writes=[('F', denc)])
            S.op('dve', lambda: dve.tensor_copy(out=fs(attc, 0, TM), in_=pair(ATT, 0, TM)),
                 reads=[('ps', ATT), ('ps', ATT + 1)], writes=[('F', attc)])
            S.op('act', lambda: act.activation(out=fs(denc, 0, TM), in_=fs(denc, 0, TM), func=AF.Exp, scale=-1.0),
                 reads=[('F', denc)], writes=[('F', denc)])
            S.op('dve', lambda: dve.tensor_tensor(out=fs(attc, 0, TM), in0=fs(attc, 0, TM), in1=fs(denc, 0, TM), op=ALU.mult),
                 reads=[('F', attc), ('F', denc)], writes=[('F', attc)])
            S.op('dve', lambda: dve.tensor_tensor(out=bs(abo[4 + hp], 0, TM), in0=fs(attc, 0, TM), in1=bs(siluB[hp], 0, TM), op=ALU.mult),
                 reads=[('F', attc), ('B', siluB[hp])], writes=[('B', abo[4 + hp])])
            FP.release([denc, attc])

        LOOK = 2
        NT = len(tasks)
        for ti in range(min(LOOK, NT)):
            emit_S(ti)
        for ti in range(NT):
            if ti + LOOK < NT:
                emit_S(ti + LOOK)
            emit_exp_pv(ti)
            if ti % 20 == 19:
                finalize_pair(ti // 20)
            if ti in (10, 28, 46, 60, 68, 74):
                mod_piece(1, (10, 28, 46, 60, 68, 74).index(ti))
        mod_finish(1)
        mod_finish_gate(1)
        BP.release(PT)
        BP.release(qT)
        BP.release(kT_w)
        BP.release(siluB)

        if stop == 8:
            S.finish()
            return nc
        ABO = [('B', s) for s in abo]
        ring_limit[0] = piece_idx[('wio', 1)]
        wslots = [ring_get('woe', 0), ring_get('woe', 1)]
        sq1 = BP.alloc(2)
        sbk4 = [0, 1, 2, 3, 6, 7]
        wcnt = [0]
        tmpL1 = FP.alloc(8)
        tmpsL1 = [((lambda a_, b_, t=t: fs(t, a_, b_)), ('F', t)) for t in tmpL1]

        def l1_stat(n, region):
            q = sq1[n % 2]
            c0, ncol, bk = (0, 512, 4) if region == 'p' else (512, 320, 5)
            S.op('act', lambda: act.activation(out=bs(q, c0, c0 + ncol), in_=fs(xm[n], c0, c0 + ncol), func=AF.Square),
                 reads=[XM[n]], writes=[('B', q)])
            S.group('pe', [lambda: pe.matmul(bank(bk, 0, ncol), lhsT=ones_b[:, :], rhs=bs(q, c0, c0 + ncol), start=(n == 0), stop=(n == 7))],
                    reads=[('B', q), 'ones_b'], writes=[('ps', bk)])

        def chain_gen(region):
            if region == 'p':
                rstd_from(4, 512, 1.0 / 1024, to_psum=True)
                yield
                rk, lo, hi, regs, rb = [('ps', 4)], 0, 512, [(0, 512, 0, 0)], 4
            else:
                rstd_from(5, 320, 1.0 / 1024, to_psum=True)
                yield
                rk, lo, hi, regs, rb = [('ps', 5)], 512, 832, [(512, 320, 1, 512)], 4
            for c in range(8):
                oap, tk = tmpsL1[c][0](lo, hi), tmpsL1[c][1]
                S.op('dve', lambda c=c, oap=oap: dve.tensor_tensor(out=oap, in0=fs(xm[c], lo, hi),
                                                                    in1=ps[:, rb * 512 + lo:rb * 512 + hi], op=ALU.mult),
                     reads=[XM[c]] + rk, writes=[tk])
                (c0, n_, q_, h0) = regs[0]
                if region == 'p':
                    S.op('act', lambda c=c, oap=oap: act.activation(out=hT[:, c, h0:h0 + n_], in_=oap, func=AF.Identity,
                                                                    bias=shift_col(1, c, q_), scale=gs_col(1, c, q_)),
                         reads=[tk, ('mT', 1), ('gs', 1)], writes=[('hTp', c)])
                else:
                    S.op('pool', lambda c=c, oap=oap: gp.tensor_scalar(out=hT[:, c, h0:h0 + n_], in0=oap, scalar1=gs_col(1, c, q_),
                                                                       scalar2=shift_col(1, c, q_), op0=ALU.mult, op1=ALU.add),
                         reads=[tk, ('mT', 1), ('gs', 1)], writes=[('hTs', c)])
                yield

        def woe_region(region, stepper):
            pend = []
            for n in range(8):
                slot, n4 = wslots[n // 4], n % 4
                bk = sbk4[wcnt[0] % 6]
                wcnt[0] += 1
                c0, ncol, q_ = (0, 512, 0) if region == 'p' else (512 + 16, 288, 1)
                for (k0, k1) in (((0, 7), (7, 8)) if region == 'p' else ((0, 8),)):
                    fns = [lambda k=k: pe.matmul(bank(bk, 0, ncol), lhsT=ring[:, slot, k, n4 * 128:(n4 + 1) * 128],
                                                 rhs=bs(abo[k], c0, c0 + ncol), start=(k == 0), stop=(k == 7)) for k in range(k0, k1)]
                    S.group('pe', fns, reads=[('ring', slot)] + ABO[k0:k1], writes=[('ps', bk)])
                S.op('dve', lambda: dve.scalar_tensor_tensor(out=fs(xm[n], c0, c0 + ncol), in0=bank(bk, 0, ncol), scalar=gate_col(0, n, q_),
                                                             in1=fs(xm[n], c0, c0 + ncol), op0=ALU.mult, op1=ALU.add),
                     reads=[('ps', bk), ('mTg', 0), XM[n]], writes=[XM[n]])
                if pend:
                    l1_stat(pend.pop(0), region)
                pend.append(n)
                if stepper is not None:
                    stepper()
            l1_stat(pend.pop(0), region)

        woe_region('p', None)
        gp_ = chain_gen('p')

        def step_p():
            try:
                next(gp_)
            except StopIteration:
                pass
        woe_region('s', step_p)
        for _ in range(12):
            step_p()
        ring_limit[0] = None
        BP.release(sq1)
        if debug and 'x1' in debug:
            for c in range(8):
                S.dma('sp', dbg_d['x1'][c], fs(xm[c], 0, TM), reads=[XM[c]])

        if stop == 9:
            S.finish()
            return nc
        pbs[:] = [0, 2, 4, 6]

        def proj1(slot, n4, pb, sn=320, s0=512):
            if sn == 320:
                proj_chunk(slot, n4, pb, sample_cols=(s0 + 16, 288, 16))
            else:
                proj_chunk(slot, n4, pb, sample_cols=(s0, sn))

        def build_diag(i):
            sl = BP.alloc(5, consecutive=True)
            base = sl[0] * SLOT
            outap = poolb[:, base:base + 31 * 128].rearrange("p (j d) -> p j d", j=31)
            in0 = ident_f[:, :].unsqueeze(1).broadcast_to([128, 31, 128])
            in1 = par[:, PC_CD + i * 31:PC_CD + (i + 1) * 31].unsqueeze(2).broadcast_to([128, 31, 128])
            S.op('dve', lambda: dve.tensor_tensor(out=outap, in0=in0, in1=in1, op=ALU.mult),
                 reads=['ident_f', 'par'], writes=[('B', x) for x in sl])
            return sl

        cdo = abo
        slot = ring_get('wio', 1)
        gs_ = chain_gen('s')
        cpb = [next_pb() for _ in range(3)]
        for i in range(3):
            proj_P(slot, i, cpb[i], fine=(i == 0))
            try:
                next(gs_)
                next(gs_)
                next(gs_)
            except StopIteration:
                pass
        for _ in gs_:
            pass
        FP.release(tmpL1)
        cc = FP.alloc(4)
        acc = FP.alloc(4)

        def cc_evac(i, pb):
            S.op('act', lambda: act.copy(out=fs(cc[i], 0, 512), in_=bank(pb, 0, 512)), reads=[('ps', pb)], writes=[('F', cc[i])])
            S.op('dve', lambda: dve.tensor_copy(out=fs(cc[i], 512, TM), in_=bank(pb + 1, 0, 320)), reads=[('ps', pb + 1)], writes=[('F', cc[i])])
        for i in range(3):
            proj_S(slot, i, cpb[i], (512 + 16, 288, 16))
            cc_evac(i, cpb[i])
        pb = next_pb()
        proj1(slot, 3, pb)
        cc_evac(3, pb)
        slot = ring_get('wio', 2)
        for i in range(4):
            pb = next_pb()
            proj1(slot, i, pb)
            c_ = cc[i]
            a_ = acc[i]
            S.op('dve', lambda c_=c_, pb=pb: dve.tensor_tensor(out=fs(c_, 0, TM), in0=pair(pb, 0, TM), in1=fs(c_, 0, TM), op=ALU.mult),
                 reads=[('ps', pb), ('ps', pb + 1), ('F', c_)], writes=[('F', c_)])
            S.op('dve', lambda c_=c_: dve.tensor_tensor(out=fs(c_, 512, 832), in0=fs(c_, 512, 832), in1=vmask[:, :], op=ALU.mult),
                 reads=[('F', c_), 'vmask'], writes=[('F', c_)])
            w0, w1, w2 = (pcol(PC_CC + i * 3 + j) for j in range(3))
            S.op('act', lambda c_=c_, a_=a_, w1=w1: act.activation(out=fs(a_, 0, TM), in_=fs(c_, 0, TM), func=AF.Identity, scale=w1),
                 reads=[('F', c_), 'par'], writes=[('F', a_)])
            v3 = lambda s, a, b: fs(s, 0, 512).rearrange("p (s t) -> p s t", s=2)[:, :, a:b]
            S.op('dve', lambda c_=c_, a_=a_, w0=w0: dve.scalar_tensor_tensor(out=v3(a_, 1, 256), in0=v3(c_, 0, 255), scalar=w0, in1=v3(a_, 1, 256),
                                                                             op0=ALU.mult, op1=ALU.add),
                 reads=[('F', c_), ('F', a_), 'par'], writes=[('F', a_)])
            S.op('dve', lambda c_=c_, a_=a_, w2=w2: dve.scalar_tensor_tensor(out=v3(a_, 0, 255), in0=v3(c_, 1, 256), scalar=w2, in1=v3(a_, 0, 255),
                                                                             op0=ALU.mult, op1=ALU.add),
                 reads=[('F', c_), ('F', a_), 'par'], writes=[('F', a_)])
            S.op('dve', lambda c_=c_, a_=a_, w0=w0: dve.scalar_tensor_tensor(out=fs(a_, 513, 832), in0=fs(c_, 512, 831), scalar=w0, in1=fs(a_, 513, 832),
                                                                             op0=ALU.mult, op1=ALU.add),
                 reads=[('F', c_), ('F', a_), 'par'], writes=[('F', a_)])
            S.op('dve', lambda c_=c_, a_=a_, w2=w2: dve.scalar_tensor_tensor(out=fs(a_, 512, 831), in0=fs(c_, 513, 832), scalar=w2, in1=fs(a_, 512, 831),
                                                                             op0=ALU.mult, op1=ALU.add),
                 reads=[('F', c_), ('F', a_), 'par'], writes=[('F', a_)])
        FP.release(cc)
        slot = ring_get('wio', 0)
        for i in range(4):
            pb = next_pb()
            proj1(slot, i, pb)
            a_ = acc[i]
            S.op('dve', lambda a_=a_, pb=pb: dve.tensor_tensor(out=fs(a_, 0, TM), in0=pair(pb, 0, TM), in1=fs(a_, 0, TM), op=ALU.mult),
                 reads=[('ps', pb), ('ps', pb + 1), ('F', a_)], writes=[('F', a_)])
        sgt = FP.alloc(2)
        slot = ring_get('wio', 3)
        for i in range(4):
            pb = next_pb()
            proj1(slot, i, pb)
            a_, t_ = acc[i], sgt[i % 2]
            S.op('act', lambda t_=t_, pb=pb: act.activation(out=fs(t_, 0, TM), in_=pair(pb, 0, TM), func=AF.Silu),
                 reads=[('ps', pb), ('ps', pb + 1)], writes=[('F', t_)])
            S.op('dve', lambda a_=a_, t_=t_, i=i: dve.tensor_tensor(out=bs(cdo[i], 0, TM), in0=fs(a_, 0, TM), in1=fs(t_, 0, TM), op=ALU.mult),
                 reads=[('F', a_), ('F', t_)], writes=[('B', cdo[i])])
        FP.release(acc)
        ad = FP.alloc(4)
        slot = ring_get('wio', 4)
        for i in range(4):
            pb = next_pb()
            proj1(slot, i, pb)
            S.op('act', lambda i=i, pb=pb: act.copy(out=fs(ad[i], 0, 512), in_=bank(pb, 0, 512)),
                 reads=[('ps', pb)], writes=[('F', ad[i])])
            S.op('dve', lambda i=i, pb=pb: dve.tensor_tensor(out=fs(ad[i], 512, 832), in0=bank(pb + 1, 0, 320), in1=vmask[:, :], op=ALU.mult),
                 reads=[('ps', pb + 1), 'vmask'], writes=[('F', ad[i])])
        glu = BP.alloc(4)
        slot = ring_get('wio', 5)
        for i in range(4):
            g_ = glu[i]
            S.op('pool', lambda g_=g_: gp.memset(bs(g_, 0, SLOT), 0.0), writes=[('B', g_)])
            pb = next_pb()
            proj1(slot, i, pb)
            t_ = sgt[i % 2]
            S.op('act', lambda t_=t_, pb=pb: act.activation(out=fs(t_, 0, TM), in_=pair(pb, 0, TM), func=AF.Sigmoid),
                 reads=[('ps', pb), ('ps', pb + 1)], writes=[('F', t_)])
            S.op('dve', lambda g_=g_, t_=t_, i=i: dve.tensor_tensor(
                out=bs(g_, 15, 15 + 542).rearrange("p (s t) -> p s t", s=2)[:, :, 0:256],
                in0=fs(ad[i], 0, 512).rearrange("p (s t) -> p s t", s=2), in1=fs(t_, 0, 512).rearrange("p (s t) -> p s t", s=2), op=ALU.mult),
                reads=[('F', ad[i]), ('F', t_)], writes=[('B', g_)])
            S.op('dve', lambda g_=g_, t_=t_, i=i: dve.tensor_tensor(out=bs(g_, 557, 877), in0=fs(ad[i], 512, 832), in1=fs(t_, 512, 832), op=ALU.mult),
                 reads=[('F', ad[i]), ('F', t_)], writes=[('B', g_)])
        FP.release(ad)
        if stop == 10:
            S.finish()
            return nc
        T1 = 768
        sgd = BP.alloc(4)
        slot = ring_get('wio', 6)
        for i in range(4):
            pb = next_pb()
            proj1(slot, i, pb, sn=256, s0=512 + OWN0)
            S.op('act', lambda i=i, pb=pb: act.activation(out=bs(sgd[i], 0, T1), in_=pair(pb, 0, T1), func=AF.Silu),
                 reads=[('ps', pb), ('ps', pb + 1)], writes=[('B', sgd[i])])
        act_preload(AF.Ln, 2)
        z = FP.alloc(4)
        zb = BP.alloc(4)
        MEANB, SQB = 4, 6

        def conv_stats(i):
            q0, q1 = zb[(i % 2) * 2], zb[(i % 2) * 2 + 1]
            fns = []
            for (bk, q) in ((MEANB, q0), (SQB, q1)):
                fns.append(lambda bk=bk, q=q: pe.matmul(bank(bk, 0, 512), lhsT=ones_b[:, :], rhs=bs(q, 0, 512), start=(i == 0), stop=(i == 3)))
                fns.append(lambda bk=bk, q=q: pe.matmul(bank(bk + 1, 0, 256), lhsT=ones_b[:, :], rhs=bs(q, 512, 768), start=(i == 0), stop=(i == 3)))
            S.group('pe', fns, reads=[('B', q0), ('B', q1), 'ones_b'], writes=[('ps', MEANB), ('ps', MEANB + 1), ('ps', SQB), ('ps', SQB + 1)])

        dgs = {0: build_diag(0)}
        for i in range(4):
            if i + 1 < 4:
                dgs[i + 1] = build_diag(i + 1)
            dg = dgs[i]
            pb = 0 if i % 2 == 0 else 2
            g_ = glu[i]
            regions = ((0, 0, pb, 0), (271, 0, pb, 256), (557 + OWN0 - 15, 0, pb + 1, 0))
            fns = []
            for (off, _, bk, bo) in regions:
                for j in range(31):
                    o = dg[0] * SLOT + j * 128
                    fns.append(lambda off=off, bk=bk, bo=bo, j=j, o=o, g_=g_: pe.matmul(
                        bank(bk, bo, 256), lhsT=poolb[:, o:o + 128], rhs=bs(g_, off + j, off + j + 256), start=(j == 0), stop=(j == 30)))
            S.group('pe', fns, reads=[('B', g_)] + [('B', s) for s in dg], writes=[('ps', pb), ('ps', pb + 1)])
            BP.release(dg)
            S.op('act', lambda i=i, pb=pb: act.activation(out=fs(z[i], 0, T1), in_=pair(pb, 0, T1), func=AF.Identity, bias=pcol(PC_CDB + i)),
                 reads=[('ps', pb), ('ps', pb + 1), 'par'], writes=[('F', z[i])])
            q0, q1 = zb[(i % 2) * 2], zb[(i % 2) * 2 + 1]
            S.op('dve', lambda i=i, q0=q0, pb=pb: dve.tensor_scalar(out=bs(q0, 0, T1), in0=pair(pb, 0, T1), scalar1=pcol(PC_CDB + i), scalar2=None,
                                                                      op0=ALU.add),
                 reads=[('ps', pb), ('ps', pb + 1), 'par'], writes=[('B', q0)])
            S.op('act', lambda i=i, q1=q1, pb=pb: act.activation(out=bs(q1, 0, T1), in_=pair(pb, 0, T1), func=AF.Square, bias=pcol(PC_CDB + i)),
                 reads=[('ps', pb), ('ps', pb + 1), 'par'], writes=[('B', q1)])
            if i >= 1:
                conv_stats(i - 1)
        conv_stats(3)
        BP.release(zb)
        BP.release(glu)
        CDO = [('B', s_) for s_ in cdo]

        def woo_pass(k0, k1, mode, chunks=range(8), per_k=False):
            wt = FP.alloc(2) if mode != 'dve' else None
            for n in chunks:
                if True:
                    pc, n4 = n // 4, n % 4
                    slot = ring_get('woo', pc)
                    pb = next_pb()
                    fns = []
                    for k in range(k0, k1):
                        fns.append(lambda k=k, n4=n4, pb=pb: pe.matmul(bank(pb, 0, 512), lhsT=ring[:, slot, k, n4 * 128:(n4 + 1) * 128],
                                                                       rhs=bs(cdo[k], 0, 512), start=(k == k0), stop=(k == k1 - 1)))
                    for k in range(k0, k1):
                        c0 = 512 + OWN0 if k < 4 else 512
                        fns.append(lambda k=k, n4=n4, pb=pb, c0=c0: pe.matmul(bank(pb + 1, 0, 256), lhsT=ring[:, slot, k, n4 * 128:(n4 + 1) * 128],
                                                                              rhs=bs(cdo[k], c0, c0 + 256), start=(k == k0), stop=(k == k1 - 1)))
                    if per_k:
                        nk_ = k1 - k0
                        for j_ in range(nk_):
                            S.group('pe', [fns[j_], fns[nk_ + j_]], reads=[('ring', slot), CDO[k0 + j_]], writes=[('ps', pb), ('ps', pb + 1)])
                    else:
                        S.group('pe', fns, reads=[('ring', slot)] + CDO[k0:k1], writes=[('ps', pb), ('ps', pb + 1)])
                    if mode == 'dve':
                        S.op('dve', lambda n=n, pb=pb: dve.scalar_tensor_tensor(out=fs(xm[n], 0, 512), in0=bank(pb, 0, 512), scalar=gate_col(1, n, 0),
                                                                                in1=fs(xm[n], 0, 512), op0=ALU.mult, op1=ALU.add),
                             reads=[('ps', pb), ('mTg', 1), XM[n]], writes=[XM[n]])
                        S.op('dve', lambda n=n, pb=pb: dve.scalar_tensor_tensor(out=fs(xm[n], 544, 800), in0=bank(pb + 1, 0, 256), scalar=gate_col(1, n, 1),
                                                                                in1=fs(xm[n], 544, 800), op0=ALU.mult, op1=ALU.add),
                             reads=[('ps', pb + 1), ('mTg', 1), XM[n]], writes=[XM[n]])
                    else:
                        t_ = wt[n % 2]
                        S.op('act', lambda n=n, pb=pb, t_=t_: act.activation(out=fs(t_, 0, 512), in_=bank(pb, 0, 512), func=AF.Identity,
                                                                             scale=gate_col(1, n, 0)),
                             reads=[('ps', pb), ('mTg', 1)], writes=[('F', t_)])
                        S.op('act', lambda n=n, pb=pb, t_=t_: act.activation(out=fs(t_, 512, 768), in_=bank(pb + 1, 0, 256), func=AF.Identity,
                                                                             scale=gate_col(1, n, 1)),
                             reads=[('ps', pb + 1), ('mTg', 1)], writes=[('F', t_)])
                        S.op('pool', lambda n=n, t_=t_: gp.tensor_tensor(out=fs(xm[n], 0, 512), in0=fs(xm[n], 0, 512), in1=fs(t_, 0, 512), op=ALU.add),
                             reads=[('F', t_), XM[n]], writes=[XM[n]])
                        S.op('pool', lambda n=n, t_=t_: gp.tensor_tensor(out=fs(xm[n], 544, 800), in0=fs(xm[n], 544, 800), in1=fs(t_, 512, 768), op=ALU.add),
                             reads=[('F', t_), XM[n]], writes=[XM[n]])
            if wt is not None:
                FP.release(wt)

        var = FP.alloc()
        MK = [('ps', MEANB), ('ps', MEANB + 1)]
        VK = [('ps', SQB), ('ps', SQB + 1)]
        S.op('act', lambda: act.activation(out=pair(MEANB, 0, T1), in_=pair(MEANB, 0, T1), func=AF.Copy, scale=1.0 / 512),
             reads=MK, writes=MK)
        S.op('act', lambda: act.activation(out=fs(var, 0, T1), in_=pair(MEANB, 0, T1), func=AF.Square),
             reads=MK, writes=[('F', var)])
        S.op('dve', lambda: dve.scalar_tensor_tensor(out=fs(var, 0, T1), in0=pair(SQB, 0, T1), scalar=1.0 / 512, in1=fs(var, 0, T1),
                                                     op0=ALU.mult, op1=ALU.subtract),
             reads=VK + [('F', var)], writes=[('F', var)])
        S.op('act', lambda: act.activation(out=fs(var, 0, T1), in_=fs(var, 0, T1), func=AF.Ln, bias=par[:, NPAR - 1:NPAR], scale=1.0),
             reads=[('F', var), 'par'], writes=[('F', var)])
        S.op('act', lambda: act.activation(out=pair(SQB, 0, T1), in_=fs(var, 0, T1), func=AF.Exp, scale=-0.5), reads=[('F', var)], writes=VK)
        for i in range(4):
            S.op('dve', lambda i=i: dve.tensor_tensor(out=fs(z[i], 0, T1), in0=fs(z[i], 0, T1), in1=pair(MEANB, 0, T1), op=ALU.subtract),
                 reads=[('F', z[i])] + MK, writes=[('F', z[i])])
        for i in range(4):
            S.op('dve', lambda i=i: dve.tensor_tensor(out=fs(z[i], 0, T1), in0=fs(z[i], 0, T1), in1=pair(SQB, 0, T1), op=ALU.mult),
                 reads=[('F', z[i])] + VK, writes=[('F', z[i])])
        pbs[:] = [0, 2]
        woo_pass(0, 4, 'actpool', chunks=(0, 1))
        for i in range(4):
            S.op('act', lambda i=i: act.activation(out=fs(z[i], 0, T1), in_=fs(z[i], 0, T1), func=AF.Silu,
                                                   bias=pcol(PC_LNB + i), scale=pcol(PC_LNG + i)),
                 reads=[('F', z[i]), 'par'], writes=[('F', z[i])])
        FP.release(var)
        woo_pass(0, 4, 'dve', chunks=(2, 3))
        for i in range(4):
            S.op('dve', lambda i=i: dve.tensor_tensor(out=bs(cdo[4 + i], 0, T1), in0=fs(z[i], 0, T1), in1=bs(sgd[i], 0, T1), op=ALU.mult),
                 reads=[('F', z[i]), ('B', sgd[i])], writes=[('B', cdo[4 + i])])
        BP.release(sgd)
        FP.release(sgt)
        FP.release(z)

        woo_pass(0, 4, 'actpool', chunks=(4, 5, 6, 7))
        pbs[:] = [0, 2, 4, 6]
        woo_pass(4, 8, 'dve', chunks=(0,), per_k=True)
        woo_pass(4, 8, 'dve', chunks=range(1, 8))
        BP.release(abo)

        if stop == 11:
            S.finish()
            return nc
        fgb = FP.alloc(2)
        S.dma('sp', fs(fgb[0], 0, 512), fgb_d[:, 0:512], writes=[('F', fgb[0])])
        S.dma('sp', fs(fgb[1], 0, 512), fgb_d[:, 512:1024], writes=[('F', fgb[1])])
        junk = FP.alloc()
        ost = FP.alloc(4)
        for tb in range(6):
            pb = (0, 2, 4)[tb % 3]
            c0 = tb * 128 if tb < 4 else 544 + (tb - 4) * 128
            fns = [lambda c=c, c0=c0, pb=pb: pe.transpose(out=ps[:, pb * 512 + c * 128: pb * 512 + (c + 1) * 128],
                                                          in_=fs(xm[c], c0, c0 + 128), identity=ident_f[:, :]) for c in range(8)]
            S.group('pe', fns, reads=XM + ['ident_f'], writes=[('ps', pb), ('ps', pb + 1)])
            for hf in range(2):
                S.op('act', lambda hf=hf, pb=pb, tb=tb: act.activation(out=fs(junk, 0, 512), in_=bank(pb + hf), func=AF.Square,
                                                                       accum_out=stt[:, tb * 4 + hf:tb * 4 + hf + 1]),
                     reads=[('ps', pb + hf)], writes=[('F', junk), ('stt', tb)])
            S.op('dve', lambda tb=tb: dve.tensor_tensor(out=stt[:, tb * 4 + 2:tb * 4 + 3], in0=stt[:, tb * 4:tb * 4 + 1],
                                                        in1=stt[:, tb * 4 + 1:tb * 4 + 2], op=ALU.add),
                 reads=[('stt', tb)], writes=[('stt', tb)])
            S.op('act', lambda tb=tb: act.activation(out=stt[:, tb * 4 + 3:tb * 4 + 4], in_=stt[:, tb * 4 + 2:tb * 4 + 3], func=AF.Sqrt,
                                                     bias=par[:, NPAR - 1:NPAR], scale=1.0 / 1024),
                 reads=[('stt', tb), 'par'], writes=[('stt', tb)])
            S.op('dve', lambda tb=tb: dve.reciprocal(out=stt[:, tb * 4 + 3:tb * 4 + 4], in_=stt[:, tb * 4 + 3:tb * 4 + 4]),
                 reads=[('stt', tb)], writes=[('stt', tb)])
            dst = yp_d[tb * 128:(tb + 1) * 128, :] if tb < 4 else ys_d[(tb - 4) * 128:(tb - 3) * 128, :]
            for hf in range(2):
                o_ = ost[(tb % 2) * 2 + hf]
                S.op('dve', lambda hf=hf, pb=pb, tb=tb, o_=o_: dve.scalar_tensor_tensor(
                    out=fs(o_, 0, 512), in0=bank(pb + hf), scalar=stt[:, tb * 4 + 3:tb * 4 + 4], in1=fs(fgb[hf], 0, 512),
                    op0=ALU.mult, op1=ALU.mult),
                    reads=[('ps', pb + hf), ('stt', tb), ('F', fgb[hf])], writes=[('F', o_)])
                S.dma('sp', dst[:, hf * 512:(hf + 1) * 512], fs(o_, 0, 512), reads=[('F', o_)])
        S.finish()
    return nc


_CACHE = {}


def _host_consts():
    if 'c' in _CACHE:
        return _CACHE['c']
    ident = np.eye(128, dtype=np.float32)
    halfs = (1, 2, 4, 8)
    inv_p = np.zeros((4, 256), np.float32)
    for g, hf in enumerate(halfs):
        pos = np.arange(256)
        lo = np.clip(pos - hf, 0, 256)
        hi = np.clip(pos + hf, 0, 256)
        inv_p[g] = 1.0 / (hi - lo)
    per_core = []
    for i in range(8):
        j = i % 4
        t0 = 256 * j
        te = t0 - 32 + np.arange(320)
        valid = (te >= 0) & (te < 1024)
        vmask = np.broadcast_to(valid.astype(np.float32)[None, :], (128, 320)).copy()
        invc = np.zeros((4, PUW), np.float32)
        for g, hf in enumerate(halfs):
            invc[g, 8:264] = inv_p[g]
            invc[g, 272:528] = inv_p[g]
            lo = np.clip(te - hf, 0, 1024)
            hi = np.clip(te + hf, 0, 1024)
            cnt = np.maximum(hi - lo, 1)
            invc[g, 544:864] = np.where(valid, 1.0 / cnt, 0.0)
        invc = np.broadcast_to(invc.reshape(1, 4 * PUW), (128, 4 * PUW)).copy()
        kw0 = 4 * j - 6
        qe = np.arange(320)
        hs = qe // 32 + 1
        s = hs // 2
        r = 4 * j - 1 + s
        start = np.clip(r - 4, 0, 8)
        mall = np.zeros((128, 320), np.float32)
        mall[0:14] = NEG
        for kb in range(7):
            for a in range(2):
                kr = kw0 + 2 * kb + a
                ok = (kr >= 0) & (kr < 16) & (r >= 0) & (r < 16) & (kr >= start) & (kr < start + 8)
                mall[kb * 2 + a] = np.where(ok, 0.0, NEG)
        per_core.append((vmask, invc, mall))
    indall = np.zeros((128, 7, 128), np.float32)
    for kb in range(7):
        for a in range(2):
            indall[kb * 2 + a, kb, a * 64:(a + 1) * 64] = 1.0
    indall = indall.reshape(128, 896)
    _CACHE['c'] = (ident, per_core, indall)
    return _CACHE['c']


def _t2r_table(rpb, j):
    out = np.full((128, 8, QTOT), np.float32(NEG), np.float32)
    a = (np.arange(128) // 64)[:, None]
    kcol = (np.arange(128) % 64)[:, None]
    kw0 = 4 * j - 6
    for kb in range(7):
        qa, qb = QR[kb]
        q = np.arange(qa, qb)[None, :]
        hs = q // 32 + 1
        s_ = hs // 2
        r = 4 * j - 1 + s_
        qcol = (hs % 2) * 32 + q % 32
        kr = kw0 + 2 * kb + a
        start = np.clip(r - 4, 0, 8)
        col_start = np.clip(qcol - 8, 0, 48)
        ok = ((kr >= 0) & (kr < 16) & (r >= 0) & (r < 16) & (kr >= start) & (kr < start + 8)
              & (kcol >= col_start) & (kcol < col_start + 16))
        dr = np.clip(kr - r + 7, 0, 14)
        dc = np.clip(kcol - qcol, -15, 15) + 15
        for h in range(8):
            out[:, h, QOFF[kb]:QOFF[kb] + (qb - qa)] = np.where(ok, rpb[h][dr, dc], np.float32(NEG))
    return out


def _col(v):
    v = np.asarray(v, np.float32)
    return np.ascontiguousarray(v.reshape(-1, 128).T)


def _prepare(inputs):
    x_prompt = np.asarray(inputs['x_prompt'], np.float32)
    x_sample = np.asarray(inputs['x_sample'], np.float32)
    ident, per_core, indall = _host_consts()
    t2r_j = [_t2r_table(np.asarray(inputs['rpb'], np.float32)[0], j_) for j_ in range(4)]
    shared = {
        'ident': ident,
        'fgb': np.ascontiguousarray(np.broadcast_to(np.asarray(inputs['final_g'], np.float32)[None, :], (128, 1024))),
        'w_mod': np.ascontiguousarray(inputs['w_mod'], np.float32),
        'w_in_even': np.ascontiguousarray(inputs['w_in_even'][0], np.float32),
        'w_pool': np.ascontiguousarray(inputs['w_pool'][0], np.float32),
        'w_out_even': np.ascontiguousarray(inputs['w_out_even'][0], np.float32),
        'w_in_odd': np.ascontiguousarray(inputs['w_in_odd'][0], np.float32),
        'w_out_odd': np.ascontiguousarray(inputs['w_out_odd'][0], np.float32),
    }
    c = np.asarray(inputs['c'], np.float32)
    c_ctx = np.asarray(inputs['c_ctx'], np.float32)
    norm_g = np.asarray(inputs['norm_g'], np.float32)
    b_mod = np.asarray(inputs['b_mod'], np.float32)
    ckt, cvz = [], []
    for b_ in range(2):
        ck_ = np.asarray(inputs['cache_k'][b_, 0], np.float32)
        cv_ = np.asarray(inputs['cache_v'][b_, 0], np.float32)
        ckt.append(np.ascontiguousarray(ck_.transpose(0, 2, 1).reshape(512, 256)))
        vt = cv_.transpose(1, 0, 2)
        z_ = np.zeros((2, 256, 8, 64), np.float32)
        z_[0, :, 0::2, :] = vt[:, 0::2, :]
        z_[1, :, 1::2, :] = vt[:, 1::2, :]
        cvz.append(z_.reshape(2, 256, 512))
    in_maps = []
    for i in range(8):
        b, j = i // 4, i % 4
        par = np.zeros((128, NPAR), np.float32)
        cc = np.stack([_col(c_ctx), _col(c[b])], axis=2)
        par[:, PC_COND:PC_COND + 16] = cc.reshape(128, 16)
        for l in range(2):
            g2 = np.repeat(_col(norm_g[l])[:, :, None], 2, axis=2)
            par[:, PC_G + l * 16:PC_G + (l + 1) * 16] = g2.reshape(128, 16)
            b2 = np.repeat(_col(b_mod[l])[:, :, None], 2, axis=2)
            par[:, PC_BMOD + l * 48:PC_BMOD + (l + 1) * 48] = b2.reshape(128, 48)
        par[:, PC_PSC:PC_PSC + 4] = _col(inputs['pool_scale'][0])
        par[:, PC_CC:PC_CC + 12] = np.stack([_col(inputs['conv_c'][0][t]) for t in range(3)], axis=2).reshape(128, 12)
        par[:, PC_CD:PC_CD + 124] = np.stack([_col(inputs['conv_d'][0][t]) for t in range(31)], axis=2).reshape(128, 124)
        par[:, PC_CDB:PC_CDB + 4] = _col(inputs['conv_d_b'][0])
        par[:, PC_LNG:PC_LNG + 4] = _col(inputs['ln_g'][0])
        par[:, PC_LNB:PC_LNB + 4] = _col(inputs['ln_b'][0])
        par[:, PC_FG:PC_FG + 8] = _col(inputs['final_g'])
        par[:, NPAR - 1] = EPS
        xp = np.ascontiguousarray(x_prompt[2 * i:2 * i + 2].reshape(512, 1024))
        xw = np.zeros((896, 1024), np.float32)
        kw0 = 4 * j - 6
        lo_r, hi_r = max(kw0, 0), min(kw0 + 14, 16)
        xw[(lo_r - kw0) * 64:(hi_r - kw0) * 64] = x_sample[b, lo_r * 64:hi_r * 64]
        vmask, invc, mall = per_core[i]
        m = dict(shared)
        m.update({'xpT': np.ascontiguousarray(xp.T), 'xwT': np.ascontiguousarray(xw.T),
                  'ck': ckt[b], 'cv': cvz[b],
                  'params': par, 'vmask': vmask, 'invcnt': invc, 't2r': t2r_j[j]})
        in_maps.append(m)
    return in_maps


def kernel(**inputs):
    in_maps = _prepare(inputs)
    if 'nc' not in _CACHE:
        _CACHE['nc'] = build_program()
    nc = _CACHE['nc']
    res = run_bass_kernel_spmd(nc, in_maps, core_ids=list(range(8)))
    R = res.results
    y_prompt = np.concatenate([R[i]['yp'].reshape(2, 256, 1024) for i in range(8)], axis=0)
    y_sample = np.stack([np.concatenate([R[b * 4 + j]['ys'] for j in range(4)], axis=0) for b in range(2)], axis=0)
    nk = np.concatenate([R[i]['nk'].reshape(8, 64, 2, 256).transpose(2, 0, 3, 1).reshape(2, 1, 8, 256, 64) for i in range(8)], axis=0)
    nv = np.concatenate([R[i]['nv'].reshape(2, 256, 8, 64).transpose(0, 2, 1, 3).reshape(2, 1, 8, 256, 64) for i in range(8)], axis=0)
    return (y_prompt.astype(np.float32), y_sample.astype(np.float32), nk.astype(np.float32), nv.astype(np.float32))
```

```python
import contextlib
import numpy as np
import concourse.bass as bass
import concourse.mybir as mybir
from concourse.bass_utils import run_bass_kernel_spmd

F32 = mybir.dt.float32
BF16 = mybir.dt.bfloat16
AF = mybir.ActivationFunctionType
ALU = mybir.AluOpType

NEG = -30000.0
EPS = 1e-6
TP, TS, TM, TW = 512, 320, 832, 896
EXT0 = 352
OWN0 = 32
SLOT = 896
NF, NB = 18, 30
PC_COND, PC_G, PC_BMOD, PC_PSC, PC_CC, PC_CD, PC_CDB, PC_LNG, PC_LNB, PC_FG = 0, 16, 48, 144, 148, 160, 284, 288, 292, 296
NPAR = 305
QR = {0: (16, 32), 1: (16, 288), 2: (16, 288), 3: (16, 304), 4: (16, 304), 5: (32, 304), 6: (32, 304)}
QOFF = {}
_o = 0
for _kb in range(7):
    QOFF[_kb] = _o
    _o += QR[_kb][1] - QR[_kb][0]
QTOT = _o
PU_SEQ = (8, 272, 544)
PUW = 872
PG_SEQ = (15, 286, 557)
PGW = 892


class Sched:
    def __init__(self, nc, es):
        self.nc = nc
        self.E = {'pe': nc.tensor, 'act': nc.scalar, 'dve': nc.vector, 'pool': nc.gpsimd, 'sp': nc.sync}
        self.sems = {}
        for e in ('pe', 'act', 'dve', 'pool'):
            self.sems[e] = es.enter_context(nc.semaphore('s_' + e))
        self.cnt = {e: 0 for e in ('pe', 'act', 'dve', 'pool')}
        self.dma_pool = {}
        for q, n in (('sp', 40), ('pool', 40)):
            lst = []
            for i in range(n):
                nm = 'd_%s%d' % (q, i)
                self.sems[nm] = es.enter_context(nc.semaphore(nm))
                lst.append([nm, 0])
            self.dma_pool[q] = lst
        self.dma_rr = {q: 0 for q in self.dma_pool}
        self.waited = {}
        self.lastw = {}
        self.readers = {}

    @staticmethod
    def _is_ps(k):
        return isinstance(k, tuple) and k[0] == 'ps'

    def _deps(self, reads, writes, eng=None):
        d = {}

        def add(tok):
            if tok is None:
                return
            n, v = tok
            if d.get(n, 0) < v:
                d[n] = v
        for k in reads:
            add(self.lastw.get(k))
            if self._is_ps(k):
                for n, v in self.readers.get(k, {}).items():
                    if n != eng:
                        add((n, v))
        for k in writes:
            add(self.lastw.get(k))
            for n, v in self.readers.get(k, {}).items():
                add((n, v))
        return d

    def _wait(self, eng, d):
        for n, v in d.items():
            if self.waited.get((eng, n), 0) < v:
                self.E[eng].wait_ge(self.sems[n], v)
                self.waited[(eng, n)] = v

    def _commit(self, tok, reads, writes):
        for k in writes:
            self.lastw[k] = tok
            self.readers[k] = {}
        for k in reads:
            r = self.readers.setdefault(k, {})
            if r.get(tok[0], 0) < tok[1]:
                r[tok[0]] = tok[1]

    def op(self, eng, fn, reads=(), writes=()):
        self._wait(eng, self._deps(reads, writes, eng))
        ins = fn()
        self.cnt[eng] += 1
        tok = (eng, self.cnt[eng])
        ins.then_inc(self.sems[eng], 1)
        self._commit(tok, reads, writes)

    def group(self, eng, fns, reads=(), writes=()):
        self._wait(eng, self._deps(reads, writes, eng))
        ins = None
        for fn in fns:
            ins = fn()
        self.cnt[eng] += 1
        tok = (eng, self.cnt[eng])
        ins.then_inc(self.sems[eng], 1)
        self._commit(tok, reads, writes)

    def dma(self, q, out, in_, reads=(), writes=(), after=(), **kw):
        d = self._deps(reads, writes)
        for k in after:
            tok = self.lastw.get(k)
            if tok is not None and d.get(tok[0], 0) < tok[1]:
                d[tok[0]] = tok[1]
        self._wait(q, d)
        pool = self.dma_pool[q]
        i = self.dma_rr[q]
        self.dma_rr[q] = (i + 1) % len(pool)
        ent = pool[i]
        if ent[1] > 0 and self.waited.get((q, ent[0]), 0) < ent[1]:
            self.E[q].wait_ge(self.sems[ent[0]], ent[1])
            self.waited[(q, ent[0])] = ent[1]
        ins = self.E[q].dma_start(out=out, in_=in_, **kw)
        ent[1] += 16
        ins.then_inc(self.sems[ent[0]], 16)
        self._commit((ent[0], ent[1]), reads, writes)

    def finish(self):
        for q, pool in self.dma_pool.items():
            for nm, v in pool:
                if v > 0 and self.waited.get(('sp', nm), 0) < v:
                    self.E['sp'].wait_ge(self.sems[nm], v)
                    self.waited[('sp', nm)] = v


class SlotPool:
    def __init__(self, name, n):
        self.name = name
        self.free = list(range(n))
        self.peak = 0
        self.n = n

    def alloc(self, k=1, consecutive=False):
        if consecutive:
            fs = sorted(self.free)
            for i in range(len(fs) - k + 1):
                if fs[i + k - 1] - fs[i] == k - 1:
                    got = fs[i:i + k]
                    for g in got:
                        self.free.remove(g)
                    self.peak = max(self.peak, self.n - len(self.free))
                    return got
            raise RuntimeError('no consecutive slots in ' + self.name)
        assert len(self.free) >= k, 'out of slots in %s' % self.name
        got = [self.free.pop(0) for _ in range(k)]
        self.peak = max(self.peak, self.n - len(self.free))
        return got if k > 1 else got[0]

    def release(self, s):
        if isinstance(s, (list, tuple)):
            for x in s:
                self.release(x)
        else:
            assert s not in self.free
            self.free.append(s)
            self.free.sort()


def build_program(debug=None, stop=None):
    nc = bass.Bass("TRN2", target_bir_lowering=False)
    dt_in = lambda n, s: nc.dram_tensor(n, list(s), F32, kind="ExternalInput").ap()
    dt_out = lambda n, s: nc.dram_tensor(n, list(s), F32, kind="ExternalOutput").ap()
    xpT_d = dt_in("xpT", (1024, 512))
    xwT_d = dt_in("xwT", (1024, 896))
    ck_d = dt_in("ck", (256, 512))
    cv_d = dt_in("cv", (2, 256, 512))
    par_d = dt_in("params", (128, NPAR))
    ident_d = dt_in("ident", (128, 128))
    vmask_d = dt_in("vmask", (128, 320))
    invc_d = dt_in("invcnt", (128, 4 * PUW))
    t2r_d = dt_in("t2r", (128, 8, QTOT))
    fgb_d = dt_in("fgb", (128, 1024))
    wmod_d = dt_in("w_mod", (2, 1024, 3072))
    wine_d = dt_in("w_in_even", (1024, 3072))
    wpool_d = dt_in("w_pool", (4, 128, 128))
    woe_d = dt_in("w_out_even", (1024, 1024))
    wino_d = dt_in("w_in_odd", (1024, 3584))
    woo_d = dt_in("w_out_odd", (1024, 1024))
    yp_d = dt_out("yp", (512, 1024))
    ys_d = dt_out("ys", (256, 1024))
    nk_d = dt_out("nk", (512, 512))
    nv_d = dt_out("nv", (2, 256, 512))
    dbg_d = {}
    if debug:
        for nm, shp in debug.items():
            dbg_d[nm] = dt_out("dbg_" + nm, shp)

    with contextlib.ExitStack() as es:
        sb = lambda n, s, d: es.enter_context(nc.sbuf_tensor("sb_" + n, list(s), d))
        S = Sched(nc, es)
        poolf = sb("poolf", (128, NF, SLOT), F32)
        poolb = sb("poolb", (128, NB * SLOT), BF16)
        hT = sb("hT", (128, 8, 1408), BF16)
        ring = sb("ring", (128, 3, 8, 512), BF16)
        kT_p = sb("kT_p", (128, 4, 512), BF16)
        v_p = sb("v_p", (128, 2, 4, 512), BF16)
        v_w = sb("v_w", (128, 2, 7, 512), BF16)
        kcT = sb("kcT", (128, 4, 256), BF16)
        cvb = sb("cvb", (128, 2, 2, 512), BF16)
        oh = sb("oh", (128, 2, 128), BF16)
        t2rb = sb("t2rb", (128, 2, QTOT), BF16)
        wpool_b = sb("wpool_b", (128, 4, 128), BF16)
        par = sb("par", (128, NPAR), F32)
        mT = sb("mT", (128, 2, 48), F32)
        gs = sb("gs", (128, 2, 16), F32)
        scond = sb("scond", (128, 16), BF16)
        stt = sb("stt", (128, 24), F32)
        dummy = sb("dummy", (128, 4), F32)
        sst = sb("sst", (128, 44), F32)
        ones_f = sb("ones_f", (128, 128), F32)
        ident_f = sb("ident_f", (128, 128), F32)
        ident_b = sb("ident_b", (128, 128), BF16)
        ones_b = sb("ones_b", (128, 128), BF16)
        vmask = sb("vmask", (128, 320), F32)
        ps = es.enter_context(nc.psum_tensor("ps", [128, 4096], F32))
        FP = SlotPool('F', NF)
        BP = SlotPool('B', NB)

        pe, act, dve, gp, sp = nc.tensor, nc.scalar, nc.vector, nc.gpsimd, nc.sync

        def fs(s, a=0, b=SLOT):
            return poolf[:, s, a:b]

        def bs(s, a=0, b=SLOT):
            return poolb[:, s * SLOT + a:s * SLOT + b]

        def bsp(pr, s, a, b):
            return poolb[pr, s * SLOT + a:s * SLOT + b]

        def bank(b, a=0, n=512):
            return ps[:, b * 512 + a: b * 512 + a + n]

        def pair(b, a=0, n=1024):
            return ps[:, b * 512 + a: b * 512 + a + n]

        def pcol(off, n=1):
            return par[:, off:off + n]

        S.dma('sp', par[:, :], par_d[:, :], writes=['par'])
        S.dma('sp', ident_f[:, :], ident_d[:, :], writes=['ident_f'])
        S.dma('sp', vmask[:, :], vmask_d[:, :], writes=['vmask'])
        S.op('dve', lambda: dve.memset(ones_b[:, :], 1.0), writes=['ones_b'])
        S.op('dve', lambda: dve.memset(ps[:, 7 * 512 + 96:8 * 512], 0.0), writes=[('ps', 7)])
        S.op('dve', lambda: dve.memset(ones_f[:, :], 1.0), writes=['ones_f'])
        S.op('dve', lambda: dve.tensor_copy(out=ident_b[:, :], in_=ident_f[:, :]), reads=['ident_f'], writes=['ident_b'])
        S.op('act', lambda: act.activation(out=scond[:, :], in_=par[:, PC_COND:PC_COND + 16], func=AF.Silu),
             reads=['par'], writes=['scond'])

        def act_preload(func, col):
            S.op('act', lambda: act.activation(out=dummy[:, col:col + 1], in_=par[:, NPAR - 1:NPAR], func=func),
                 reads=['par'], writes=[('dummy', col)])

        act_preload(AF.Ln, 0)
        pieces = []

        def wpiece(w2d, c0):
            return w2d[:, c0:c0 + 512].rearrange("(c p) n -> p c n", p=128)
        for pc in range(4):
            pieces.append(('wm0', pc, wpiece(wmod_d[0], pc * 512)))
        for pc in range(6):
            pieces.append(('wie', pc, wpiece(wine_d, pc * 512)))
            if pc in (3, 4):
                pieces.append(('wm0', pc + 1, wpiece(wmod_d[0], (pc + 1) * 512)))
        for pc in range(4):
            pieces.append(('wm1', pc, wpiece(wmod_d[1], pc * 512)))
        for pc in range(2):
            pieces.append(('wm1', 4 + pc, wpiece(wmod_d[1], (4 + pc) * 512)))
        for pc in range(2):
            pieces.append(('woe', pc, wpiece(woe_d, pc * 512)))
        for pc in (1, 2, 0, 3, 4, 5, 6):
            pieces.append(('wio', pc, wpiece(wino_d, pc * 512)))
        for pc in range(2):
            pieces.append(('woo', pc, wpiece(woo_d, pc * 512)))
        piece_idx = {(a, b): i for i, (a, b, _) in enumerate(pieces)}
        issued = [0]
        xtra = BP.alloc(5, consecutive=True)
        xtra_ap = poolb[:, xtra[0] * SLOT:xtra[0] * SLOT + 4096].rearrange("p (k n) -> p k n", k=8)
        XK = [('B', x_) for x_ in xtra]

        def slot_of(i):
            return i if i < 3 else ('X' if i == 3 else (i - 1) % 3)

        ring_limit = [None]

        def ring_issue_upto(i, after=()):
            if ring_limit[0] is not None:
                i = min(i, ring_limit[0])
            while issued[0] <= i and issued[0] < len(pieces):
                p = issued[0]
                sl = slot_of(p)
                if sl == 'X':
                    S.dma('pool', xtra_ap, pieces[p][2], writes=XK, after=list(after))
                else:
                    S.dma('pool', ring[:, sl, :, :], pieces[p][2], writes=[('ring', sl)], after=list(after))
                issued[0] += 1

        def ring_get(kind, pc):
            i = piece_idx[(kind, pc)]
            ring_issue_upto(i + 2)
            return slot_of(i)

        ring_issue_upto(0)
        S.op('pool', lambda: gp.memset(v_p[:, :, :, :], 0.0), writes=['v_p'])
        S.op('pool', lambda: gp.memset(v_w[:, :, :, :], 0.0), writes=[('v_w', wb) for wb in range(7)])
        S.op('pool', lambda: gp.memset(oh[:, :, :], 0.0), writes=['oh'])
        S.op('pool', lambda: gp.memset(oh[:, 0, 0:64], 1.0), writes=['oh'])
        S.op('pool', lambda: gp.memset(oh[:, 1, 64:128], 1.0), writes=['oh'])

        MODB = 7

        def mod_piece(l, pc):
            slot = ring_get('wm%d' % l, pc)
            fns = []
            for n4 in range(4):
                n = pc * 4 + n4
                for k in range(8):
                    wsrc = xtra_ap if slot == 'X' else ring[:, slot, :, :]
                    fns.append(lambda n=n, n4=n4, k=k, wsrc=wsrc: pe.matmul(
                        bank(MODB, l * 48 + n * 2, 2), lhsT=wsrc[:, k, n4 * 128:(n4 + 1) * 128],
                        rhs=scond[:, k * 2:k * 2 + 2], start=(k == 0), stop=(k == 7)))
            S.group('pe', fns, reads=(XK if slot == 'X' else [('ring', slot)]) + ['scond'], writes=[('ps', MODB)])
            if slot == 'X':
                BP.release(xtra)

        def mod_finish(l):
            S.op('dve', lambda: dve.tensor_tensor(out=mT[:, l, 0:32], in0=bank(MODB, l * 48, 32),
                                                  in1=par[:, PC_BMOD + l * 48:PC_BMOD + l * 48 + 32], op=ALU.add),
                 reads=[('ps', MODB), 'par'], writes=[('mT', l)])
            S.op('dve', lambda: dve.scalar_tensor_tensor(out=gs[:, l, :], in0=mT[:, l, 16:32], scalar=1.0,
                                                         in1=par[:, PC_G + l * 16:PC_G + (l + 1) * 16],
                                                         op0=ALU.add, op1=ALU.mult),
                 reads=[('mT', l), 'par'], writes=[('gs', l)])

        def mod_finish_gate(l):
            S.op('dve', lambda: dve.tensor_tensor(out=mT[:, l, 32:48], in0=bank(MODB, l * 48 + 32, 16),
                                                  in1=par[:, PC_BMOD + l * 48 + 32:PC_BMOD + (l + 1) * 48], op=ALU.add),
                 reads=[('ps', MODB), 'par'], writes=[('mTg', l)])

        def shift_col(l, c, q):
            return mT[:, l, c * 2 + q:c * 2 + q + 1]

        def gs_col(l, c, q):
            return gs[:, l, c * 2 + q:c * 2 + q + 1]

        def gate_col(l, c, q):
            return mT[:, l, 32 + c * 2 + q:32 + c * 2 + q + 1]

        if stop == 0:
            S.finish()
            return nc
        xm = FP.alloc(8, consecutive=True)
        xw = FP.alloc(8, consecutive=True)
        XM = [('F', s) for s in xm]
        XW = [('F', s) for s in xw]
        poolbf = poolb.bitcast(F32)
        for c in range(8):
            S.dma('sp', fs(xm[c], 0, 512), xpT_d[c * 128:(c + 1) * 128, :], writes=[XM[c]])
        for c in range(8):
            S.dma('sp', fs(xw[c], 0, TW), xwT_d[c * 128:(c + 1) * 128, :], writes=[XW[c]])
            if c == 1:
                ring_issue_upto(3, after=[XW[1]])

        def rms_stats(chunks, keys, col_ranges, psbanks):
            sq = BP.alloc(2)
            for c in range(8):
                q = sq[c % 2]
                ncols = max(r[0] + r[1] for r in col_ranges)
                S.op('act', lambda c=c, q=q, ncols=ncols: act.activation(out=bs(q, 0, ncols), in_=fs(chunks[c], 0, ncols), func=AF.Square),
                     reads=[keys[c]], writes=[('B', q)])
                fns = [lambda c=c, q=q, r=r: pe.matmul(bank(r[2], r[3], r[1]), lhsT=ones_b[:, :], rhs=bs(q, r[0], r[0] + r[1]),
                                                      start=(c == 0), stop=(c == 7)) for r in col_ranges]
                S.group('pe', fns, reads=[('B', q), 'ones_b'], writes=[('ps', b) for b in psbanks])
            BP.release(sq)

        def rstd_from(bank0, ncols, scale, to_psum=False):
            r = FP.alloc()
            nb = (ncols + 511) // 512
            keys = [('ps', bank0 + i) for i in range(nb)]
            S.op('act', lambda: act.activation(out=fs(r, 0, ncols), in_=ps[:, bank0 * 512:bank0 * 512 + ncols], func=AF.Ln,
                                               bias=par[:, NPAR - 1:NPAR], scale=scale),
                 reads=keys + ['par'], writes=[('F', r)])
            if to_psum:
                S.op('act', lambda: act.activation(out=ps[:, bank0 * 512:bank0 * 512 + ncols], in_=fs(r, 0, ncols), func=AF.Exp, scale=-0.5),
                     reads=[('F', r)], writes=keys)
                FP.release(r)
                return None
            S.op('act', lambda: act.activation(out=fs(r, 0, ncols), in_=fs(r, 0, ncols), func=AF.Exp, scale=-0.5),
                 reads=[('F', r)], writes=[('F', r)])
            return r

        def make_h_mult(chunks, keys, rbank, lo, hi, tmps):
            rk = [('ps', rbank + i) for i in range(lo // 512, (hi + 511) // 512)]
            for c in range(8):
                if tmps is None:
                    oap, tk = fs(chunks[c], lo, hi), keys[c]
                else:
                    oap, tk = tmps[c][0](lo, hi), tmps[c][1]
                S.op('dve', lambda c=c, oap=oap: dve.tensor_tensor(out=oap, in0=fs(chunks[c], lo, hi),
                                                                    in1=ps[:, rbank * 512 + lo:rbank * 512 + hi], op=ALU.mult),
                     reads=[keys[c]] + rk, writes=[tk] if not isinstance(tk, list) else tk)

        def make_h_affine(l, chunks, keys, regions, tmps, eng_of):
            for c in range(8):
                for ri, (c0, n, q, h0) in enumerate(regions):
                    if tmps is None:
                        iap, tk = fs(chunks[c], c0, c0 + n), keys[c]
                    else:
                        iap, tk = tmps[c][0](c0, c0 + n), tmps[c][1]
                    rd = (tk if isinstance(tk, list) else [tk]) + [('mT', l), ('gs', l)]
                    e_ = eng_of(c, ri)
                    if e_ == 'act':
                        S.op('act', lambda c=c, iap=iap, n=n, q=q, h0=h0: act.activation(
                            out=hT[:, c, h0:h0 + n], in_=iap, func=AF.Identity, bias=shift_col(l, c, q), scale=gs_col(l, c, q)),
                            reads=rd, writes=[('hTp' if h0 < 512 else 'hTs', c)])
                    else:
                        eo = dve if e_ == 'dve' else gp
                        S.op(e_, lambda c=c, iap=iap, n=n, q=q, h0=h0, eo=eo: eo.tensor_scalar(
                            out=hT[:, c, h0:h0 + n], in0=iap, scalar1=gs_col(l, c, q), scalar2=shift_col(l, c, q),
                            op0=ALU.mult, op1=ALU.add),
                            reads=rd, writes=[('hTp' if h0 < 512 else 'hTs', c)])

        def make_h(l, chunks, keys, rbank, regions, tmp, pool_regions=()):
            lo = min(r[0] for r in regions)
            hi = max(r[0] + r[1] for r in regions)
            tmps = None if tmp is None else [((lambda a, b, t=t: fs(t, a, b)), ('F', t)) for t in tmp]
            make_h_mult(chunks, keys, rbank, lo, hi, tmps)
            make_h_affine(l, chunks, keys, regions, tmps, lambda c, ri: 'pool' if ri in pool_regions else 'act')

        if stop == 1:
            S.finish()
            return nc
        rms_stats(xm, XM, [(0, 512, 2, 0)], [2])
        rms_stats(xw, XW, [(0, 512, 3, 0), (512, 384, 4, 0)], [3, 4])
        S.op('dve', lambda: dve.tensor_copy(out=poolf[:, xm[0]:xm[0] + 8, 512:832], in_=poolf[:, xw[0]:xw[0] + 8, EXT0:EXT0 + 320]),
             reads=XW, writes=XM)
        mod_piece(0, 0)
        mod_piece(0, 1)
        rstd_from(2, 512, 1.0 / 1024, to_psum=True)
        rstd_from(3, 896, 1.0 / 1024, to_psum=True)
        mod_piece(0, 2)
        rp = FP.alloc()
        S.op('act', lambda: act.copy(out=fs(rp, 0, 512), in_=bank(2, 0, 512)), reads=[('ps', 2)], writes=[('F', rp)])
        make_h_mult(xw, XW, 3, 0, 896, None)
        tB = BP.alloc(16, consecutive=True)
        tmpsP = [((lambda a, b_, k=k: poolbf[:, (tB[0] + 2 * k) * 448 + a:(tB[0] + 2 * k) * 448 + b_]),
                  [('B', tB[2 * k]), ('B', tB[2 * k + 1])]) for k in range(8)]
        for c in range(8):
            S.op('pool', lambda c=c: gp.tensor_tensor(out=tmpsP[c][0](0, 512), in0=fs(xm[c], 0, 512), in1=fs(rp, 0, 512), op=ALU.mult),
                 reads=[XM[c], ('F', rp)], writes=tmpsP[c][1])
        mod_piece(0, 3)
        mod_finish(0)
        make_h_affine(0, xw, XW, [(0, 896, 1, 512)], None, lambda c, ri: 'act' if c < 4 else 'dve')
        make_h_affine(0, xm, XM, [(0, 512, 0, 0)], tmpsP, lambda c, ri: 'pool' if c < 4 else ('act' if c < 6 else 'dve'))
        FP.release(xw)
        FP.release(rp)
        BP.release(tB)
        HTP = [('hTp', c) for c in range(8)]
        HTS = [('hTs', c) for c in range(8)]
        HT = HTP + HTS

        if stop == 2:
            S.finish()
            return nc
        def proj_P(slot, n4, pb, fine=False):
            fns = [lambda k=k: pe.matmul(bank(pb, 0, 512), lhsT=ring[:, slot, k, n4 * 128:(n4 + 1) * 128],
                                         rhs=hT[:, k, 0:512], start=(k == 0), stop=(k == 7)) for k in range(8)]
            if fine:
                for k in range(8):
                    S.group('pe', [fns[k]], reads=[('ring', slot), HTP[k]], writes=[('ps', pb)])
            else:
                S.group('pe', fns, reads=[('ring', slot)] + HTP, writes=[('ps', pb)])

        def proj_S(slot, n4, pb, sample_cols=(864, 320), fine=False):
            s0, sn = sample_cols[0], sample_cols[1]
            so = sample_cols[2] if len(sample_cols) > 2 else 0
            fns = [lambda k=k: pe.matmul(bank(pb + 1, so, sn), lhsT=ring[:, slot, k, n4 * 128:(n4 + 1) * 128],
                                         rhs=hT[:, k, s0:s0 + sn], start=(k == 0), stop=(k == 7)) for k in range(8)]
            if fine:
                for k in range(8):
                    S.group('pe', [fns[k]], reads=[('ring', slot), HTS[k]], writes=[('ps', pb + 1)])
            else:
                S.group('pe', fns, reads=[('ring', slot)] + HTS, writes=[('ps', pb + 1)])

        def proj_chunk(slot, n4, pb, sample_cols=(864, 320), fine=False):
            proj_P(slot, n4, pb, fine=fine)
            proj_S(slot, n4, pb, sample_cols, fine=fine)

        pbs = [0, 2, 4]
        pbi = [0]

        bg_hook = [None]

        def next_pb():
            b = pbs[pbi[0] % len(pbs)]
            pbi[0] += 1
            if bg_hook[0] is not None:
                bg_hook[0]()
            return b

        upad = FP.alloc(4)
        slot = ring_get('wie', 0)
        for g in range(4):
            u = upad[g]
            S.op('pool', lambda u=u: gp.memset(fs(u, 0, PUW), 0.0), writes=[('F', u)])
            pb = next_pb()
            proj_chunk(slot, g, pb, fine=(g == 0))
            S.op('act', lambda u=u, pb=pb: act.copy(out=fs(u, 8, 8 + 528).rearrange("p (s t) -> p s t", s=2)[:, :, 0:256],
                                                    in_=bank(pb).rearrange("p (s t) -> p s t", s=2)),
                 reads=[('ps', pb)], writes=[('F', u)])
            S.op('dve', lambda u=u, pb=pb: dve.tensor_tensor(out=fs(u, 544, 864), in0=bank(pb + 1, 0, 320), in1=vmask[:, :], op=ALU.mult),
                 reads=[('ps', pb + 1), 'vmask'], writes=[('F', u)])
        abo = BP.alloc(8)

        PPS = {}

        def pool_pre_gen(g):
            u = upad[g]
            ic = FP.alloc()
            S.dma('sp', fs(ic, 0, PUW), invc_d[:, g * PUW:(g + 1) * PUW], writes=[('F', ic)])
            wa, wb_ = FP.alloc(), FP.alloc()
            S.op('dve', lambda u=u, wa=wa: dve.tensor_tensor(out=fs(wa, 1, PUW), in0=fs(u, 0, PUW - 1), in1=fs(u, 1, PUW), op=ALU.add),
                 reads=[('F', u)], writes=[('F', wa)])
            yield
            cur, nxt = wa, wb_
            lo = 1
            for step in range(g):
                sh = 1 << step
                a0, a1 = lo + sh, PUW - sh
                S.op('dve', lambda cur=cur, nxt=nxt, a0=a0, a1=a1, sh=sh: dve.tensor_tensor(
                    out=fs(nxt, a0, a1), in0=fs(cur, a0 - sh, a1 - sh), in1=fs(cur, a0 + sh, a1 + sh), op=ALU.add),
                    reads=[('F', cur)], writes=[('F', nxt)])
                yield
                cur, nxt = nxt, cur
                lo = a0
            S.op('dve', lambda cur=cur, ic=ic: dve.tensor_tensor(out=fs(cur, 8, 864), in0=fs(cur, 8, 864), in1=fs(ic, 8, 864), op=ALU.mult),
                 reads=[('F', cur), ('F', ic)], writes=[('F', cur)])
            yield
            pp = BP.alloc()
            S.op('dve', lambda cur=cur, u=u, pp=pp: dve.tensor_tensor(out=bs(pp, 8, 864), in0=fs(cur, 8, 864), in1=fs(u, 8, 864), op=ALU.subtract),
                 reads=[('F', cur), ('F', u)], writes=[('B', pp)])
            yield
            FP.release([ic, wa, wb_])
            PPS[g] = pp
            PRE_DONE.add(g)

        PRE_DONE = set()
        bg = []

        def bg_step(n=1):
            for _ in range(n):
                while bg:
                    try:
                        next(bg[0])
                        break
                    except StopIteration:
                        bg.pop(0)

        def pool_pre(g):
            bg.append(pool_pre_gen(g))
            bg_hook[0] = bg_step

        def pool_need(g):
            while g not in PRE_DONE:
                bg_step()

        def pool_post(g):
            pool_need(g)
            pp = PPS[g]
            pb = next_pb()
            fns = [lambda pp=pp, pb=pb, g=g, r=r: pe.matmul(ps[:, pb * 512 + r[1]:pb * 512 + r[1] + r[2]], lhsT=wpool_b[:, g, :],
                                                           rhs=bs(pp, r[0], r[0] + r[2]), start=True, stop=True)
                   for r in ((8, 0, 256), (272, 256, 256), (544, 512, 320))]
            S.group('pe', fns, reads=[('B', pp), 'wpool_b'], writes=[('ps', pb), ('ps', pb + 1)])
            S.op('dve', lambda g=g, pb=pb: dve.scalar_tensor_tensor(out=bs(abo[g], 0, TM), in0=pair(pb, 0, TM), scalar=pcol(PC_PSC + g),
                                                                    in1=bs(siluA[g], 0, TM), op0=ALU.mult, op1=ALU.mult),
                 reads=[('ps', pb), ('ps', pb + 1), 'par', ('B', siluA[g])], writes=[('B', abo[g])])
            BP.release(pp)

        if stop == 3:
            S.finish()
            return nc
        S.dma('pool', wpool_b[:, :, :], wpool_d.rearrange("g c d -> c g d"), writes=['wpool_b'])
        siluA = BP.alloc(4)
        slot = ring_get('wie', 1)
        for g in range(4):
            pb = next_pb()
            proj_chunk(slot, g, pb, sample_cols=(864 + 16, 288, 16))
            S.op('act', lambda g=g, pb=pb: act.activation(out=bs(siluA[g], 0, TM), in_=pair(pb, 0, TM), func=AF.Silu),
                 reads=[('ps', pb), ('ps', pb + 1)], writes=[('B', siluA[g])])
        pool_pre(0)
        pool_pre(1)
        qT = BP.alloc(8)
        slot = ring_get('wie', 2)
        for g in range(4):
            pb = next_pb()
            proj_chunk(slot, g, pb, sample_cols=(864 + 16, 288, 16))
            for hh in range(2):
                qs = qT[2 * g + hh]
                pr = slice(hh * 64, hh * 64 + 64)
                S.op('pool', lambda qs=qs: gp.memset(bs(qs, 0, TM), 0.0), writes=[('B', qs)])
                S.op('dve', lambda qs=qs, pr=pr, pb=pb: dve.tensor_scalar(out=bsp(pr, qs, 0, TM), in0=ps[pr, pb * 512:pb * 512 + TM], scalar1=0.125,
                                                                        scalar2=None, op0=ALU.mult),
                     reads=[('ps', pb), ('ps', pb + 1)], writes=[('B', qs)])
        pool_pre(2)
        pool_pre(3)


        if stop == 4:
            S.finish()
            return nc
        def t2r_load(h):
            S.dma('pool', t2rb[:, h % 2, :], t2r_d[:, h, :], writes=[('t2rb', h % 2)])
        ck_tm = []

        def attn_table_loads():
            t2r_load(0)
            for lb in range(2):
                for hh in range(2):
                    S.dma('pool', cvb[:, hh, lb, :], cv_d[hh, lb * 128:(lb + 1) * 128, :], writes=[('cvb', lb)])
            ck_tm.extend(BP.alloc(2))
            for lb in range(2):
                S.dma('pool', bs(ck_tm[lb], 0, 512), ck_d[lb * 128:(lb + 1) * 128, :], writes=[('B', ck_tm[lb])])

        kT_w = BP.alloc(4)
        kvst = FP.alloc(3)
        ktp_pend = []
        for which, pcn in (('k', 3), ('v', 4)):
            slot = ring_get('wie', pcn)
            if which == 'k':
                attn_table_loads()
            out_d = nk_d if which == 'k' else nv_d
            if which == 'k':
                for g in range(4):
                    pb = next_pb()
                    fns = [lambda k=k, g=g, pb=pb: pe.matmul(bank(pb, 0, 512), lhsT=ring[:, slot, k, g * 128:(g + 1) * 128],
                                                             rhs=hT[:, k, 0:512], start=(k == 0), stop=(k == 7)) for k in range(8)]
                    S.group('pe', fns, reads=[('ring', slot)] + HTP, writes=[('ps', pb)])
                    st = kvst[g % 3]
                    S.op('act', lambda st=st, pb=pb: act.copy(out=fs(st, 0, 512), in_=bank(pb)), reads=[('ps', pb)], writes=[('F', st)])
                    S.dma('sp', nk_d[g * 128:(g + 1) * 128, :], fs(st, 0, 512), reads=[('F', st)])
                    S.op('dve', lambda g=g, pb=pb: dve.tensor_copy(out=kT_p[:, g, :], in_=bank(pb)), reads=[('ps', pb)], writes=['kT_p'])
            for tb in (range(4) if which == 'v' else ()):
                pb = next_pb()
                fns = [lambda k=k, tb=tb, pb=pb: pe.matmul(bank(pb, 0, 512), lhsT=hT[:, k, tb * 128:(tb + 1) * 128],
                                                           rhs=ring[:, slot, k, :], start=(k == 0), stop=(k == 7)) for k in range(8)]
                S.group('pe', fns, reads=[('ring', slot)] + HTP, writes=[('ps', pb)])
                st = kvst[tb % 3]
                S.op('act', lambda st=st, pb=pb: act.copy(out=fs(st, 0, 512), in_=bank(pb)), reads=[('ps', pb)], writes=[('F', st)])
                sq, t0 = tb // 2, (tb % 2) * 128
                S.dma('sp', out_d[sq, t0:t0 + 128, :], fs(st, 0, 512), reads=[('F', st)])
                if which == 'k':
                    def k_transp(tb=tb, st=st):
                        pb2 = 6
                        fns = [lambda c4=c4: pe.transpose(out=bank(pb2, c4 * 128, 128), in_=fs(st, c4 * 128, c4 * 128 + 128),
                                                          identity=ident_f[:, :]) for c4 in range(4)]
                        S.group('pe', fns, reads=[('F', st), 'ident_f'], writes=[('ps', pb2)])
                        S.op('dve', lambda: dve.tensor_copy(out=kT_p[:, :, tb * 128:(tb + 1) * 128],
                                                            in_=bank(pb2).rearrange("p (c t) -> p c t", c=4)),
                             reads=[('ps', pb2)], writes=['kT_p'])
                    if ktp_pend:
                        ktp_pend.pop(0)()
                    ktp_pend.append(k_transp)
                else:
                    for hh in range(2):
                        S.op('dve', lambda tb=tb, pb=pb, hh=hh: dve.tensor_copy(
                            out=v_p[:, hh, tb, :].rearrange("p (g e d) -> p g e d", g=4, e=2)[:, :, hh, :],
                            in_=bank(pb).rearrange("p (g e d) -> p g e d", g=4, e=2)[:, :, hh, :]),
                            reads=[('ps', pb)], writes=['v_p'])
            if which == 'k':
                for g in range(4):
                    pb = next_pb()
                    if g == 1 and ktp_pend:
                        ktp_pend.pop(0)()
                    fns = []
                    for k in range(8):
                        fns.append(lambda k=k, g=g, pb=pb: pe.matmul(bank(pb, 0, 512), lhsT=ring[:, slot, k, g * 128:(g + 1) * 128],
                                                                     rhs=hT[:, k, 512:1024], start=(k == 0), stop=(k == 7)))
                    for k in range(8):
                        fns.append(lambda k=k, g=g, pb=pb: pe.matmul(bank(pb + 1, 0, 384), lhsT=ring[:, slot, k, g * 128:(g + 1) * 128],
                                                                     rhs=hT[:, k, 1024:1408], start=(k == 0), stop=(k == 7)))
                    S.group('pe', fns, reads=[('ring', slot)] + HT, writes=[('ps', pb), ('ps', pb + 1)])
                    S.op('act', lambda g=g, pb=pb: act.copy(out=bs(kT_w[g], 0, TW), in_=pair(pb, 0, TW)),
                         reads=[('ps', pb), ('ps', pb + 1)], writes=[('B', kT_w[g])])
            else:
                for wb in range(7):
                    pb = next_pb()
                    fns = [lambda k=k, wb=wb, pb=pb: pe.matmul(bank(pb, 0, 512), lhsT=hT[:, k, 512 + wb * 128:512 + (wb + 1) * 128],
                                                               rhs=ring[:, slot, k, :], start=(k == 0), stop=(k == 7)) for k in range(8)]
                    S.group('pe', fns, reads=[('ring', slot)] + HT, writes=[('ps', pb)])
                    for hh in range(2):
                        oap = v_w[:, hh, wb, :].rearrange("p (g e d) -> p g e d", g=4, e=2)[:, :, hh, :]
                        iap = bank(pb).rearrange("p (g e d) -> p g e d", g=4, e=2)[:, :, hh, :]
                        if hh == 1:
                            S.op('act', lambda oap=oap, iap=iap: act.copy(out=oap, in_=iap), reads=[('ps', pb)], writes=[('v_w', wb)])
                        else:
                            S.op('dve', lambda oap=oap, iap=iap: dve.tensor_copy(out=oap, in_=iap), reads=[('ps', pb)], writes=[('v_w', wb)])
            mod_piece(0, pcn + 1)
            if pcn == 4:
                mod_finish_gate(0)
            pool_post(2 * (pcn - 3))
            pool_post(2 * (pcn - 3) + 1)
        FP.release(kvst)
        FP.release(upad)
        BP.release(siluA)

        siluB = BP.alloc(4)
        slot = ring_get('wie', 5)
        for g in range(4):
            pb = next_pb()
            proj_chunk(slot, g, pb, sample_cols=(864 + 16, 288, 16))
            S.op('act', lambda g=g, pb=pb: act.activation(out=bs(siluB[g], 0, TM), in_=pair(pb, 0, TM), func=AF.Silu),
                 reads=[('ps', pb), ('ps', pb + 1)], writes=[('B', siluB[g])])

        if stop == 5:
            S.finish()
            return nc
        psb = ps.bitcast(BF16)
        CKB = 6
        fns = []
        for hp in range(4):
            for lb in range(2):
                idx = hp * 2 + lb
                fns.append(lambda hp=hp, lb=lb, idx=idx: pe.transpose(
                    out=psb[:, CKB * 1024 + idx * 128: CKB * 1024 + (idx + 1) * 128],
                    in_=bs(ck_tm[lb], hp * 128, (hp + 1) * 128), identity=ident_b[:, :]))
        S.group('pe', fns, reads=[('B', ck_tm[0]), ('B', ck_tm[1]), 'ident_b'], writes=[('ps', CKB)])
        BP.release(ck_tm)
        S.op('dve', lambda: dve.tensor_copy(out=kcT[:, :, :].rearrange("p a b -> p (a b)"), in_=psb[:, CKB * 1024:CKB * 1024 + 1024]),
             reads=[('ps', CKB)], writes=['kcT'])

        if stop == 6:
            S.finish()
            return nc
        ATT, DEN = 0, 2
        sbanks = [4, 5, 6]
        PT = BP.alloc(4)

        tasks = []
        for hp in range(4):
            for sq in range(2):
                for hh in range(2):
                    tasks.append(('p', hp, hh, sq, 0))
            for hh in range(2):
                for kb in (-1, 2, 3, 4, 5, 6, 7, 8):
                    tasks.append(('s', hp, hh, 0, kb))

        def emit_S(ti):
            kind, hp, hh, sq, kb = tasks[ti]
            h = 2 * hp + hh
            pr = slice(hh * 64, hh * 64 + 64)
            sbk = sbanks[ti % 3]
            if kind == 'p':
                S.group('pe', [lambda kb_=kb_: pe.matmul(
                    bank(sbk, kb_ * 256, 256), lhsT=kT_p[:, hp, sq * 256 + kb_ * 128: sq * 256 + (kb_ + 1) * 128],
                    rhs=bs(qT[h], sq * 256, (sq + 1) * 256), start=True, stop=True) for kb_ in range(2)],
                    reads=['kT_p', ('B', qT[h])], writes=[('ps', sbk)])
            elif kb < 7:
                if kb == -1 and h + 1 < 8:
                    t2r_load(h + 1)
                fns = []
                off_ = 0
                for kb_ in ((0, 1) if kb == -1 else (kb,)):
                    qa, qb = QR[kb_]
                    nq = qb - qa
                    fns.append(lambda kb_=kb_, off_=off_, nq=nq: pe.matmul(
                        bank(sbk, off_, nq), lhsT=ident_b[:, :], rhs=t2rb[:, h % 2, QOFF[kb_]:QOFF[kb_] + nq], start=True, stop=False))
                    fns.append(lambda kb_=kb_, off_=off_, nq=nq, qa=qa, qb=qb: pe.matmul(
                        bank(sbk, off_, nq), lhsT=bs(kT_w[hp], kb_ * 128, (kb_ + 1) * 128),
                        rhs=bs(qT[h], 512 + qa, 512 + qb), start=False, stop=True))
                    off_ += nq
                S.group('pe', fns, reads=['ident_b', ('t2rb', h % 2), ('B', kT_w[hp]), ('B', qT[h])],
                        writes=[('ps', sbk)])
            else:
                lb = kb - 7
                S.group('pe', [lambda: pe.matmul(bank(sbk, 0, 320), lhsT=kcT[:, hp, lb * 128:(lb + 1) * 128],
                                                 rhs=bs(qT[h], 512, 832), start=True, stop=True)],
                        reads=['kcT', ('B', qT[h])], writes=[('ps', sbk)])

        def emit_exp_pv(ti):
            kind, hp, hh, sq, kb = tasks[ti]
            h = 2 * hp + hh
            pr = slice(hh * 64, hh * 64 + 64)
            sbk = sbanks[ti % 3]
            p_ = PT[ti % 4]
            qa = 0
            if kind == 's' and kb == -1:
                n = (QR[0][1] - QR[0][0]) + (QR[1][1] - QR[1][0])
            elif kind == 's' and kb < 7:
                qa, qb_ = QR[kb]
                n = qb_ - qa
            else:
                n = 512 if kind == 'p' else 320
            S.op('act', lambda: act.activation(out=bs(p_, 0, n), in_=bank(sbk, 0, n), func=AF.Exp),
                 reads=[('ps', sbk)], writes=[('B', p_)])
            if kind == 'p':
                c0 = sq * 256
                fns = []
                for (bk_, lh_) in ((ATT, None), (DEN, oh[:, hh, :])):
                    for kb_ in range(2):
                        l_ = v_p[:, hh, sq * 2 + kb_, hp * 128:(hp + 1) * 128] if lh_ is None else lh_
                        fns.append(lambda bk_=bk_, kb_=kb_, l_=l_: pe.matmul(
                            ps[:, bk_ * 512 + c0:bk_ * 512 + c0 + 256], lhsT=l_, rhs=bs(p_, kb_ * 256, kb_ * 256 + 256),
                            start=(hh == 0 and kb_ == 0), stop=(hh == 1 and kb_ == 1)))
                S.group('pe', fns, reads=['v_p', 'oh', ('B', p_)], writes=[('ps', ATT), ('ps', DEN)])
                return
            elif kb == -1:
                ab, db = ATT + 1, DEN + 1
                fns = []
                off_ = 0
                for kb_ in (0, 1):
                    qa_, qb_ = QR[kb_]
                    nq = qb_ - qa_
                    for (bk_, l_) in ((ab, v_w[:, hh, kb_, hp * 128:(hp + 1) * 128]), (db, oh[:, hh, :])):
                        fns.append(lambda bk_=bk_, l_=l_, qa_=qa_, nq=nq, off_=off_, kb_=kb_: pe.matmul(
                            ps[:, bk_ * 512 + qa_:bk_ * 512 + qa_ + nq], lhsT=l_, rhs=bs(p_, off_, off_ + nq),
                            start=(hh == 0 and kb_ == 0), stop=False, skip_group_check=True))
                    off_ += nq
                S.group('pe', fns, reads=[('v_w', 0), ('v_w', 1), 'oh', ('B', p_)], writes=[('ps', ab), ('ps', db)])
                return
            elif kb < 7:
                c0, ab, db, last = 0, ATT + 1, DEN + 1, 8
                vl, vkey = v_w[:, hh, kb, hp * 128:(hp + 1) * 128], ('v_w', kb)
            else:
                c0, ab, db, last = 0, ATT + 1, DEN + 1, 8
                vl, vkey = cvb[:, hh, kb - 7, hp * 128:(hp + 1) * 128], ('cvb', kb - 7)
            st_ = False
            sp_ = (hh == 1 and kb == last)
            c0 = c0 + qa
            fns = [lambda: pe.matmul(ps[:, ab * 512 + c0:ab * 512 + c0 + n], lhsT=vl, rhs=bs(p_, 0, n), start=st_, stop=sp_,
                                     skip_group_check=(kind == 's')),
                   lambda: pe.matmul(ps[:, db * 512 + c0:db * 512 + c0 + n], lhsT=oh[:, hh, :], rhs=bs(p_, 0, n), start=st_, stop=sp_,
                                     skip_group_check=(kind == 's'))]
            S.group('pe', fns, reads=[vkey, 'oh', ('B', p_)], writes=[('ps', ab), ('ps', db)])

        def finalize_pair(hp):
            denc = FP.alloc()
            attc = FP.alloc()
            S.op('act', lambda: act.activation(out=fs(denc, 0, TM), in_=pair(DEN, 0, TM), func=AF.Ln),
                 reads=[('ps', DEN), ('ps', DEN + 1)], writes=[('F', denc)])
            S.op('dve', lambda: dve.tensor_copy(out=fs(attc, 0, TM), in_=pair(ATT, 0, TM)),
                 reads=[('ps', ATT), ('ps', ATT + 1)], writes=[('F', attc)])
            S.op('act', lambda: act.activation(out=fs(denc, 0, TM), in_=fs(denc, 0, TM), func=AF.Exp, scale=-1.0),
                 reads=[('F', denc)], writes=[('F', denc)])
            S.op('dve', lambda: dve.tensor_tensor(out=fs(attc, 0, TM), in0=fs(attc, 0, TM), in1=fs(denc, 0, TM), op=ALU.mult),
                 reads=[('F', attc), ('F', denc)], writes=[('F', attc)])
            S.op('dve', lambda: dve.tensor_tensor(out=bs(abo[4 + hp], 0, TM), in0=fs(attc, 0, TM), in1=bs(siluB[hp], 0, TM), op=ALU.mult),
                 reads=[('F', attc), ('B', siluB[hp])], writes=[('B', abo[4 + hp])])
            FP.release([denc, attc])

        LOOK = 2
        NT = len(tasks)
        for ti in range(min(LOOK, NT)):
            emit_S(ti)
        for ti in range(NT):
            if ti + LOOK < NT:
                emit_S(ti + LOOK)
            emit_exp_pv(ti)
            if ti % 20 == 19:
                finalize_pair(ti // 20)
            if ti in (10, 28, 46, 60, 68, 74):
                mod_piece(1, (10, 28, 46, 60, 68, 74).index(ti))
        mod_finish(1)
        mod_finish_gate(1)
        BP.release(PT)
        BP.release(qT)
        BP.release(kT_w)
        BP.release(siluB)

        if stop == 8:
            S.finish()
            return nc
        ABO = [('B', s) for s in abo]
        ring_limit[0] = piece_idx[('wio', 1)]
        wslots = [ring_get('woe', 0), ring_get('woe', 1)]
        sq1 = BP.alloc(2)
        sbk4 = [0, 1, 2, 3, 6, 7]
        wcnt = [0]
        tmpL1 = FP.alloc(8)
        tmpsL1 = [((lambda a_, b_, t=t: fs(t, a_, b_)), ('F', t)) for t in tmpL1]

        def l1_stat(n, region):
            q = sq1[n % 2]
            c0, ncol, bk = (0, 512, 4) if region == 'p' else (512, 320, 5)
            S.op('act', lambda: act.activation(out=bs(q, c0, c0 + ncol), in_=fs(xm[n], c0, c0 + ncol), func=AF.Square),
                 reads=[XM[n]], writes=[('B', q)])
            S.group('pe', [lambda: pe.matmul(bank(bk, 0, ncol), lhsT=ones_b[:, :], rhs=bs(q, c0, c0 + ncol), start=(n == 0), stop=(n == 7))],
                    reads=[('B', q), 'ones_b'], writes=[('ps', bk)])

        def chain_gen(region):
            if region == 'p':
                rstd_from(4, 512, 1.0 / 1024, to_psum=True)
                yield
                rk, lo, hi, regs, rb = [('ps', 4)], 0, 512, [(0, 512, 0, 0)], 4
            else:
                rstd_from(5, 320, 1.0 / 1024, to_psum=True)
                yield
                rk, lo, hi, regs, rb = [('ps', 5)], 512, 832, [(512, 320, 1, 512)], 4
            for c in range(8):
                oap, tk = tmpsL1[c][0](lo, hi), tmpsL1[c][1]
                S.op('dve', lambda c=c, oap=oap: dve.tensor_tensor(out=oap, in0=fs(xm[c], lo, hi),
                                                                    in1=ps[:, rb * 512 + lo:rb * 512 + hi], op=ALU.mult),
                     reads=[XM[c]] + rk, writes=[tk])
                (c0, n_, q_, h0) = regs[0]
                if region == 'p':
                    S.op('act', lambda c=c, oap=oap: act.activation(out=hT[:, c, h0:h0 + n_], in_=oap, func=AF.Identity,
                                                                    bias=shift_col(1, c, q_), scale=gs_col(1, c, q_)),
                         reads=[tk, ('mT', 1), ('gs', 1)], writes=[('hTp', c)])
                else:
                    S.op('pool', lambda c=c, oap=oap: gp.tensor_scalar(out=hT[:, c, h0:h0 + n_], in0=oap, scalar1=gs_col(1, c, q_),
                                                                       scalar2=shift_col(1, c, q_), op0=ALU.mult, op1=ALU.add),
                         reads=[tk, ('mT', 1), ('gs', 1)], writes=[('hTs', c)])
                yield

        def woe_region(region, stepper):
            pend = []
            for n in range(8):
                slot, n4 = wslots[n // 4], n % 4
                bk = sbk4[wcnt[0] % 6]
                wcnt[0] += 1
                c0, ncol, q_ = (0, 512, 0) if region == 'p' else (512 + 16, 288, 1)
                for (k0, k1) in (((0, 7), (7, 8)) if region == 'p' else ((0, 8),)):
                    fns = [lambda k=k: pe.matmul(bank(bk, 0, ncol), lhsT=ring[:, slot, k, n4 * 128:(n4 + 1) * 128],
                                                 rhs=bs(abo[k], c0, c0 + ncol), start=(k == 0), stop=(k == 7)) for k in range(k0, k1)]
                    S.group('pe', fns, reads=[('ring', slot)] + ABO[k0:k1], writes=[('ps', bk)])
                S.op('dve', lambda: dve.scalar_tensor_tensor(out=fs(xm[n], c0, c0 + ncol), in0=bank(bk, 0, ncol), scalar=gate_col(0, n, q_),
                                                             in1=fs(xm[n], c0, c0 + ncol), op0=ALU.mult, op1=ALU.add),
                     reads=[('ps', bk), ('mTg', 0), XM[n]], writes=[XM[n]])
                if pend:
                    l1_stat(pend.pop(0), region)
                pend.append(n)
                if stepper is not None:
                    stepper()
            l1_stat(pend.pop(0), region)

        woe_region('p', None)
        gp_ = chain_gen('p')

        def step_p():
            try:
                next(gp_)
            except StopIteration:
                pass
        woe_region('s', step_p)
        for _ in range(12):
            step_p()
        ring_limit[0] = None
        BP.release(sq1)
        if debug and 'x1' in debug:
            for c in range(8):
                S.dma('sp', dbg_d['x1'][c], fs(xm[c], 0, TM), reads=[XM[c]])

        if stop == 9:
            S.finish()
            return nc
        pbs[:] = [0, 2, 4, 6]

        def proj1(slot, n4, pb, sn=320, s0=512):
            if sn == 320:
                proj_chunk(slot, n4, pb, sample_cols=(s0 + 16, 288, 16))
            else:
                proj_chunk(slot, n4, pb, sample_cols=(s0, sn))

        def build_diag(i):
            sl = BP.alloc(5, consecutive=True)
            base = sl[0] * SLOT
            outap = poolb[:, base:base + 31 * 128].rearrange("p (j d) -> p j d", j=31)
            in0 = ident_f[:, :].unsqueeze(1).broadcast_to([128, 31, 128])
            in1 = par[:, PC_CD + i * 31:PC_CD + (i + 1) * 31].unsqueeze(2).broadcast_to([128, 31, 128])
            S.op('dve', lambda: dve.tensor_tensor(out=outap, in0=in0, in1=in1, op=ALU.mult),
                 reads=['ident_f', 'par'], writes=[('B', x) for x in sl])
            return sl

        cdo = abo
        slot = ring_get('wio', 1)
        gs_ = chain_gen('s')
        cpb = [next_pb() for _ in range(3)]
        for i in range(3):
            proj_P(slot, i, cpb[i], fine=(i == 0))
            try:
                next(gs_)
                next(gs_)
                next(gs_)
            except StopIteration:
                pass
        for _ in gs_:
            pass
        FP.release(tmpL1)
        cc = FP.alloc(4)
        acc = FP.alloc(4)

        def cc_evac(i, pb):
            S.op('act', lambda: act.copy(out=fs(cc[i], 0, 512), in_=bank(pb, 0, 512)), reads=[('ps', pb)], writes=[('F', cc[i])])
            S.op('dve', lambda: dve.tensor_copy(out=fs(cc[i], 512, TM), in_=bank(pb + 1, 0, 320)), reads=[('ps', pb + 1)], writes=[('F', cc[i])])
        for i in range(3):
            proj_S(slot, i, cpb[i], (512 + 16, 288, 16))
            cc_evac(i, cpb[i])
        pb = next_pb()
        proj1(slot, 3, pb)
        cc_evac(3, pb)
        slot = ring_get('wio', 2)
        for i in range(4):
            pb = next_pb()
            proj1(slot, i, pb)
            c_ = cc[i]
            a_ = acc[i]
            S.op('dve', lambda c_=c_, pb=pb: dve.tensor_tensor(out=fs(c_, 0, TM), in0=pair(pb, 0, TM), in1=fs(c_, 0, TM), op=ALU.mult),
                 reads=[('ps', pb), ('ps', pb + 1), ('F', c_)], writes=[('F', c_)])
            S.op('dve', lambda c_=c_: dve.tensor_tensor(out=fs(c_, 512, 832), in0=fs(c_, 512, 832), in1=vmask[:, :], op=ALU.mult),
                 reads=[('F', c_), 'vmask'], writes=[('F', c_)])
            w0, w1, w2 = (pcol(PC_CC + i * 3 + j) for j in range(3))
            S.op('act', lambda c_=c_, a_=a_, w1=w1: act.activation(out=fs(a_, 0, TM), in_=fs(c_, 0, TM), func=AF.Identity, scale=w1),
                 reads=[('F', c_), 'par'], writes=[('F', a_)])
            v3 = lambda s, a, b: fs(s, 0, 512).rearrange("p (s t) -> p s t", s=2)[:, :, a:b]
            S.op('dve', lambda c_=c_, a_=a_, w0=w0: dve.scalar_tensor_tensor(out=v3(a_, 1, 256), in0=v3(c_, 0, 255), scalar=w0, in1=v3(a_, 1, 256),
                                                                             op0=ALU.mult, op1=ALU.add),
                 reads=[('F', c_), ('F', a_), 'par'], writes=[('F', a_)])
            S.op('dve', lambda c_=c_, a_=a_, w2=w2: dve.scalar_tensor_tensor(out=v3(a_, 0, 255), in0=v3(c_, 1, 256), scalar=w2, in1=v3(a_, 0, 255),
                                                                             op0=ALU.mult, op1=ALU.add),
                 reads=[('F', c_), ('F', a_), 'par'], writes=[('F', a_)])
            S.op('dve', lambda c_=c_, a_=a_, w0=w0: dve.scalar_tensor_tensor(out=fs(a_, 513, 832), in0=fs(c_, 512, 831), scalar=w0, in1=fs(a_, 513, 832),
                                                                             op0=ALU.mult, op1=ALU.add),
                 reads=[('F', c_), ('F', a_), 'par'], writes=[('F', a_)])
            S.op('dve', lambda c_=c_, a_=a_, w2=w2: dve.scalar_tensor_tensor(out=fs(a_, 512, 831), in0=fs(c_, 513, 832), scalar=w2, in1=fs(a_, 512, 831),
                                                                             op0=ALU.mult, op1=ALU.add),
                 reads=[('F', c_), ('F', a_), 'par'], writes=[('F', a_)])
        FP.release(cc)
        slot = ring_get('wio', 0)
        for i in range(4):
            pb = next_pb()
            proj1(slot, i, pb)
            a_ = acc[i]
            S.op('dve', lambda a_=a_, pb=pb: dve.tensor_tensor(out=fs(a_, 0, TM), in0=pair(pb, 0, TM), in1=fs(a_, 0, TM), op=ALU.mult),
                 reads=[('ps', pb), ('ps', pb + 1), ('F', a_)], writes=[('F', a_)])
        sgt = FP.alloc(2)
        slot = ring_get('wio', 3)
        for i in range(4):
            pb = next_pb()
            proj1(slot, i, pb)
            a_, t_ = acc[i], sgt[i % 2]
            S.op('act', lambda t_=t_, pb=pb: act.activation(out=fs(t_, 0, TM), in_=pair(pb, 0, TM), func=AF.Silu),
                 reads=[('ps', pb), ('ps', pb + 1)], writes=[('F', t_)])
            S.op('dve', lambda a_=a_, t_=t_, i=i: dve.tensor_tensor(out=bs(cdo[i], 0, TM), in0=fs(a_, 0, TM), in1=fs(t_, 0, TM), op=ALU.mult),
                 reads=[('F', a_), ('F', t_)], writes=[('B', cdo[i])])
        FP.release(acc)
        ad = FP.alloc(4)
        slot = ring_get('wio', 4)
        for i in range(4):
            pb = next_pb()
            proj1(slot, i, pb)
            S.op('act', lambda i=i, pb=pb: act.copy(out=fs(ad[i], 0, 512), in_=bank(pb, 0, 512)),
                 reads=[('ps', pb)], writes=[('F', ad[i])])
            S.op('dve', lambda i=i, pb=pb: dve.tensor_tensor(out=fs(ad[i], 512, 832), in0=bank(pb + 1, 0, 320), in1=vmask[:, :], op=ALU.mult),
                 reads=[('ps', pb + 1), 'vmask'], writes=[('F', ad[i])])
        glu = BP.alloc(4)
        slot = ring_get('wio', 5)
        for i in range(4):
            g_ = glu[i]
            S.op('pool', lambda g_=g_: gp.memset(bs(g_, 0, SLOT), 0.0), writes=[('B', g_)])
            pb = next_pb()
            proj1(slot, i, pb)
            t_ = sgt[i % 2]
            S.op('act', lambda t_=t_, pb=pb: act.activation(out=fs(t_, 0, TM), in_=pair(pb, 0, TM), func=AF.Sigmoid),
                 reads=[('ps', pb), ('ps', pb + 1)], writes=[('F', t_)])
            S.op('dve', lambda g_=g_, t_=t_, i=i: dve.tensor_tensor(
                out=bs(g_, 15, 15 + 542).rearrange("p (s t) -> p s t", s=2)[:, :, 0:256],
                in0=fs(ad[i], 0, 512).rearrange("p (s t) -> p s t", s=2), in1=fs(t_, 0, 512).rearrange("p (s t) -> p s t", s=2), op=ALU.mult),
                reads=[('F', ad[i]), ('F', t_)], writes=[('B', g_)])
            S.op('dve', lambda g_=g_, t_=t_, i=i: dve.tensor_tensor(out=bs(g_, 557, 877), in0=fs(ad[i], 512, 832), in1=fs(t_, 512, 832), op=ALU.mult),
                 reads=[('F', ad[i]), ('F', t_)], writes=[('B', g_)])
        FP.release(ad)
        if stop == 10:
            S.finish()
            return nc
        T1 = 768
        sgd = BP.alloc(4)
        slot = ring_get('wio', 6)
        for i in range(4):
            pb = next_pb()
            proj1(slot, i, pb, sn=256, s0=512 + OWN0)
            S.op('act', lambda i=i, pb=pb: act.activation(out=bs(sgd[i], 0, T1), in_=pair(pb, 0, T1), func=AF.Silu),
                 reads=[('ps', pb), ('ps', pb + 1)], writes=[('B', sgd[i])])
        act_preload(AF.Ln, 2)
        z = FP.alloc(4)
        zb = BP.alloc(4)
        MEANB, SQB = 4, 6

        def conv_stats(i):
            q0, q1 = zb[(i % 2) * 2], zb[(i % 2) * 2 + 1]
            fns = []
            for (bk, q) in ((MEANB, q0), (SQB, q1)):
                fns.append(lambda bk=bk, q=q: pe.matmul(bank(bk, 0, 512), lhsT=ones_b[:, :], rhs=bs(q, 0, 512), start=(i == 0), stop=(i == 3)))
                fns.append(lambda bk=bk, q=q: pe.matmul(bank(bk + 1, 0, 256), lhsT=ones_b[:, :], rhs=bs(q, 512, 768), start=(i == 0), stop=(i == 3)))
            S.group('pe', fns, reads=[('B', q0), ('B', q1), 'ones_b'], writes=[('ps', MEANB), ('ps', MEANB + 1), ('ps', SQB), ('ps', SQB + 1)])

        dgs = {0: build_diag(0)}
        for i in range(4):
            if i + 1 < 4:
                dgs[i + 1] = build_diag(i + 1)
            dg = dgs[i]
            pb = 0 if i % 2 == 0 else 2
            g_ = glu[i]
            regions = ((0, 0, pb, 0), (271, 0, pb, 256), (557 + OWN0 - 15, 0, pb + 1, 0))
            fns = []
            for (off, _, bk, bo) in regions:
                for j in range(31):
                    o = dg[0] * SLOT + j * 128
                    fns.append(lambda off=off, bk=bk, bo=bo, j=j, o=o, g_=g_: pe.matmul(
                        bank(bk, bo, 256), lhsT=poolb[:, o:o + 128], rhs=bs(g_, off + j, off + j + 256), start=(j == 0), stop=(j == 30)))
            S.group('pe', fns, reads=[('B', g_)] + [('B', s) for s in dg], writes=[('ps', pb), ('ps', pb + 1)])
            BP.release(dg)
            S.op('act', lambda i=i, pb=pb: act.activation(out=fs(z[i], 0, T1), in_=pair(pb, 0, T1), func=AF.Identity, bias=pcol(PC_CDB + i)),
                 reads=[('ps', pb), ('ps', pb + 1), 'par'], writes=[('F', z[i])])
            q0, q1 = zb[(i % 2) * 2], zb[(i % 2) * 2 + 1]
            S.op('dve', lambda i=i, q0=q0, pb=pb: dve.tensor_scalar(out=bs(q0, 0, T1), in0=pair(pb, 0, T1), scalar1=pcol(PC_CDB + i), scalar2=None,
                                                                      op0=ALU.add),
                 reads=[('ps', pb), ('ps', pb + 1), 'par'], writes=[('B', q0)])
            S.op('act', lambda i=i, q1=q1, pb=pb: act.activation(out=bs(q1, 0, T1), in_=pair(pb, 0, T1), func=AF.Square, bias=pcol(PC_CDB + i)),
                 reads=[('ps', pb), ('ps', pb + 1), 'par'], writes=[('B', q1)])
            if i >= 1:
                conv_stats(i - 1)
        conv_stats(3)
        BP.release(zb)
        BP.release(glu)
        CDO = [('B', s_) for s_ in cdo]

        def woo_pass(k0, k1, mode, chunks=range(8), per_k=False):
            wt = FP.alloc(2) if mode != 'dve' else None
            for n in chunks:
                if True:
                    pc, n4 = n // 4, n % 4
                    slot = ring_get('woo', pc)
                    pb = next_pb()
                    fns = []
                    for k in range(k0, k1):
                        fns.append(lambda k=k, n4=n4, pb=pb: pe.matmul(bank(pb, 0, 512), lhsT=ring[:, slot, k, n4 * 128:(n4 + 1) * 128],
                                                                       rhs=bs(cdo[k], 0, 512), start=(k == k0), stop=(k == k1 - 1)))
                    for k in range(k0, k1):
                        c0 = 512 + OWN0 if k < 4 else 512
                        fns.append(lambda k=k, n4=n4, pb=pb, c0=c0: pe.matmul(bank(pb + 1, 0, 256), lhsT=ring[:, slot, k, n4 * 128:(n4 + 1) * 128],
                                                                              rhs=bs(cdo[k], c0, c0 + 256), start=(k == k0), stop=(k == k1 - 1)))
                    if per_k:
                        nk_ = k1 - k0
                        for j_ in range(nk_):
                            S.group('pe', [fns[j_], fns[nk_ + j_]], reads=[('ring', slot), CDO[k0 + j_]], writes=[('ps', pb), ('ps', pb + 1)])
                    else:
                        S.group('pe', fns, reads=[('ring', slot)] + CDO[k0:k1], writes=[('ps', pb), ('ps', pb + 1)])
                    if mode == 'dve':
                        S.op('dve', lambda n=n, pb=pb: dve.scalar_tensor_tensor(out=fs(xm[n], 0, 512), in0=bank(pb, 0, 512), scalar=gate_col(1, n, 0),
                                                                                in1=fs(xm[n], 0, 512), op0=ALU.mult, op1=ALU.add),
                             reads=[('ps', pb), ('mTg', 1), XM[n]], writes=[XM[n]])
                        S.op('dve', lambda n=n, pb=pb: dve.scalar_tensor_tensor(out=fs(xm[n], 544, 800), in0=bank(pb + 1, 0, 256), scalar=gate_col(1, n, 1),
                                                                                in1=fs(xm[n], 544, 800), op0=ALU.mult, op1=ALU.add),
                             reads=[('ps', pb + 1), ('mTg', 1), XM[n]], writes=[XM[n]])
                    else:
                        t_ = wt[n % 2]
                        S.op('act', lambda n=n, pb=pb, t_=t_: act.activation(out=fs(t_, 0, 512), in_=bank(pb, 0, 512), func=AF.Identity,
                                                                             scale=gate_col(1, n, 0)),
                             reads=[('ps', pb), ('mTg', 1)], writes=[('F', t_)])
                        S.op('act', lambda n=n, pb=pb, t_=t_: act.activation(out=fs(t_, 512, 768), in_=bank(pb + 1, 0, 256), func=AF.Identity,
                                                                             scale=gate_col(1, n, 1)),
                             reads=[('ps', pb + 1), ('mTg', 1)], writes=[('F', t_)])
                        S.op('pool', lambda n=n, t_=t_: gp.tensor_tensor(out=fs(xm[n], 0, 512), in0=fs(xm[n], 0, 512), in1=fs(t_, 0, 512), op=ALU.add),
                             reads=[('F', t_), XM[n]], writes=[XM[n]])
                        S.op('pool', lambda n=n, t_=t_: gp.tensor_tensor(out=fs(xm[n], 544, 800), in0=fs(xm[n], 544, 800), in1=fs(t_, 512, 768), op=ALU.add),
                             reads=[('F', t_), XM[n]], writes=[XM[n]])
            if wt is not None:
                FP.release(wt)

        var = FP.alloc()
        MK = [('ps', MEANB), ('ps', MEANB + 1)]
        VK = [('ps', SQB), ('ps', SQB + 1)]
        S.op('act', lambda: act.activation(out=pair(MEANB, 0, T1), in_=pair(MEANB, 0, T1), func=AF.Copy, scale=1.0 / 512),
             reads=MK, writes=MK)
        S.op('act', lambda: act.activation(out=fs(var, 0, T1), in_=pair(MEANB, 0, T1), func=AF.Square),
             reads=MK, writes=[('F', var)])
        S.op('dve', lambda: dve.scalar_tensor_tensor(out=fs(var, 0, T1), in0=pair(SQB, 0, T1), scalar=1.0 / 512, in1=fs(var, 0, T1),
                                                     op0=ALU.mult, op1=ALU.subtract),
             reads=VK + [('F', var)], writes=[('F', var)])
        S.op('act', lambda: act.activation(out=fs(var, 0, T1), in_=fs(var, 0, T1), func=AF.Ln, bias=par[:, NPAR - 1:NPAR], scale=1.0),
             reads=[('F', var), 'par'], writes=[('F', var)])
        S.op('act', lambda: act.activation(out=pair(SQB, 0, T1), in_=fs(var, 0, T1), func=AF.Exp, scale=-0.5), reads=[('F', var)], writes=VK)
        for i in range(4):
            S.op('dve', lambda i=i: dve.tensor_tensor(out=fs(z[i], 0, T1), in0=fs(z[i], 0, T1), in1=pair(MEANB, 0, T1), op=ALU.subtract),
                 reads=[('F', z[i])] + MK, writes=[('F', z[i])])
        for i in range(4):
            S.op('dve', lambda i=i: dve.tensor_tensor(out=fs(z[i], 0, T1), in0=fs(z[i], 0, T1), in1=pair(SQB, 0, T1), op=ALU.mult),
                 reads=[('F', z[i])] + VK, writes=[('F', z[i])])
        pbs[:] = [0, 2]
        woo_pass(0, 4, 'actpool', chunks=(0, 1))
        for i in range(4):
            S.op('act', lambda i=i: act.activation(out=fs(z[i], 0, T1), in_=fs(z[i], 0, T1), func=AF.Silu,
                                                   bias=pcol(PC_LNB + i), scale=pcol(PC_LNG + i)),
                 reads=[('F', z[i]), 'par'], writes=[('F', z[i])])
        FP.release(var)
        woo_pass(0, 4, 'dve', chunks=(2, 3))
        for i in range(4):
            S.op('dve', lambda i=i: dve.tensor_tensor(out=bs(cdo[4 + i], 0, T1), in0=fs(z[i], 0, T1), in1=bs(sgd[i], 0, T1), op=ALU.mult),
                 reads=[('F', z[i]), ('B', sgd[i])], writes=[('B', cdo[4 + i])])
        BP.release(sgd)
        FP.release(sgt)
        FP.release(z)

        woo_pass(0, 4, 'actpool', chunks=(4, 5, 6, 7))
        pbs[:] = [0, 2, 4, 6]
        woo_pass(4, 8, 'dve', chunks=(0,), per_k=True)
        woo_pass(4, 8, 'dve', chunks=range(1, 8))
        BP.release(abo)

        if stop == 11:
            S.finish()
            return nc
        fgb = FP.alloc(2)
        S.dma('sp', fs(fgb[0], 0, 512), fgb_d[:, 0:512], writes=[('F', fgb[0])])
        S.dma('sp', fs(fgb[1], 0, 512), fgb_d[:, 512:1024], writes=[('F', fgb[1])])
        junk = FP.alloc()
        ost = FP.alloc(4)
        for tb in range(6):
            pb = (0, 2, 4)[tb % 3]
            c0 = tb * 128 if tb < 4 else 544 + (tb - 4) * 128
            fns = [lambda c=c, c0=c0, pb=pb: pe.transpose(out=ps[:, pb * 512 + c * 128: pb * 512 + (c + 1) * 128],
                                                          in_=fs(xm[c], c0, c0 + 128), identity=ident_f[:, :]) for c in range(8)]
            S.group('pe', fns, reads=XM + ['ident_f'], writes=[('ps', pb), ('ps', pb + 1)])
            for hf in range(2):
                S.op('act', lambda hf=hf, pb=pb, tb=tb: act.activation(out=fs(junk, 0, 512), in_=bank(pb + hf), func=AF.Square,
                                                                       accum_out=stt[:, tb * 4 + hf:tb * 4 + hf + 1]),
                     reads=[('ps', pb + hf)], writes=[('F', junk), ('stt', tb)])
            S.op('dve', lambda tb=tb: dve.tensor_tensor(out=stt[:, tb * 4 + 2:tb * 4 + 3], in0=stt[:, tb * 4:tb * 4 + 1],
                                                        in1=stt[:, tb * 4 + 1:tb * 4 + 2], op=ALU.add),
                 reads=[('stt', tb)], writes=[('stt', tb)])
            S.op('act', lambda tb=tb: act.activation(out=stt[:, tb * 4 + 3:tb * 4 + 4], in_=stt[:, tb * 4 + 2:tb * 4 + 3], func=AF.Sqrt,
                                                     bias=par[:, NPAR - 1:NPAR], scale=1.0 / 1024),
                 reads=[('stt', tb), 'par'], writes=[('stt', tb)])
            S.op('dve', lambda tb=tb: dve.reciprocal(out=stt[:, tb * 4 + 3:tb * 4 + 4], in_=stt[:, tb * 4 + 3:tb * 4 + 4]),
                 reads=[('stt', tb)], writes=[('stt', tb)])
            dst = yp_d[tb * 128:(tb + 1) * 128, :] if tb < 4 else ys_d[(tb - 4) * 128:(tb - 3) * 128, :]
            for hf in range(2):
                o_ = ost[(tb % 2) * 2 + hf]
                S.op('dve', lambda hf=hf, pb=pb, tb=tb, o_=o_: dve.scalar_tensor_tensor(
                    out=fs(o_, 0, 512), in0=bank(pb + hf), scalar=stt[:, tb * 4 + 3:tb * 4 + 4], in1=fs(fgb[hf], 0, 512),
                    op0=ALU.mult, op1=ALU.mult),
                    reads=[('ps', pb + hf), ('stt', tb), ('F', fgb[hf])], writes=[('F', o_)])
                S.dma('sp', dst[:, hf * 512:(hf + 1) * 512], fs(o_, 0, 512), reads=[('F', o_)])
        S.finish()
    return nc


_CACHE = {}


def _host_consts():
    if 'c' in _CACHE:
        return _CACHE['c']
    ident = np.eye(128, dtype=np.float32)
    halfs = (1, 2, 4, 8)
    inv_p = np.zeros((4, 256), np.float32)
    for g, hf in enumerate(halfs):
        pos = np.arange(256)
        lo = np.clip(pos - hf, 0, 256)
        hi = np.clip(pos + hf, 0, 256)
        inv_p[g] = 1.0 / (hi - lo)
    per_core = []
    for i in range(8):
        j = i % 4
        t0 = 256 * j
        te = t0 - 32 + np.arange(320)
        valid = (te >= 0) & (te < 1024)
        vmask = np.broadcast_to(valid.astype(np.float32)[None, :], (128, 320)).copy()
        invc = np.zeros((4, PUW), np.float32)
        for g, hf in enumerate(halfs):
            invc[g, 8:264] = inv_p[g]
            invc[g, 272:528] = inv_p[g]
            lo = np.clip(te - hf, 0, 1024)
            hi = np.clip(te + hf, 0, 1024)
            cnt = np.maximum(hi - lo, 1)
            invc[g, 544:864] = np.where(valid, 1.0 / cnt, 0.0)
        invc = np.broadcast_to(invc.reshape(1, 4 * PUW), (128, 4 * PUW)).copy()
        kw0 = 4 * j - 6
        qe = np.arange(320)
        hs = qe // 32 + 1
        s = hs // 2
        r = 4 * j - 1 + s
        start = np.clip(r - 4, 0, 8)
        mall = np.zeros((128, 320), np.float32)
        mall[0:14] = NEG
        for kb in range(7):
            for a in range(2):
                kr = kw0 + 2 * kb + a
                ok = (kr >= 0) & (kr < 16) & (r >= 0) & (r < 16) & (kr >= start) & (kr < start + 8)
                mall[kb * 2 + a] = np.where(ok, 0.0, NEG)
        per_core.append((vmask, invc, mall))
    indall = np.zeros((128, 7, 128), np.float32)
    for kb in range(7):
        for a in range(2):
            indall[kb * 2 + a, kb, a * 64:(a + 1) * 64] = 1.0
    indall = indall.reshape(128, 896)
    _CACHE['c'] = (ident, per_core, indall)
    return _CACHE['c']


def _t2r_table(rpb, j):
    out = np.full((128, 8, QTOT), np.float32(NEG), np.float32)
    a = (np.arange(128) // 64)[:, None]
    kcol = (np.arange(128) % 64)[:, None]
    kw0 = 4 * j - 6
    for kb in range(7):
        qa, qb = QR[kb]
        q = np.arange(qa, qb)[None, :]
        hs = q // 32 + 1
        s_ = hs // 2
        r = 4 * j - 1 + s_
        qcol = (hs % 2) * 32 + q % 32
        kr = kw0 + 2 * kb + a
        start = np.clip(r - 4, 0, 8)
        col_start = np.clip(qcol - 8, 0, 48)
        ok = ((kr >= 0) & (kr < 16) & (r >= 0) & (r < 16) & (kr >= start) & (kr < start + 8)
              & (kcol >= col_start) & (kcol < col_start + 16))
        dr = np.clip(kr - r + 7, 0, 14)
        dc = np.clip(kcol - qcol, -15, 15) + 15
        for h in range(8):
            out[:, h, QOFF[kb]:QOFF[kb] + (qb - qa)] = np.where(ok, rpb[h][dr, dc], np.float32(NEG))
    return out


def _col(v):
    v = np.asarray(v, np.float32)
    return np.ascontiguousarray(v.reshape(-1, 128).T)


def _prepare(inputs):
    x_prompt = np.asarray(inputs['x_prompt'], np.float32)
    x_sample = np.asarray(inputs['x_sample'], np.float32)
    ident, per_core, indall = _host_consts()
    t2r_j = [_t2r_table(np.asarray(inputs['rpb'], np.float32)[0], j_) for j_ in range(4)]
    shared = {
        'ident': ident,
        'fgb': np.ascontiguousarray(np.broadcast_to(np.asarray(inputs['final_g'], np.float32)[None, :], (128, 1024))),
        'w_mod': np.ascontiguousarray(inputs['w_mod'], np.float32),
        'w_in_even': np.ascontiguousarray(inputs['w_in_even'][0], np.float32),
        'w_pool': np.ascontiguousarray(inputs['w_pool'][0], np.float32),
        'w_out_even': np.ascontiguousarray(inputs['w_out_even'][0], np.float32),
        'w_in_odd': np.ascontiguousarray(inputs['w_in_odd'][0], np.float32),
        'w_out_odd': np.ascontiguousarray(inputs['w_out_odd'][0], np.float32),
    }
    c = np.asarray(inputs['c'], np.float32)
    c_ctx = np.asarray(inputs['c_ctx'], np.float32)
    norm_g = np.asarray(inputs['norm_g'], np.float32)
    b_mod = np.asarray(inputs['b_mod'], np.float32)
    ckt, cvz = [], []
    for b_ in range(2):
        ck_ = np.asarray(inputs['cache_k'][b_, 0], np.float32)
        cv_ = np.asarray(inputs['cache_v'][b_, 0], np.float32)
        ckt.append(np.ascontiguousarray(ck_.transpose(1, 0, 2).reshape(256, 512)))
        vt = cv_.transpose(1, 0, 2)
        z_ = np.zeros((2, 256, 8, 64), np.float32)
        z_[0, :, 0::2, :] = vt[:, 0::2, :]
        z_[1, :, 1::2, :] = vt[:, 1::2, :]
        cvz.append(z_.reshape(2, 256, 512))
    in_maps = []
    for i in range(8):
        b, j = i // 4, i % 4
        par = np.zeros((128, NPAR), np.float32)
        cc = np.stack([_col(c_ctx), _col(c[b])], axis=2)
        par[:, PC_COND:PC_COND + 16] = cc.reshape(128, 16)
        for l in range(2):
            g2 = np.repeat(_col(norm_g[l])[:, :, None], 2, axis=2)
            par[:, PC_G + l * 16:PC_G + (l + 1) * 16] = g2.reshape(128, 16)
            b2 = np.repeat(_col(b_mod[l])[:, :, None], 2, axis=2)
            par[:, PC_BMOD + l * 48:PC_BMOD + (l + 1) * 48] = b2.reshape(128, 48)
        par[:, PC_PSC:PC_PSC + 4] = _col(inputs['pool_scale'][0])
        par[:, PC_CC:PC_CC + 12] = np.stack([_col(inputs['conv_c'][0][t]) for t in range(3)], axis=2).reshape(128, 12)
        par[:, PC_CD:PC_CD + 124] = np.stack([_col(inputs['conv_d'][0][t]) for t in range(31)], axis=2).reshape(128, 124)
        par[:, PC_CDB:PC_CDB + 4] = _col(inputs['conv_d_b'][0])
        par[:, PC_LNG:PC_LNG + 4] = _col(inputs['ln_g'][0])
        par[:, PC_LNB:PC_LNB + 4] = _col(inputs['ln_b'][0])
        par[:, PC_FG:PC_FG + 8] = _col(inputs['final_g'])
        par[:, NPAR - 1] = EPS
        xp = np.ascontiguousarray(x_prompt[2 * i:2 * i + 2].reshape(512, 1024))
        xw = np.zeros((896, 1024), np.float32)
        kw0 = 4 * j - 6
        lo_r, hi_r = max(kw0, 0), min(kw0 + 14, 16)
        xw[(lo_r - kw0) * 64:(hi_r - kw0) * 64] = x_sample[b, lo_r * 64:hi_r * 64]
        vmask, invc, mall = per_core[i]
        m = dict(shared)
        m.update({'xpT': np.ascontiguousarray(xp.T), 'xwT': np.ascontiguousarray(xw.T),
                  'ck': ckt[b], 'cv': cvz[b],
                  'params': par, 'vmask': vmask, 'invcnt': invc, 't2r': t2r_j[j]})
        in_maps.append(m)
    return in_maps


def kernel(**inputs):
    in_maps = _prepare(inputs)
    if 'nc' not in _CACHE:
        _CACHE['nc'] = build_program()
    nc = _CACHE['nc']
    res = run_bass_kernel_spmd(nc, in_maps, core_ids=list(range(8)))
    R = res.results
    y_prompt = np.concatenate([R[i]['yp'].reshape(2, 256, 1024) for i in range(8)], axis=0)
    y_sample = np.stack([np.concatenate([R[b * 4 + j]['ys'] for j in range(4)], axis=0) for b in range(2)], axis=0)
    nk = np.concatenate([R[i]['nk'].reshape(8, 64, 2, 256).transpose(2, 0, 3, 1).reshape(2, 1, 8, 256, 64) for i in range(8)], axis=0)
    nv = np.concatenate([R[i]['nv'].reshape(2, 256, 8, 64).transpose(0, 2, 1, 3).reshape(2, 1, 8, 256, 64) for i in range(8)], axis=0)
    return (y_prompt.astype(np.float32), y_sample.astype(np.float32), nk.astype(np.float32), nv.astype(np.float32))
```

```python
import contextlib
import numpy as np
import concourse.bass as bass
import concourse.mybir as mybir
from concourse.bass_utils import run_bass_kernel_spmd

F32 = mybir.dt.float32
BF16 = mybir.dt.bfloat16
AF = mybir.ActivationFunctionType
ALU = mybir.AluOpType

NEG = -30000.0
EPS = 1e-6
TP, TS, TM, TW = 512, 320, 832, 896
EXT0 = 352
OWN0 = 32
SLOT = 896
NF, NB = 18, 30
PC_COND, PC_G, PC_BMOD, PC_PSC, PC_CC, PC_CD, PC_CDB, PC_LNG, PC_LNB, PC_FG = 0, 16, 48, 144, 148, 160, 284, 288, 292, 296
NPAR = 305
QR = {0: (16, 32), 1: (16, 288), 2: (16, 288), 3: (16, 304), 4: (16, 304), 5: (32, 304), 6: (32, 304)}
QOFF = {}
_o = 0
for _kb in range(7):
    QOFF[_kb] = _o
    _o += QR[_kb][1] - QR[_kb][0]
QTOT = _o
PU_SEQ = (8, 272, 544)
PUW = 872
PG_SEQ = (15, 286, 557)
PGW = 892


class Sched:
    def __init__(self, nc, es):
        self.nc = nc
        self.E = {'pe': nc.tensor, 'act': nc.scalar, 'dve': nc.vector, 'pool': nc.gpsimd, 'sp': nc.sync}
        self.sems = {}
        for e in ('pe', 'act', 'dve', 'pool'):
            self.sems[e] = es.enter_context(nc.semaphore('s_' + e))
        self.cnt = {e: 0 for e in ('pe', 'act', 'dve', 'pool')}
        self.dma_pool = {}
        for q, n in (('sp', 40), ('pool', 40)):
            lst = []
            for i in range(n):
                nm = 'd_%s%d' % (q, i)
                self.sems[nm] = es.enter_context(nc.semaphore(nm))
                lst.append([nm, 0])
            self.dma_pool[q] = lst
        self.dma_rr = {q: 0 for q in self.dma_pool}
        self.waited = {}
        self.lastw = {}
        self.readers = {}

    @staticmethod
    def _is_ps(k):
        return isinstance(k, tuple) and k[0] == 'ps'

    def _deps(self, reads, writes, eng=None):
        d = {}

        def add(tok):
            if tok is None:
                return
            n, v = tok
            if d.get(n, 0) < v:
                d[n] = v
        for k in reads:
            add(self.lastw.get(k))
            if self._is_ps(k):
                for n, v in self.readers.get(k, {}).items():
                    if n != eng:
                        add((n, v))
        for k in writes:
            add(self.lastw.get(k))
            for n, v in self.readers.get(k, {}).items():
                add((n, v))
        return d

    def _wait(self, eng, d):
        for n, v in d.items():
            if self.waited.get((eng, n), 0) < v:
                self.E[eng].wait_ge(self.sems[n], v)
                self.waited[(eng, n)] = v

    def _commit(self, tok, reads, writes):
        for k in writes:
            self.lastw[k] = tok
            self.readers[k] = {}
        for k in reads:
            r = self.readers.setdefault(k, {})
            if r.get(tok[0], 0) < tok[1]:
                r[tok[0]] = tok[1]

    def op(self, eng, fn, reads=(), writes=()):
        self._wait(eng, self._deps(reads, writes, eng))
        ins = fn()
        self.cnt[eng] += 1
        tok = (eng, self.cnt[eng])
        ins.then_inc(self.sems[eng], 1)
        self._commit(tok, reads, writes)

    def group(self, eng, fns, reads=(), writes=()):
        self._wait(eng, self._deps(reads, writes, eng))
        ins = None
        for fn in fns:
            ins = fn()
        self.cnt[eng] += 1
        tok = (eng, self.cnt[eng])
        ins.then_inc(self.sems[eng], 1)
        self._commit(tok, reads, writes)

    def dma(self, q, out, in_, reads=(), writes=(), after=(), **kw):
        d = self._deps(reads, writes)
        for k in after:
            tok = self.lastw.get(k)
            if tok is not None and d.get(tok[0], 0) < tok[1]:
                d[tok[0]] = tok[1]
        self._wait(q, d)
        pool = self.dma_pool[q]
        i = self.dma_rr[q]
        self.dma_rr[q] = (i + 1) % len(pool)
        ent = pool[i]
        if ent[1] > 0 and self.waited.get((q, ent[0]), 0) < ent[1]:
            self.E[q].wait_ge(self.sems[ent[0]], ent[1])
            self.waited[(q, ent[0])] = ent[1]
        ins = self.E[q].dma_start(out=out, in_=in_, **kw)
        ent[1] += 16
        ins.then_inc(self.sems[ent[0]], 16)
        self._commit((ent[0], ent[1]), reads, writes)

    def finish(self):
        for q, pool in self.dma_pool.items():
            for nm, v in pool:
                if v > 0 and self.waited.get(('sp', nm), 0) < v:
                    self.E['sp'].wait_ge(self.sems[nm], v)
                    self.waited[('sp', nm)] = v


class SlotPool:
    def __init__(self, name, n):
        self.name = name
        self.free = list(range(n))
        self.peak = 0
        self.n = n

    def alloc(self, k=1, consecutive=False):
        if consecutive:
            fs = sorted(self.free)
            for i in range(len(fs) - k + 1):
                if fs[i + k - 1] - fs[i] == k - 1:
                    got = fs[i:i + k]
                    for g in got:
                        self.free.remove(g)
                    self.peak = max(self.peak, self.n - len(self.free))
                    return got
            raise RuntimeError('no consecutive slots in ' + self.name)
        assert len(self.free) >= k, 'out of slots in %s' % self.name
        got = [self.free.pop(0) for _ in range(k)]
        self.peak = max(self.peak, self.n - len(self.free))
        return got if k > 1 else got[0]

    def release(self, s):
        if isinstance(s, (list, tuple)):
            for x in s:
                self.release(x)
        else:
            assert s not in self.free
            self.free.append(s)
            self.free.sort()


def build_program(debug=None, stop=None):
    nc = bass.Bass("TRN2", target_bir_lowering=False)
    dt_in = lambda n, s: nc.dram_tensor(n, list(s), F32, kind="ExternalInput").ap()
    dt_out = lambda n, s: nc.dram_tensor(n, list(s), F32, kind="ExternalOutput").ap()
    xpT_d = dt_in("xpT", (1024, 512))
    xwT_d = dt_in("xwT", (1024, 896))
    ck_d = dt_in("ck", (256, 512))
    cv_d = dt_in("cv", (2, 256, 512))
    par_d = dt_in("params", (128, NPAR))
    ident_d = dt_in("ident", (128, 128))
    vmask_d = dt_in("vmask", (128, 320))
    invc_d = dt_in("invcnt", (128, 4 * PUW))
    t2r_d = dt_in("t2r", (128, 8, QTOT))
    fgb_d = dt_in("fgb", (128, 1024))
    wmod_d = dt_in("w_mod", (2, 1024, 3072))
    wine_d = dt_in("w_in_even", (1024, 3072))
    wpool_d = dt_in("w_pool", (4, 128, 128))
    woe_d = dt_in("w_out_even", (1024, 1024))
    wino_d = dt_in("w_in_odd", (1024, 3584))
    woo_d = dt_in("w_out_odd", (1024, 1024))
    yp_d = dt_out("yp", (512, 1024))
    ys_d = dt_out("ys", (256, 1024))
    nk_d = dt_out("nk", (512, 512))
    nv_d = dt_out("nv", (2, 256, 512))
    dbg_d = {}
    if debug:
        for nm, shp in debug.items():
            dbg_d[nm] = dt_out("dbg_" + nm, shp)

    with contextlib.ExitStack() as es:
        sb = lambda n, s, d: es.enter_context(nc.sbuf_tensor("sb_" + n, list(s), d))
        S = Sched(nc, es)
        poolf = sb("poolf", (128, NF, SLOT), F32)
        poolb = sb("poolb", (128, NB * SLOT), BF16)
        hT = sb("hT", (128, 8, 1408), BF16)
        ring = sb("ring", (128, 3, 8, 512), BF16)
        kT_p = sb("kT_p", (128, 4, 512), BF16)
        v_p = sb("v_p", (128, 2, 4, 512), BF16)
        v_w = sb("v_w", (128, 2, 7, 512), BF16)
        kcT = sb("kcT", (128, 4, 256), BF16)
        cvb = sb("cvb", (128, 2, 2, 512), BF16)
        oh = sb("oh", (128, 2, 128), BF16)
        t2rb = sb("t2rb", (128, 2, QTOT), BF16)
        wpool_b = sb("wpool_b", (128, 4, 128), BF16)
        par = sb("par", (128, NPAR), F32)
        mT = sb("mT", (128, 2, 48), F32)
        gs = sb("gs", (128, 2, 16), F32)
        scond = sb("scond", (128, 16), BF16)
        stt = sb("stt", (128, 24), F32)
        dummy = sb("dummy", (128, 4), F32)
        sst = sb("sst", (128, 44), F32)
        ones_f = sb("ones_f", (128, 128), F32)
        ident_f = sb("ident_f", (128, 128), F32)
        ident_b = sb("ident_b", (128, 128), BF16)
        ones_b = sb("ones_b", (128, 128), BF16)
        vmask = sb("vmask", (128, 320), F32)
        ps = es.enter_context(nc.psum_tensor("ps", [128, 4096], F32))
        FP = SlotPool('F', NF)
        BP = SlotPool('B', NB)

        pe, act, dve, gp, sp = nc.tensor, nc.scalar, nc.vector, nc.gpsimd, nc.sync

        def fs(s, a=0, b=SLOT):
            return poolf[:, s, a:b]

        def bs(s, a=0, b=SLOT):
            return poolb[:, s * SLOT + a:s * SLOT + b]

        def bsp(pr, s, a, b):
            return poolb[pr, s * SLOT + a:s * SLOT + b]

        def bank(b, a=0, n=512):
            return ps[:, b * 512 + a: b * 512 + a + n]

        def pair(b, a=0, n=1024):
            return ps[:, b * 512 + a: b * 512 + a + n]

        def pcol(off, n=1):
            return par[:, off:off + n]

        S.dma('sp', par[:, :], par_d[:, :], writes=['par'])
        S.dma('sp', ident_f[:, :], ident_d[:, :], writes=['ident_f'])
        S.dma('sp', vmask[:, :], vmask_d[:, :], writes=['vmask'])
        S.op('dve', lambda: dve.memset(ones_b[:, :], 1.0), writes=['ones_b'])
        S.op('dve', lambda: dve.memset(ps[:, 7 * 512 + 96:8 * 512], 0.0), writes=[('ps', 7)])
        S.op('dve', lambda: dve.memset(ones_f[:, :], 1.0), writes=['ones_f'])
        S.op('dve', lambda: dve.tensor_copy(out=ident_b[:, :], in_=ident_f[:, :]), reads=['ident_f'], writes=['ident_b'])
        S.op('act', lambda: act.activation(out=scond[:, :], in_=par[:, PC_COND:PC_COND + 16], func=AF.Silu),
             reads=['par'], writes=['scond'])

        def act_preload(func, col):
            S.op('act', lambda: act.activation(out=dummy[:, col:col + 1], in_=par[:, NPAR - 1:NPAR], func=func),
                 reads=['par'], writes=[('dummy', col)])

        act_preload(AF.Ln, 0)
        pieces = []

        def wpiece(w2d, c0):
            return w2d[:, c0:c0 + 512].rearrange("(c p) n -> p c n", p=128)
        for pc in range(4):
            pieces.append(('wm0', pc, wpiece(wmod_d[0], pc * 512)))
        for pc in range(6):
            pieces.append(('wie', pc, wpiece(wine_d, pc * 512)))
            if pc in (3, 4):
                pieces.append(('wm0', pc + 1, wpiece(wmod_d[0], (pc + 1) * 512)))
        for pc in range(4):
            pieces.append(('wm1', pc, wpiece(wmod_d[1], pc * 512)))
        for pc in range(2):
            pieces.append(('wm1', 4 + pc, wpiece(wmod_d[1], (4 + pc) * 512)))
        for pc in range(2):
            pieces.append(('woe', pc, wpiece(woe_d, pc * 512)))
        for pc in (1, 2, 0, 3, 4, 5, 6):
            pieces.append(('wio', pc, wpiece(wino_d, pc * 512)))
        for pc in range(2):
            pieces.append(('woo', pc, wpiece(woo_d, pc * 512)))
        piece_idx = {(a, b): i for i, (a, b, _) in enumerate(pieces)}
        issued = [0]
        xtra = BP.alloc(5, consecutive=True)
        xtra_ap = poolb[:, xtra[0] * SLOT:xtra[0] * SLOT + 4096].rearrange("p (k n) -> p k n", k=8)
        XK = [('B', x_) for x_ in xtra]

        def slot_of(i):
            return i if i < 3 else ('X' if i == 3 else (i - 1) % 3)

        ring_limit = [None]

        def ring_issue_upto(i, after=()):
            if ring_limit[0] is not None:
                i = min(i, ring_limit[0])
            while issued[0] <= i and issued[0] < len(pieces):
                p = issued[0]
                sl = slot_of(p)
                if sl == 'X':
                    S.dma('pool', xtra_ap, pieces[p][2], writes=XK, after=list(after))
                else:
                    S.dma('pool', ring[:, sl, :, :], pieces[p][2], writes=[('ring', sl)], after=list(after))
                issued[0] += 1

        def ring_get(kind, pc):
            i = piece_idx[(kind, pc)]
            ring_issue_upto(i + 2)
            return slot_of(i)

        ring_issue_upto(0)
        S.op('pool', lambda: gp.memset(v_p[:, :, :, :], 0.0), writes=['v_p'])
        S.op('pool', lambda: gp.memset(v_w[:, :, :, :], 0.0), writes=[('v_w', wb) for wb in range(7)])
        S.op('pool', lambda: gp.memset(oh[:, :, :], 0.0), writes=['oh'])
        S.op('pool', lambda: gp.memset(oh[:, 0, 0:64], 1.0), writes=['oh'])
        S.op('pool', lambda: gp.memset(oh[:, 1, 64:128], 1.0), writes=['oh'])

        MODB = 7

        def mod_piece(l, pc):
            slot = ring_get('wm%d' % l, pc)
            fns = []
            for n4 in range(4):
                n = pc * 4 + n4
                for k in range(8):
                    wsrc = xtra_ap if slot == 'X' else ring[:, slot, :, :]
                    fns.append(lambda n=n, n4=n4, k=k, wsrc=wsrc: pe.matmul(
                        bank(MODB, l * 48 + n * 2, 2), lhsT=wsrc[:, k, n4 * 128:(n4 + 1) * 128],
                        rhs=scond[:, k * 2:k * 2 + 2], start=(k == 0), stop=(k == 7)))
            S.group('pe', fns, reads=(XK if slot == 'X' else [('ring', slot)]) + ['scond'], writes=[('ps', MODB)])
            if slot == 'X':
                BP.release(xtra)

        def mod_finish(l):
            S.op('dve', lambda: dve.tensor_tensor(out=mT[:, l, 0:32], in0=bank(MODB, l * 48, 32),
                                                  in1=par[:, PC_BMOD + l * 48:PC_BMOD + l * 48 + 32], op=ALU.add),
                 reads=[('ps', MODB), 'par'], writes=[('mT', l)])
            S.op('dve', lambda: dve.scalar_tensor_tensor(out=gs[:, l, :], in0=mT[:, l, 16:32], scalar=1.0,
                                                         in1=par[:, PC_G + l * 16:PC_G + (l + 1) * 16],
                                                         op0=ALU.add, op1=ALU.mult),
                 reads=[('mT', l), 'par'], writes=[('gs', l)])

        def mod_finish_gate(l):
            S.op('dve', lambda: dve.tensor_tensor(out=mT[:, l, 32:48], in0=bank(MODB, l * 48 + 32, 16),
                                                  in1=par[:, PC_BMOD + l * 48 + 32:PC_BMOD + (l + 1) * 48], op=ALU.add),
                 reads=[('ps', MODB), 'par'], writes=[('mTg', l)])

        def shift_col(l, c, q):
            return mT[:, l, c * 2 + q:c * 2 + q + 1]

        def gs_col(l, c, q):
            return gs[:, l, c * 2 + q:c * 2 + q + 1]

        def gate_col(l, c, q):
            return mT[:, l, 32 + c * 2 + q:32 + c * 2 + q + 1]

        if stop == 0:
            S.finish()
            return nc
        xm = FP.alloc(8, consecutive=True)
        xw = FP.alloc(8, consecutive=True)
        XM = [('F', s) for s in xm]
        XW = [('F', s) for s in xw]
        poolbf = poolb.bitcast(F32)
        for c in range(8):
            S.dma('sp', fs(xm[c], 0, 512), xpT_d[c * 128:(c + 1) * 128, :], writes=[XM[c]])
        for c in range(8):
            S.dma('sp', fs(xw[c], 0, TW), xwT_d[c * 128:(c + 1) * 128, :], writes=[XW[c]])
            if c == 3:
                ring_issue_upto(3, after=[XW[3]])

        def rms_stats(chunks, keys, col_ranges, psbanks):
            sq = BP.alloc(2)
            for c in range(8):
                q = sq[c % 2]
                ncols = max(r[0] + r[1] for r in col_ranges)
                S.op('act', lambda c=c, q=q, ncols=ncols: act.activation(out=bs(q, 0, ncols), in_=fs(chunks[c], 0, ncols), func=AF.Square),
                     reads=[keys[c]], writes=[('B', q)])
                fns = [lambda c=c, q=q, r=r: pe.matmul(bank(r[2], r[3], r[1]), lhsT=ones_b[:, :], rhs=bs(q, r[0], r[0] + r[1]),
                                                      start=(c == 0), stop=(c == 7)) for r in col_ranges]
                S.group('pe', fns, reads=[('B', q), 'ones_b'], writes=[('ps', b) for b in psbanks])
            BP.release(sq)

        def rstd_from(bank0, ncols, scale, to_psum=False):
            r = FP.alloc()
            nb = (ncols + 511) // 512
            keys = [('ps', bank0 + i) for i in range(nb)]
            S.op('act', lambda: act.activation(out=fs(r, 0, ncols), in_=ps[:, bank0 * 512:bank0 * 512 + ncols], func=AF.Ln,
                                               bias=par[:, NPAR - 1:NPAR], scale=scale),
                 reads=keys + ['par'], writes=[('F', r)])
            if to_psum:
                S.op('act', lambda: act.activation(out=ps[:, bank0 * 512:bank0 * 512 + ncols], in_=fs(r, 0, ncols), func=AF.Exp, scale=-0.5),
                     reads=[('F', r)], writes=keys)
                FP.release(r)
                return None
            S.op('act', lambda: act.activation(out=fs(r, 0, ncols), in_=fs(r, 0, ncols), func=AF.Exp, scale=-0.5),
                 reads=[('F', r)], writes=[('F', r)])
            return r

        def make_h_mult(chunks, keys, rbank, lo, hi, tmps):
            rk = [('ps', rbank + i) for i in range(lo // 512, (hi + 511) // 512)]
            for c in range(8):
                if tmps is None:
                    oap, tk = fs(chunks[c], lo, hi), keys[c]
                else:
                    oap, tk = tmps[c][0](lo, hi), tmps[c][1]
                S.op('dve', lambda c=c, oap=oap: dve.tensor_tensor(out=oap, in0=fs(chunks[c], lo, hi),
                                                                    in1=ps[:, rbank * 512 + lo:rbank * 512 + hi], op=ALU.mult),
                     reads=[keys[c]] + rk, writes=[tk] if not isinstance(tk, list) else tk)

        def make_h_affine(l, chunks, keys, regions, tmps, eng_of):
            for c in range(8):
                for ri, (c0, n, q, h0) in enumerate(regions):
                    if tmps is None:
                        iap, tk = fs(chunks[c], c0, c0 + n), keys[c]
                    else:
                        iap, tk = tmps[c][0](c0, c0 + n), tmps[c][1]
                    rd = (tk if isinstance(tk, list) else [tk]) + [('mT', l), ('gs', l)]
                    e_ = eng_of(c, ri)
                    if e_ == 'act':
                        S.op('act', lambda c=c, iap=iap, n=n, q=q, h0=h0: act.activation(
                            out=hT[:, c, h0:h0 + n], in_=iap, func=AF.Identity, bias=shift_col(l, c, q), scale=gs_col(l, c, q)),
                            reads=rd, writes=[('hTp' if h0 < 512 else 'hTs', c)])
                    else:
                        eo = dve if e_ == 'dve' else gp
                        S.op(e_, lambda c=c, iap=iap, n=n, q=q, h0=h0, eo=eo: eo.tensor_scalar(
                            out=hT[:, c, h0:h0 + n], in0=iap, scalar1=gs_col(l, c, q), scalar2=shift_col(l, c, q),
                            op0=ALU.mult, op1=ALU.add),
                            reads=rd, writes=[('hTp' if h0 < 512 else 'hTs', c)])

        def make_h(l, chunks, keys, rbank, regions, tmp, pool_regions=()):
            lo = min(r[0] for r in regions)
            hi = max(r[0] + r[1] for r in regions)
            tmps = None if tmp is None else [((lambda a, b, t=t: fs(t, a, b)), ('F', t)) for t in tmp]
            make_h_mult(chunks, keys, rbank, lo, hi, tmps)
            make_h_affine(l, chunks, keys, regions, tmps, lambda c, ri: 'pool' if ri in pool_regions else 'act')

        if stop == 1:
            S.finish()
            return nc
        rms_stats(xm, XM, [(0, 512, 2, 0)], [2])
        rms_stats(xw, XW, [(0, 512, 3, 0), (512, 384, 4, 0)], [3, 4])
        S.op('dve', lambda: dve.tensor_copy(out=poolf[:, xm[0]:xm[0] + 8, 512:832], in_=poolf[:, xw[0]:xw[0] + 8, EXT0:EXT0 + 320]),
             reads=XW, writes=XM)
        mod_piece(0, 0)
        mod_piece(0, 1)
        rstd_from(2, 512, 1.0 / 1024, to_psum=True)
        rstd_from(3, 896, 1.0 / 1024, to_psum=True)
        mod_piece(0, 2)
        rp = FP.alloc()
        S.op('act', lambda: act.copy(out=fs(rp, 0, 512), in_=bank(2, 0, 512)), reads=[('ps', 2)], writes=[('F', rp)])
        make_h_mult(xw, XW, 3, 0, 896, None)
        tB = BP.alloc(16, consecutive=True)
        tmpsP = [((lambda a, b_, k=k: poolbf[:, (tB[0] + 2 * k) * 448 + a:(tB[0] + 2 * k) * 448 + b_]),
                  [('B', tB[2 * k]), ('B', tB[2 * k + 1])]) for k in range(8)]
        for c in range(8):
            S.op('pool', lambda c=c: gp.tensor_tensor(out=tmpsP[c][0](0, 512), in0=fs(xm[c], 0, 512), in1=fs(rp, 0, 512), op=ALU.mult),
                 reads=[XM[c], ('F', rp)], writes=tmpsP[c][1])
        mod_piece(0, 3)
        mod_finish(0)
        make_h_affine(0, xw, XW, [(0, 896, 1, 512)], None, lambda c, ri: 'act' if c < 4 else 'dve')
        make_h_affine(0, xm, XM, [(0, 512, 0, 0)], tmpsP, lambda c, ri: 'pool' if c < 4 else ('act' if c < 6 else 'dve'))
        FP.release(xw)
        FP.release(rp)
        BP.release(tB)
        HTP = [('hTp', c) for c in range(8)]
        HTS = [('hTs', c) for c in range(8)]
        HT = HTP + HTS

        if stop == 2:
            S.finish()
            return nc
        def proj_P(slot, n4, pb, fine=False):
            fns = [lambda k=k: pe.matmul(bank(pb, 0, 512), lhsT=ring[:, slot, k, n4 * 128:(n4 + 1) * 128],
                                         rhs=hT[:, k, 0:512], start=(k == 0), stop=(k == 7)) for k in range(8)]
            if fine:
                for k in range(8):
                    S.group('pe', [fns[k]], reads=[('ring', slot), HTP[k]], writes=[('ps', pb)])
            else:
                S.group('pe', fns, reads=[('ring', slot)] + HTP, writes=[('ps', pb)])

        def proj_S(slot, n4, pb, sample_cols=(864, 320), fine=False):
            s0, sn = sample_cols[0], sample_cols[1]
            so = sample_cols[2] if len(sample_cols) > 2 else 0
            fns = [lambda k=k: pe.matmul(bank(pb + 1, so, sn), lhsT=ring[:, slot, k, n4 * 128:(n4 + 1) * 128],
                                         rhs=hT[:, k, s0:s0 + sn], start=(k == 0), stop=(k == 7)) for k in range(8)]
            if fine:
                for k in range(8):
                    S.group('pe', [fns[k]], reads=[('ring', slot), HTS[k]], writes=[('ps', pb + 1)])
            else:
                S.group('pe', fns, reads=[('ring', slot)] + HTS, writes=[('ps', pb + 1)])

        def proj_chunk(slot, n4, pb, sample_cols=(864, 320), fine=False):
            proj_P(slot, n4, pb, fine=fine)
            proj_S(slot, n4, pb, sample_cols, fine=fine)

        pbs = [0, 2, 4]
        pbi = [0]

        bg_hook = [None]

        def next_pb():
            b = pbs[pbi[0] % len(pbs)]
            pbi[0] += 1
            if bg_hook[0] is not None:
                bg_hook[0]()
            return b

        pbs[:] = [0, 2, 4, 6]
        upad = FP.alloc(4)
        slot = ring_get('wie', 0)
        for g in range(4):
            u = upad[g]
            S.op('pool', lambda u=u: gp.memset(fs(u, 0, PUW), 0.0), writes=[('F', u)])
            pb = next_pb()
            proj_chunk(slot, g, pb, fine=(g == 0))
            S.op('act', lambda u=u, pb=pb: act.copy(out=fs(u, 8, 8 + 528).rearrange("p (s t) -> p s t", s=2)[:, :, 0:256],
                                                    in_=bank(pb).rearrange("p (s t) -> p s t", s=2)),
                 reads=[('ps', pb)], writes=[('F', u)])
            S.op('dve', lambda u=u, pb=pb: dve.tensor_tensor(out=fs(u, 544, 864), in0=bank(pb + 1, 0, 320), in1=vmask[:, :], op=ALU.mult),
                 reads=[('ps', pb + 1), 'vmask'], writes=[('F', u)])
        abo = BP.alloc(8)

        PPS = {}

        def pool_pre_gen(g):
            u = upad[g]
            ic = FP.alloc()
            S.dma('sp', fs(ic, 0, PUW), invc_d[:, g * PUW:(g + 1) * PUW], writes=[('F', ic)])
            wa, wb_ = FP.alloc(), FP.alloc()
            S.op('dve', lambda u=u, wa=wa: dve.tensor_tensor(out=fs(wa, 1, PUW), in0=fs(u, 0, PUW - 1), in1=fs(u, 1, PUW), op=ALU.add),
                 reads=[('F', u)], writes=[('F', wa)])
            yield
            cur, nxt = wa, wb_
            lo = 1
            for step in range(g):
                sh = 1 << step
                a0, a1 = lo + sh, PUW - sh
                S.op('dve', lambda cur=cur, nxt=nxt, a0=a0, a1=a1, sh=sh: dve.tensor_tensor(
                    out=fs(nxt, a0, a1), in0=fs(cur, a0 - sh, a1 - sh), in1=fs(cur, a0 + sh, a1 + sh), op=ALU.add),
                    reads=[('F', cur)], writes=[('F', nxt)])
                yield
                cur, nxt = nxt, cur
                lo = a0
            S.op('dve', lambda cur=cur, ic=ic: dve.tensor_tensor(out=fs(cur, 8, 864), in0=fs(cur, 8, 864), in1=fs(ic, 8, 864), op=ALU.mult),
                 reads=[('F', cur), ('F', ic)], writes=[('F', cur)])
            yield
            pp = BP.alloc()
            S.op('dve', lambda cur=cur, u=u, pp=pp: dve.tensor_tensor(out=bs(pp, 8, 864), in0=fs(cur, 8, 864), in1=fs(u, 8, 864), op=ALU.subtract),
                 reads=[('F', cur), ('F', u)], writes=[('B', pp)])
            yield
            FP.release([ic, wa, wb_])
            PPS[g] = pp
            PRE_DONE.add(g)

        PRE_DONE = set()
        bg = []

        def bg_step(n=1):
            for _ in range(n):
                while bg:
                    try:
                        next(bg[0])
                        break
                    except StopIteration:
                        bg.pop(0)

        def pool_pre(g):
            bg.append(pool_pre_gen(g))
            bg_hook[0] = bg_step

        def pool_need(g):
            while g not in PRE_DONE:
                bg_step()

        def pool_post(g):
            pool_need(g)
            pp = PPS[g]
            pb = next_pb()
            fns = [lambda pp=pp, pb=pb, g=g, r=r: pe.matmul(ps[:, pb * 512 + r[1]:pb * 512 + r[1] + r[2]], lhsT=wpool_b[:, g, :],
                                                           rhs=bs(pp, r[0], r[0] + r[2]), start=True, stop=True)
                   for r in ((8, 0, 256), (272, 256, 256), (544, 512, 320))]
            S.group('pe', fns, reads=[('B', pp), 'wpool_b'], writes=[('ps', pb), ('ps', pb + 1)])
            S.op('dve', lambda g=g, pb=pb: dve.scalar_tensor_tensor(out=bs(abo[g], 0, TM), in0=pair(pb, 0, TM), scalar=pcol(PC_PSC + g),
                                                                    in1=bs(siluA[g], 0, TM), op0=ALU.mult, op1=ALU.mult),
                 reads=[('ps', pb), ('ps', pb + 1), 'par', ('B', siluA[g])], writes=[('B', abo[g])])
            BP.release(pp)

        if stop == 3:
            S.finish()
            return nc
        S.dma('pool', wpool_b[:, :, :], wpool_d.rearrange("g c d -> c g d"), writes=['wpool_b'])
        siluA = BP.alloc(4)
        slot = ring_get('wie', 1)
        for g in range(4):
            pb = next_pb()
            proj_chunk(slot, g, pb, sample_cols=(864 + 16, 288, 16))
            S.op('act', lambda g=g, pb=pb: act.activation(out=bs(siluA[g], 0, TM), in_=pair(pb, 0, TM), func=AF.Silu),
                 reads=[('ps', pb), ('ps', pb + 1)], writes=[('B', siluA[g])])
        pool_pre(0)
        pool_pre(1)
        qT = BP.alloc(8)
        slot = ring_get('wie', 2)
        for g in range(4):
            pb = next_pb()
            proj_chunk(slot, g, pb, sample_cols=(864 + 16, 288, 16))
            for hh in range(2):
                qs = qT[2 * g + hh]
                pr = slice(hh * 64, hh * 64 + 64)
                S.op('pool', lambda qs=qs: gp.memset(bs(qs, 0, TM), 0.0), writes=[('B', qs)])
                S.op('dve', lambda qs=qs, pr=pr, pb=pb: dve.tensor_scalar(out=bsp(pr, qs, 0, TM), in0=ps[pr, pb * 512:pb * 512 + TM], scalar1=0.125,
                                                                        scalar2=None, op0=ALU.mult),
                     reads=[('ps', pb), ('ps', pb + 1)], writes=[('B', qs)])
        pool_pre(2)
        pool_pre(3)


        if stop == 4:
            S.finish()
            return nc
        def t2r_load(h):
            S.dma('pool', t2rb[:, h % 2, :], t2r_d[:, h, :], writes=[('t2rb', h % 2)])
        ck_tm = []

        def attn_table_loads():
            t2r_load(0)
            for lb in range(2):
                for hh in range(2):
                    S.dma('pool', cvb[:, hh, lb, :], cv_d[hh, lb * 128:(lb + 1) * 128, :], writes=[('cvb', lb)])
            ck_tm.extend(BP.alloc(2))
            for lb in range(2):
                S.dma('pool', bs(ck_tm[lb], 0, 512), ck_d[lb * 128:(lb + 1) * 128, :], writes=[('B', ck_tm[lb])])

        pbs[:] = [0, 2, 4]
        kT_w = BP.alloc(4)
        kvst = FP.alloc(3)
        ktp_pend = []
        for which, pcn in (('k', 3), ('v', 4)):
            slot = ring_get('wie', pcn)
            if which == 'k':
                attn_table_loads()
            out_d = nk_d if which == 'k' else nv_d
            if which == 'k':
                for g in range(4):
                    pb = next_pb()
                    fns = [lambda k=k, g=g, pb=pb: pe.matmul(bank(pb, 0, 512), lhsT=ring[:, slot, k, g * 128:(g + 1) * 128],
                                                             rhs=hT[:, k, 0:512], start=(k == 0), stop=(k == 7)) for k in range(8)]
                    S.group('pe', fns, reads=[('ring', slot)] + HTP, writes=[('ps', pb)])
                    st = kvst[g % 3]
                    S.op('act', lambda st=st, pb=pb: act.copy(out=fs(st, 0, 512), in_=bank(pb)), reads=[('ps', pb)], writes=[('F', st)])
                    S.dma('sp', nk_d[g * 128:(g + 1) * 128, :], fs(st, 0, 512), reads=[('F', st)])
                    S.op('dve', lambda g=g, pb=pb: dve.tensor_copy(out=kT_p[:, g, :], in_=bank(pb)), reads=[('ps', pb)], writes=['kT_p'])
            for tb in (range(4) if which == 'v' else ()):
                pb = next_pb()
                fns = [lambda k=k, tb=tb, pb=pb: pe.matmul(bank(pb, 0, 512), lhsT=hT[:, k, tb * 128:(tb + 1) * 128],
                                                           rhs=ring[:, slot, k, :], start=(k == 0), stop=(k == 7)) for k in range(8)]
                S.group('pe', fns, reads=[('ring', slot)] + HTP, writes=[('ps', pb)])
                st = kvst[tb % 3]
                S.op('act', lambda st=st, pb=pb: act.copy(out=fs(st, 0, 512), in_=bank(pb)), reads=[('ps', pb)], writes=[('F', st)])
                sq, t0 = tb // 2, (tb % 2) * 128
                S.dma('sp', out_d[sq, t0:t0 + 128, :], fs(st, 0, 512), reads=[('F', st)])
                if which == 'k':
                    def k_transp(tb=tb, st=st):
                        pb2 = 6
                        fns = [lambda c4=c4: pe.transpose(out=bank(pb2, c4 * 128, 128), in_=fs(st, c4 * 128, c4 * 128 + 128),
                                                          identity=ident_f[:, :]) for c4 in range(4)]
                        S.group('pe', fns, reads=[('F', st), 'ident_f'], writes=[('ps', pb2)])
                        S.op('dve', lambda: dve.tensor_copy(out=kT_p[:, :, tb * 128:(tb + 1) * 128],
                                                            in_=bank(pb2).rearrange("p (c t) -> p c t", c=4)),
                             reads=[('ps', pb2)], writes=['kT_p'])
                    if ktp_pend:
                        ktp_pend.pop(0)()
                    ktp_pend.append(k_transp)
                else:
                    for hh in range(2):
                        S.op('dve', lambda tb=tb, pb=pb, hh=hh: dve.tensor_copy(
                            out=v_p[:, hh, tb, :].rearrange("p (g e d) -> p g e d", g=4, e=2)[:, :, hh, :],
                            in_=bank(pb).rearrange("p (g e d) -> p g e d", g=4, e=2)[:, :, hh, :]),
                            reads=[('ps', pb)], writes=['v_p'])
            if which == 'k':
                for g in range(4):
                    pb = next_pb()
                    if g == 1 and ktp_pend:
                        ktp_pend.pop(0)()
                    fns = []
                    for k in range(8):
                        fns.append(lambda k=k, g=g, pb=pb: pe.matmul(bank(pb, 0, 512), lhsT=ring[:, slot, k, g * 128:(g + 1) * 128],
                                                                     rhs=hT[:, k, 512:1024], start=(k == 0), stop=(k == 7)))
                    for k in range(8):
                        fns.append(lambda k=k, g=g, pb=pb: pe.matmul(bank(pb + 1, 0, 384), lhsT=ring[:, slot, k, g * 128:(g + 1) * 128],
                                                                     rhs=hT[:, k, 1024:1408], start=(k == 0), stop=(k == 7)))
                    S.group('pe', fns, reads=[('ring', slot)] + HT, writes=[('ps', pb), ('ps', pb + 1)])
                    S.op('act', lambda g=g, pb=pb: act.copy(out=bs(kT_w[g], 0, TW), in_=pair(pb, 0, TW)),
                         reads=[('ps', pb), ('ps', pb + 1)], writes=[('B', kT_w[g])])
            else:
                for wb in range(7):
                    pb = next_pb()
                    fns = [lambda k=k, wb=wb, pb=pb: pe.matmul(bank(pb, 0, 512), lhsT=hT[:, k, 512 + wb * 128:512 + (wb + 1) * 128],
                                                               rhs=ring[:, slot, k, :], start=(k == 0), stop=(k == 7)) for k in range(8)]
                    S.group('pe', fns, reads=[('ring', slot)] + HT, writes=[('ps', pb)])
                    for hh in range(2):
                        oap = v_w[:, hh, wb, :].rearrange("p (g e d) -> p g e d", g=4, e=2)[:, :, hh, :]
                        iap = bank(pb).rearrange("p (g e d) -> p g e d", g=4, e=2)[:, :, hh, :]
                        if hh == 1:
                            S.op('act', lambda oap=oap, iap=iap: act.copy(out=oap, in_=iap), reads=[('ps', pb)], writes=[('v_w', wb)])
                        else:
                            S.op('dve', lambda oap=oap, iap=iap: dve.tensor_copy(out=oap, in_=iap), reads=[('ps', pb)], writes=[('v_w', wb)])
            mod_piece(0, pcn + 1)
            if pcn == 4:
                mod_finish_gate(0)
            pool_post(2 * (pcn - 3))
            pool_post(2 * (pcn - 3) + 1)
        FP.release(kvst)
        FP.release(upad)
        BP.release(siluA)

        siluB = BP.alloc(4)
        slot = ring_get('wie', 5)
        for g in range(4):
            pb = next_pb()
            proj_chunk(slot, g, pb, sample_cols=(864 + 16, 288, 16))
            S.op('act', lambda g=g, pb=pb: act.activation(out=bs(siluB[g], 0, TM), in_=pair(pb, 0, TM), func=AF.Silu),
                 reads=[('ps', pb), ('ps', pb + 1)], writes=[('B', siluB[g])])

        if stop == 5:
            S.finish()
            return nc
        psb = ps.bitcast(BF16)
        CKB = 6
        fns = []
        for hp in range(4):
            for lb in range(2):
                idx = hp * 2 + lb
                fns.append(lambda hp=hp, lb=lb, idx=idx: pe.transpose(
                    out=psb[:, CKB * 1024 + idx * 128: CKB * 1024 + (idx + 1) * 128],
                    in_=bs(ck_tm[lb], hp * 128, (hp + 1) * 128), identity=ident_b[:, :]))
        S.group('pe', fns, reads=[('B', ck_tm[0]), ('B', ck_tm[1]), 'ident_b'], writes=[('ps', CKB)])
        BP.release(ck_tm)
        S.op('dve', lambda: dve.tensor_copy(out=kcT[:, :, :].rearrange("p a b -> p (a b)"), in_=psb[:, CKB * 1024:CKB * 1024 + 1024]),
             reads=[('ps', CKB)], writes=['kcT'])

        if stop == 6:
            S.finish()
            return nc
        ATT, DEN = 0, 2
        sbanks = [4, 5, 6]
        PT = BP.alloc(4)

        tasks = []
        for hp in range(4):
            for sq in range(2):
                for hh in range(2):
                    tasks.append(('p', hp, hh, sq, 0))
            for hh in range(2):
                for kb in (-1, 2, 3, 4, 5, 6, 7, 8):
                    tasks.append(('s', hp, hh, 0, kb))

        def emit_S(ti):
            kind, hp, hh, sq, kb = tasks[ti]
            h = 2 * hp + hh
            pr = slice(hh * 64, hh * 64 + 64)
            sbk = sbanks[ti % 3]
            if kind == 'p':
                S.group('pe', [lambda kb_=kb_: pe.matmul(
                    bank(sbk, kb_ * 256, 256), lhsT=kT_p[:, hp, sq * 256 + kb_ * 128: sq * 256 + (kb_ + 1) * 128],
                    rhs=bs(qT[h], sq * 256, (sq + 1) * 256), start=True, stop=True) for kb_ in range(2)],
                    reads=['kT_p', ('B', qT[h])], writes=[('ps', sbk)])
            elif kb < 7:
                if kb == -1 and h + 1 < 8:
                    t2r_load(h + 1)
                fns = []
                off_ = 0
                for kb_ in ((0, 1) if kb == -1 else (kb,)):
                    qa, qb = QR[kb_]
                    nq = qb - qa
                    fns.append(lambda kb_=kb_, off_=off_, nq=nq: pe.matmul(
                        bank(sbk, off_, nq), lhsT=ident_b[:, :], rhs=t2rb[:, h % 2, QOFF[kb_]:QOFF[kb_] + nq], start=True, stop=False))
                    fns.append(lambda kb_=kb_, off_=off_, nq=nq, qa=qa, qb=qb: pe.matmul(
                        bank(sbk, off_, nq), lhsT=bs(kT_w[hp], kb_ * 128, (kb_ + 1) * 128),
                        rhs=bs(qT[h], 512 + qa, 512 + qb), start=False, stop=True))
                    off_ += nq
                S.group('pe', fns, reads=['ident_b', ('t2rb', h % 2), ('B', kT_w[hp]), ('B', qT[h])],
                        writes=[('ps', sbk)])
            else:
                lb = kb - 7
                S.group('pe', [lambda: pe.matmul(bank(sbk, 0, 320), lhsT=kcT[:, hp, lb * 128:(lb + 1) * 128],
                                                 rhs=bs(qT[h], 512, 832), start=True, stop=True)],
                        reads=['kcT', ('B', qT[h])], writes=[('ps', sbk)])

        def emit_exp_pv(ti):
            kind, hp, hh, sq, kb = tasks[ti]
            h = 2 * hp + hh
            pr = slice(hh * 64, hh * 64 + 64)
            sbk = sbanks[ti % 3]
            p_ = PT[ti % 4]
            qa = 0
            if kind == 's' and kb == -1:
                n = (QR[0][1] - QR[0][0]) + (QR[1][1] - QR[1][0])
            elif kind == 's' and kb < 7:
                qa, qb_ = QR[kb]
                n = qb_ - qa
            else:
                n = 512 if kind == 'p' else 320
            S.op('act', lambda: act.activation(out=bs(p_, 0, n), in_=bank(sbk, 0, n), func=AF.Exp),
                 reads=[('ps', sbk)], writes=[('B', p_)])
            if kind == 'p':
                c0 = sq * 256
                fns = []
                for (bk_, lh_) in ((ATT, None), (DEN, oh[:, hh, :])):
                    for kb_ in range(2):
                        l_ = v_p[:, hh, sq * 2 + kb_, hp * 128:(hp + 1) * 128] if lh_ is None else lh_
                        fns.append(lambda bk_=bk_, kb_=kb_, l_=l_: pe.matmul(
                            ps[:, bk_ * 512 + c0:bk_ * 512 + c0 + 256], lhsT=l_, rhs=bs(p_, kb_ * 256, kb_ * 256 + 256),
                            start=(hh == 0 and kb_ == 0), stop=(hh == 1 and kb_ == 1)))
                S.group('pe', fns, reads=['v_p', 'oh', ('B', p_)], writes=[('ps', ATT), ('ps', DEN)])
                return
            elif kb == -1:
                ab, db = ATT + 1, DEN + 1
                fns = []
                off_ = 0
                for kb_ in (0, 1):
                    qa_, qb_ = QR[kb_]
                    nq = qb_ - qa_
                    for (bk_, l_) in ((ab, v_w[:, hh, kb_, hp * 128:(hp + 1) * 128]), (db, oh[:, hh, :])):
                        fns.append(lambda bk_=bk_, l_=l_, qa_=qa_, nq=nq, off_=off_, kb_=kb_: pe.matmul(
                            ps[:, bk_ * 512 + qa_:bk_ * 512 + qa_ + nq], lhsT=l_, rhs=bs(p_, off_, off_ + nq),
                            start=(hh == 0 and kb_ == 0), stop=False, skip_group_check=True))
                    off_ += nq
                S.group('pe', fns, reads=[('v_w', 0), ('v_w', 1), 'oh', ('B', p_)], writes=[('ps', ab), ('ps', db)])
                return
            elif kb < 7:
                c0, ab, db, last = 0, ATT + 1, DEN + 1, 8
                vl, vkey = v_w[:, hh, kb, hp * 128:(hp + 1) * 128], ('v_w', kb)
            else:
                c0, ab, db, last = 0, ATT + 1, DEN + 1, 8
                vl, vkey = cvb[:, hh, kb - 7, hp * 128:(hp + 1) * 128], ('cvb', kb - 7)
            st_ = False
            sp_ = (hh == 1 and kb == last)
            c0 = c0 + qa
            fns = [lambda: pe.matmul(ps[:, ab * 512 + c0:ab * 512 + c0 + n], lhsT=vl, rhs=bs(p_, 0, n), start=st_, stop=sp_,
                                     skip_group_check=(kind == 's')),
                   lambda: pe.matmul(ps[:, db * 512 + c0:db * 512 + c0 + n], lhsT=oh[:, hh, :], rhs=bs(p_, 0, n), start=st_, stop=sp_,
                                     skip_group_check=(kind == 's'))]
            S.group('pe', fns, reads=[vkey, 'oh', ('B', p_)], writes=[('ps', ab), ('ps', db)])

        def finalize_pair(hp):
            denc = FP.alloc()
            attc = FP.alloc()
            S.op('act', lambda: act.activation(out=fs(denc, 0, TM), in_=pair(DEN, 0, TM), func=AF.Ln),
                 reads=[('ps', DEN), ('ps', DEN + 1)], writes=[('F', denc)])
            S.op('dve', lambda: dve.tensor_copy(out=fs(attc, 0, TM), in_=pair(ATT, 0, TM)),
                 reads=[('ps', ATT), ('ps', ATT + 1)], writes=[('F', attc)])
            S.op('act', lambda: act.activation(out=fs(denc, 0, TM), in_=fs(denc, 0, TM), func=AF.Exp, scale=-1.0),
                 reads=[('F', denc)], writes=[('F', denc)])
            S.op('dve', lambda: dve.tensor_tensor(out=fs(attc, 0, TM), in0=fs(attc, 0, TM), in1=fs(denc, 0, TM), op=ALU.mult),
                 reads=[('F', attc), ('F', denc)], writes=[('F', attc)])
            S.op('dve', lambda: dve.tensor_tensor(out=bs(abo[4 + hp], 0, TM), in0=fs(attc, 0, TM), in1=bs(siluB[hp], 0, TM), op=ALU.mult),
                 reads=[('F', attc), ('B', siluB[hp])], writes=[('B', abo[4 + hp])])
            FP.release([denc, attc])

        LOOK = 2
        NT = len(tasks)
        for ti in range(min(LOOK, NT)):
            emit_S(ti)
        for ti in range(NT):
            if ti + LOOK < NT:
                emit_S(ti + LOOK)
            emit_exp_pv(ti)
            if ti % 20 == 19:
                finalize_pair(ti // 20)
            if ti in (10, 28, 46, 60, 68, 74):
                mod_piece(1, (10, 28, 46, 60, 68, 74).index(ti))
        mod_finish(1)
        mod_finish_gate(1)
        BP.release(PT)
        BP.release(qT)
        BP.release(kT_w)
        BP.release(siluB)

        if stop == 8:
            S.finish()
            return nc
        ABO = [('B', s) for s in abo]
        ring_limit[0] = piece_idx[('wio', 1)]
        wslots = [ring_get('woe', 0), ring_get('woe', 1)]
        sq1 = BP.alloc(2)
        sbk4 = [0, 1, 2, 3, 6, 7]
        wcnt = [0]
        tmpL1 = FP.alloc(8)
        tmpsL1 = [((lambda a_, b_, t=t: fs(t, a_, b_)), ('F', t)) for t in tmpL1]

        def l1_stat(n, region):
            q = sq1[n % 2]
            c0, ncol, bk = (0, 512, 4) if region == 'p' else (512, 320, 5)
            S.op('act', lambda: act.activation(out=bs(q, c0, c0 + ncol), in_=fs(xm[n], c0, c0 + ncol), func=AF.Square),
                 reads=[XM[n]], writes=[('B', q)])
            S.group('pe', [lambda: pe.matmul(bank(bk, 0, ncol), lhsT=ones_b[:, :], rhs=bs(q, c0, c0 + ncol), start=(n == 0), stop=(n == 7))],
                    reads=[('B', q), 'ones_b'], writes=[('ps', bk)])

        def chain_gen(region):
            if region == 'p':
                rstd_from(4, 512, 1.0 / 1024, to_psum=True)
                yield
                rk, lo, hi, regs, rb = [('ps', 4)], 0, 512, [(0, 512, 0, 0)], 4
            else:
                rstd_from(5, 320, 1.0 / 1024, to_psum=True)
                yield
                rk, lo, hi, regs, rb = [('ps', 5)], 512, 832, [(512, 320, 1, 512)], 4
            for c in range(8):
                oap, tk = tmpsL1[c][0](lo, hi), tmpsL1[c][1]
                S.op('dve', lambda c=c, oap=oap: dve.tensor_tensor(out=oap, in0=fs(xm[c], lo, hi),
                                                                    in1=ps[:, rb * 512 + lo:rb * 512 + hi], op=ALU.mult),
                     reads=[XM[c]] + rk, writes=[tk])
                (c0, n_, q_, h0) = regs[0]
                if region == 'p':
                    S.op('act', lambda c=c, oap=oap: act.activation(out=hT[:, c, h0:h0 + n_], in_=oap, func=AF.Identity,
                                                                    bias=shift_col(1, c, q_), scale=gs_col(1, c, q_)),
                         reads=[tk, ('mT', 1), ('gs', 1)], writes=[('hTp', c)])
                else:
                    S.op('pool', lambda c=c, oap=oap: gp.tensor_scalar(out=hT[:, c, h0:h0 + n_], in0=oap, scalar1=gs_col(1, c, q_),
                                                                       scalar2=shift_col(1, c, q_), op0=ALU.mult, op1=ALU.add),
                         reads=[tk, ('mT', 1), ('gs', 1)], writes=[('hTs', c)])
                yield

        def woe_region(region, stepper):
            pend = []
            for n in range(8):
                slot, n4 = wslots[n // 4], n % 4
                bk = sbk4[wcnt[0] % 6]
                wcnt[0] += 1
                c0, ncol, q_ = (0, 512, 0) if region == 'p' else (512 + 16, 288, 1)
                for (k0, k1) in (((0, 7), (7, 8)) if region == 'p' else ((0, 8),)):
                    fns = [lambda k=k: pe.matmul(bank(bk, 0, ncol), lhsT=ring[:, slot, k, n4 * 128:(n4 + 1) * 128],
                                                 rhs=bs(abo[k], c0, c0 + ncol), start=(k == 0), stop=(k == 7)) for k in range(k0, k1)]
                    S.group('pe', fns, reads=[('ring', slot)] + ABO[k0:k1], writes=[('ps', bk)])
                S.op('dve', lambda: dve.scalar_tensor_tensor(out=fs(xm[n], c0, c0 + ncol), in0=bank(bk, 0, ncol), scalar=gate_col(0, n, q_),
                                                             in1=fs(xm[n], c0, c0 + ncol), op0=ALU.mult, op1=ALU.add),
                     reads=[('ps', bk), ('mTg', 0), XM[n]], writes=[XM[n]])
                if pend:
                    l1_stat(pend.pop(0), region)
                pend.append(n)
                if stepper is not None:
                    stepper()
            l1_stat(pend.pop(0), region)

        woe_region('p', None)
        gp_ = chain_gen('p')

        def step_p():
            try:
                next(gp_)
            except StopIteration:
                pass
        woe_region('s', step_p)
        for _ in range(12):
            step_p()
        ring_limit[0] = None
        BP.release(sq1)
        if debug and 'x1' in debug:
            for c in range(8):
                S.dma('sp', dbg_d['x1'][c], fs(xm[c], 0, TM), reads=[XM[c]])

        if stop == 9:
            S.finish()
            return nc
        pbs[:] = [0, 2, 4, 6]

        def proj1(slot, n4, pb, sn=320, s0=512):
            if sn == 320:
                proj_chunk(slot, n4, pb, sample_cols=(s0 + 16, 288, 16))
            else:
                proj_chunk(slot, n4, pb, sample_cols=(s0, sn))

        def build_diag(i):
            sl = BP.alloc(5, consecutive=True)
            base = sl[0] * SLOT
            outap = poolb[:, base:base + 31 * 128].rearrange("p (j d) -> p j d", j=31)
            in0 = ident_f[:, :].unsqueeze(1).broadcast_to([128, 31, 128])
            in1 = par[:, PC_CD + i * 31:PC_CD + (i + 1) * 31].unsqueeze(2).broadcast_to([128, 31, 128])
            S.op('dve', lambda: dve.tensor_tensor(out=outap, in0=in0, in1=in1, op=ALU.mult),
                 reads=['ident_f', 'par'], writes=[('B', x) for x in sl])
            return sl

        cdo = abo
        slot = ring_get('wio', 1)
        gs_ = chain_gen('s')
        cpb = [next_pb() for _ in range(3)]
        for i in range(3):
            proj_P(slot, i, cpb[i], fine=(i == 0))
            try:
                next(gs_)
                next(gs_)
                next(gs_)
            except StopIteration:
                pass
        for _ in gs_:
            pass
        FP.release(tmpL1)
        cc = FP.alloc(4)
        acc = FP.alloc(4)

        def cc_evac(i, pb):
            S.op('act', lambda: act.copy(out=fs(cc[i], 0, 512), in_=bank(pb, 0, 512)), reads=[('ps', pb)], writes=[('F', cc[i])])
            S.op('dve', lambda: dve.tensor_copy(out=fs(cc[i], 512, TM), in_=bank(pb + 1, 0, 320)), reads=[('ps', pb + 1)], writes=[('F', cc[i])])
        for i in range(3):
            proj_S(slot, i, cpb[i], (512 + 16, 288, 16))
            cc_evac(i, cpb[i])
        pb = next_pb()
        proj1(slot, 3, pb)
        cc_evac(3, pb)
        slot = ring_get('wio', 2)
        for i in range(4):
            pb = next_pb()
            proj1(slot, i, pb)
            c_ = cc[i]
            a_ = acc[i]
            S.op('dve', lambda c_=c_, pb=pb: dve.tensor_tensor(out=fs(c_, 0, TM), in0=pair(pb, 0, TM), in1=fs(c_, 0, TM), op=ALU.mult),
                 reads=[('ps', pb), ('ps', pb + 1), ('F', c_)], writes=[('F', c_)])
            S.op('dve', lambda c_=c_: dve.tensor_tensor(out=fs(c_, 512, 832), in0=fs(c_, 512, 832), in1=vmask[:, :], op=ALU.mult),
                 reads=[('F', c_), 'vmask'], writes=[('F', c_)])
            w0, w1, w2 = (pcol(PC_CC + i * 3 + j) for j in range(3))
            S.op('act', lambda c_=c_, a_=a_, w1=w1: act.activation(out=fs(a_, 0, TM), in_=fs(c_, 0, TM), func=AF.Identity, scale=w1),
                 reads=[('F', c_), 'par'], writes=[('F', a_)])
            v3 = lambda s, a, b: fs(s, 0, 512).rearrange("p (s t) -> p s t", s=2)[:, :, a:b]
            S.op('dve', lambda c_=c_, a_=a_, w0=w0: dve.scalar_tensor_tensor(out=v3(a_, 1, 256), in0=v3(c_, 0, 255), scalar=w0, in1=v3(a_, 1, 256),
                                                                             op0=ALU.mult, op1=ALU.add),
                 reads=[('F', c_), ('F', a_), 'par'], writes=[('F', a_)])
            S.op('dve', lambda c_=c_, a_=a_, w2=w2: dve.scalar_tensor_tensor(out=v3(a_, 0, 255), in0=v3(c_, 1, 256), scalar=w2, in1=v3(a_, 0, 255),
                                                                             op0=ALU.mult, op1=ALU.add),
                 reads=[('F', c_), ('F', a_), 'par'], writes=[('F', a_)])
            S.op('dve', lambda c_=c_, a_=a_, w0=w0: dve.scalar_tensor_tensor(out=fs(a_, 513, 832), in0=fs(c_, 512, 831), scalar=w0, in1=fs(a_, 513, 832),
                                                                             op0=ALU.mult, op1=ALU.add),
                 reads=[('F', c_), ('F', a_), 'par'], writes=[('F', a_)])
            S.op('dve', lambda c_=c_, a_=a_, w2=w2: dve.scalar_tensor_tensor(out=fs(a_, 512, 831), in0=fs(c_, 513, 832), scalar=w2, in1=fs(a_, 512, 831),
                                                                             op0=ALU.mult, op1=ALU.add),
                 reads=[('F', c_), ('F', a_), 'par'], writes=[('F', a_)])
        FP.release(cc)
        slot = ring_get('wio', 0)
        for i in range(4):
            pb = next_pb()
            proj1(slot, i, pb)
            a_ = acc[i]
            S.op('dve', lambda a_=a_, pb=pb: dve.tensor_tensor(out=fs(a_, 0, TM), in0=pair(pb, 0, TM), in1=fs(a_, 0, TM), op=ALU.mult),
                 reads=[('ps', pb), ('ps', pb + 1), ('F', a_)], writes=[('F', a_)])
        sgt = FP.alloc(2)
        slot = ring_get('wio', 3)
        for i in range(4):
            pb = next_pb()
            proj1(slot, i, pb)
            a_, t_ = acc[i], sgt[i % 2]
            S.op('act', lambda t_=t_, pb=pb: act.activation(out=fs(t_, 0, TM), in_=pair(pb, 0, TM), func=AF.Silu),
                 reads=[('ps', pb), ('ps', pb + 1)], writes=[('F', t_)])
            S.op('dve', lambda a_=a_, t_=t_, i=i: dve.tensor_tensor(out=bs(cdo[i], 0, TM), in0=fs(a_, 0, TM), in1=fs(t_, 0, TM), op=ALU.mult),
                 reads=[('F', a_), ('F', t_)], writes=[('B', cdo[i])])
        FP.release(acc)
        ad = FP.alloc(4)
        slot = ring_get('wio', 4)
        for i in range(4):
            pb = next_pb()
            proj1(slot, i, pb)
            S.op('act', lambda i=i, pb=pb: act.copy(out=fs(ad[i], 0, 512), in_=bank(pb, 0, 512)),
                 reads=[('ps', pb)], writes=[('F', ad[i])])
            S.op('dve', lambda i=i, pb=pb: dve.tensor_tensor(out=fs(ad[i], 512, 832), in0=bank(pb + 1, 0, 320), in1=vmask[:, :], op=ALU.mult),
                 reads=[('ps', pb + 1), 'vmask'], writes=[('F', ad[i])])
        glu = BP.alloc(4)
        slot = ring_get('wio', 5)
        for i in range(4):
            g_ = glu[i]
            S.op('pool', lambda g_=g_: gp.memset(bs(g_, 0, SLOT), 0.0), writes=[('B', g_)])
            pb = next_pb()
            proj1(slot, i, pb)
            t_ = sgt[i % 2]
            S.op('act', lambda t_=t_, pb=pb: act.activation(out=fs(t_, 0, TM), in_=pair(pb, 0, TM), func=AF.Sigmoid),
                 reads=[('ps', pb), ('ps', pb + 1)], writes=[('F', t_)])
            S.op('dve', lambda g_=g_, t_=t_, i=i: dve.tensor_tensor(
                out=bs(g_, 15, 15 + 542).rearrange("p (s t) -> p s t", s=2)[:, :, 0:256],
                in0=fs(ad[i], 0, 512).rearrange("p (s t) -> p s t", s=2), in1=fs(t_, 0, 512).rearrange("p (s t) -> p s t", s=2), op=ALU.mult),
                reads=[('F', ad[i]), ('F', t_)], writes=[('B', g_)])
            S.op('dve', lambda g_=g_, t_=t_, i=i: dve.tensor_tensor(out=bs(g_, 557, 877), in0=fs(ad[i], 512, 832), in1=fs(t_, 512, 832), op=ALU.mult),
                 reads=[('F', ad[i]), ('F', t_)], writes=[('B', g_)])
        FP.release(ad)
        if stop == 10:
            S.finish()
            return nc
        T1 = 768
        sgd = BP.alloc(4)
        slot = ring_get('wio', 6)
        for i in range(4):
            pb = next_pb()
            proj1(slot, i, pb, sn=256, s0=512 + OWN0)
            S.op('act', lambda i=i, pb=pb: act.activation(out=bs(sgd[i], 0, T1), in_=pair(pb, 0, T1), func=AF.Silu),
                 reads=[('ps', pb), ('ps', pb + 1)], writes=[('B', sgd[i])])
        act_preload(AF.Ln, 2)
        z = FP.alloc(4)
        zb = BP.alloc(4)
        MEANB, SQB = 4, 6

        def conv_stats(i):
            q0, q1 = zb[(i % 2) * 2], zb[(i % 2) * 2 + 1]
            fns = []
            for (bk, q) in ((MEANB, q0), (SQB, q1)):
                fns.append(lambda bk=bk, q=q: pe.matmul(bank(bk, 0, 512), lhsT=ones_b[:, :], rhs=bs(q, 0, 512), start=(i == 0), stop=(i == 3)))
                fns.append(lambda bk=bk, q=q: pe.matmul(bank(bk + 1, 0, 256), lhsT=ones_b[:, :], rhs=bs(q, 512, 768), start=(i == 0), stop=(i == 3)))
            S.group('pe', fns, reads=[('B', q0), ('B', q1), 'ones_b'], writes=[('ps', MEANB), ('ps', MEANB + 1), ('ps', SQB), ('ps', SQB + 1)])

        dgs = {0: build_diag(0)}
        for i in range(4):
            if i + 1 < 4:
                dgs[i + 1] = build_diag(i + 1)
            dg = dgs[i]
            pb = 0 if i % 2 == 0 else 2
            g_ = glu[i]
            regions = ((0, 0, pb, 0), (271, 0, pb, 256), (557 + OWN0 - 15, 0, pb + 1, 0))
            fns = []
            for (off, _, bk, bo) in regions:
                for j in range(31):
                    o = dg[0] * SLOT + j * 128
                    fns.append(lambda off=off, bk=bk, bo=bo, j=j, o=o, g_=g_: pe.matmul(
                        bank(bk, bo, 256), lhsT=poolb[:, o:o + 128], rhs=bs(g_, off + j, off + j + 256), start=(j == 0), stop=(j == 30)))
            S.group('pe', fns, reads=[('B', g_)] + [('B', s) for s in dg], writes=[('ps', pb), ('ps', pb + 1)])
            BP.release(dg)
            S.op('act', lambda i=i, pb=pb: act.activation(out=fs(z[i], 0, T1), in_=pair(pb, 0, T1), func=AF.Identity, bias=pcol(PC_CDB + i)),
                 reads=[('ps', pb), ('ps', pb + 1), 'par'], writes=[('F', z[i])])
            q0, q1 = zb[(i % 2) * 2], zb[(i % 2) * 2 + 1]
            S.op('dve', lambda i=i, q0=q0, pb=pb: dve.tensor_scalar(out=bs(q0, 0, T1), in0=pair(pb, 0, T1), scalar1=pcol(PC_CDB + i), scalar2=None,
                                                                      op0=ALU.add),
                 reads=[('ps', pb), ('ps', pb + 1), 'par'], writes=[('B', q0)])
            S.op('act', lambda i=i, q1=q1, pb=pb: act.activation(out=bs(q1, 0, T1), in_=pair(pb, 0, T1), func=AF.Square, bias=pcol(PC_CDB + i)),
                 reads=[('ps', pb), ('ps', pb + 1), 'par'], writes=[('B', q1)])
            if i >= 1:
                conv_stats(i - 1)
        conv_stats(3)
        BP.release(zb)
        BP.release(glu)
        CDO = [('B', s_) for s_ in cdo]

        def woo_pass(k0, k1, mode, chunks=range(8), per_k=False):
            wt = FP.alloc(2) if mode != 'dve' else None
            for n in chunks:
                if True:
                    pc, n4 = n // 4, n % 4
                    slot = ring_get('woo', pc)
                    pb = next_pb()
                    fns = []
                    for k in range(k0, k1):
                        fns.append(lambda k=k, n4=n4, pb=pb: pe.matmul(bank(pb, 0, 512), lhsT=ring[:, slot, k, n4 * 128:(n4 + 1) * 128],
                                                                       rhs=bs(cdo[k], 0, 512), start=(k == k0), stop=(k == k1 - 1)))
                    for k in range(k0, k1):
                        c0 = 512 + OWN0 if k < 4 else 512
                        fns.append(lambda k=k, n4=n4, pb=pb, c0=c0: pe.matmul(bank(pb + 1, 0, 256), lhsT=ring[:, slot, k, n4 * 128:(n4 + 1) * 128],
                                                                              rhs=bs(cdo[k], c0, c0 + 256), start=(k == k0), stop=(k == k1 - 1)))
                    if per_k:
                        nk_ = k1 - k0
                        for j_ in range(nk_):
                            S.group('pe', [fns[j_], fns[nk_ + j_]], reads=[('ring', slot), CDO[k0 + j_]], writes=[('ps', pb), ('ps', pb + 1)])
                    else:
                        S.group('pe', fns, reads=[('ring', slot)] + CDO[k0:k1], writes=[('ps', pb), ('ps', pb + 1)])
                    if mode == 'dve':
                        S.op('dve', lambda n=n, pb=pb: dve.scalar_tensor_tensor(out=fs(xm[n], 0, 512), in0=bank(pb, 0, 512), scalar=gate_col(1, n, 0),
                                                                                in1=fs(xm[n], 0, 512), op0=ALU.mult, op1=ALU.add),
                             reads=[('ps', pb), ('mTg', 1), XM[n]], writes=[XM[n]])
                        S.op('dve', lambda n=n, pb=pb: dve.scalar_tensor_tensor(out=fs(xm[n], 544, 800), in0=bank(pb + 1, 0, 256), scalar=gate_col(1, n, 1),
                                                                                in1=fs(xm[n], 544, 800), op0=ALU.mult, op1=ALU.add),
                             reads=[('ps', pb + 1), ('mTg', 1), XM[n]], writes=[XM[n]])
                    else:
                        t_ = wt[n % 2]
                        S.op('act', lambda n=n, pb=pb, t_=t_: act.activation(out=fs(t_, 0, 512), in_=bank(pb, 0, 512), func=AF.Identity,
                                                                             scale=gate_col(1, n, 0)),
                             reads=[('ps', pb), ('mTg', 1)], writes=[('F', t_)])
                        S.op('act', lambda n=n, pb=pb, t_=t_: act.activation(out=fs(t_, 512, 768), in_=bank(pb + 1, 0, 256), func=AF.Identity,
                                                                             scale=gate_col(1, n, 1)),
                             reads=[('ps', pb + 1), ('mTg', 1)], writes=[('F', t_)])
                        S.op('pool', lambda n=n, t_=t_: gp.tensor_tensor(out=fs(xm[n], 0, 512), in0=fs(xm[n], 0, 512), in1=fs(t_, 0, 512), op=ALU.add),
                             reads=[('F', t_), XM[n]], writes=[XM[n]])
                        S.op('pool', lambda n=n, t_=t_: gp.tensor_tensor(out=fs(xm[n], 544, 800), in0=fs(xm[n], 544, 800), in1=fs(t_, 512, 768), op=ALU.add),
                             reads=[('F', t_), XM[n]], writes=[XM[n]])
            if wt is not None:
                FP.release(wt)

        var = FP.alloc()
        MK = [('ps', MEANB), ('ps', MEANB + 1)]
        VK = [('ps', SQB), ('ps', SQB + 1)]
        S.op('act', lambda: act.activation(out=pair(MEANB, 0, T1), in_=pair(MEANB, 0, T1), func=AF.Copy, scale=1.0 / 512),
             reads=MK, writes=MK)
        S.op('act', lambda: act.activation(out=fs(var, 0, T1), in_=pair(MEANB, 0, T1), func=AF.Square),
             reads=MK, writes=[('F', var)])
        S.op('dve', lambda: dve.scalar_tensor_tensor(out=fs(var, 0, T1), in0=pair(SQB, 0, T1), scalar=1.0 / 512, in1=fs(var, 0, T1),
                                                     op0=ALU.mult, op1=ALU.subtract),
             reads=VK + [('F', var)], writes=[('F', var)])
        S.op('act', lambda: act.activation(out=fs(var, 0, T1), in_=fs(var, 0, T1), func=AF.Ln, bias=par[:, NPAR - 1:NPAR], scale=1.0),
             reads=[('F', var), 'par'], writes=[('F', var)])
        S.op('act', lambda: act.activation(out=pair(SQB, 0, T1), in_=fs(var, 0, T1), func=AF.Exp, scale=-0.5), reads=[('F', var)], writes=VK)
        for i in range(4):
            S.op('dve', lambda i=i: dve.tensor_tensor(out=fs(z[i], 0, T1), in0=fs(z[i], 0, T1), in1=pair(MEANB, 0, T1), op=ALU.subtract),
                 reads=[('F', z[i])] + MK, writes=[('F', z[i])])
        for i in range(4):
            S.op('dve', lambda i=i: dve.tensor_tensor(out=fs(z[i], 0, T1), in0=fs(z[i], 0, T1), in1=pair(SQB, 0, T1), op=ALU.mult),
                 reads=[('F', z[i])] + VK, writes=[('F', z[i])])
        pbs[:] = [0, 2]
        woo_pass(0, 4, 'actpool', chunks=(0, 1))
        for i in range(4):
            S.op('act', lambda i=i: act.activation(out=fs(z[i], 0, T1), in_=fs(z[i], 0, T1), func=AF.Silu,
                                                   bias=pcol(PC_LNB + i), scale=pcol(PC_LNG + i)),
                 reads=[('F', z[i]), 'par'], writes=[('F', z[i])])
        FP.release(var)
        woo_pass(0, 4, 'dve', chunks=(2, 3))
        for i in range(4):
            S.op('dve', lambda i=i: dve.tensor_tensor(out=bs(cdo[4 + i], 0, T1), in0=fs(z[i], 0, T1), in1=bs(sgd[i], 0, T1), op=ALU.mult),
                 reads=[('F', z[i]), ('B', sgd[i])], writes=[('B', cdo[4 + i])])
        BP.release(sgd)
        FP.release(sgt)
        FP.release(z)

        woo_pass(0, 4, 'actpool', chunks=(4, 5, 6, 7))
        pbs[:] = [0, 2, 4, 6]
        woo_pass(4, 8, 'dve', chunks=(0,), per_k=True)
        woo_pass(4, 8, 'dve', chunks=range(1, 8))
        BP.release(abo)

        if stop == 11:
            S.finish()
            return nc
        fgb = FP.alloc(2)
        S.dma('sp', fs(fgb[0], 0, 512), fgb_d[:, 0:512], writes=[('F', fgb[0])])
        S.dma('sp', fs(fgb[1], 0, 512), fgb_d[:, 512:1024], writes=[('F', fgb[1])])
        junk = FP.alloc()
        ost = FP.alloc(4)
        for tb in range(6):
            pb = (0, 2, 4)[tb % 3]
            c0 = tb * 128 if tb < 4 else 544 + (tb - 4) * 128
            fns = [lambda c=c, c0=c0, pb=pb: pe.transpose(out=ps[:, pb * 512 + c * 128: pb * 512 + (c + 1) * 128],
                                                          in_=fs(xm[c], c0, c0 + 128), identity=ident_f[:, :]) for c in range(8)]
            S.group('pe', fns, reads=XM + ['ident_f'], writes=[('ps', pb), ('ps', pb + 1)])
            for hf in range(2):
                S.op('act', lambda hf=hf, pb=pb, tb=tb: act.activation(out=fs(junk, 0, 512), in_=bank(pb + hf), func=AF.Square,
                                                                       accum_out=stt[:, tb * 4 + hf:tb * 4 + hf + 1]),
                     reads=[('ps', pb + hf)], writes=[('F', junk), ('stt', tb)])
            S.op('dve', lambda tb=tb: dve.tensor_tensor(out=stt[:, tb * 4 + 2:tb * 4 + 3], in0=stt[:, tb * 4:tb * 4 + 1],
                                                        in1=stt[:, tb * 4 + 1:tb * 4 + 2], op=ALU.add),
                 reads=[('stt', tb)], writes=[('stt', tb)])
            S.op('act', lambda tb=tb: act.activation(out=stt[:, tb * 4 + 3:tb * 4 + 4], in_=stt[:, tb * 4 + 2:tb * 4 + 3], func=AF.Sqrt,
                                                     bias=par[:, NPAR - 1:NPAR], scale=1.0 / 1024),
                 reads=[('stt', tb), 'par'], writes=[('stt', tb)])
            S.op('dve', lambda tb=tb: dve.reciprocal(out=stt[:, tb * 4 + 3:tb * 4 + 4], in_=stt[:, tb * 4 + 3:tb * 4 + 4]),
                 reads=[('stt', tb)], writes=[('stt', tb)])
            dst = yp_d[tb * 128:(tb + 1) * 128, :] if tb < 4 else ys_d[(tb - 4) * 128:(tb - 3) * 128, :]
            for hf in range(2):
                o_ = ost[(tb % 2) * 2 + hf]
                S.op('dve', lambda hf=hf, pb=pb, tb=tb, o_=o_: dve.scalar_tensor_tensor(
                    out=fs(o_, 0, 512), in0=bank(pb + hf), scalar=stt[:, tb * 4 + 3:tb * 4 + 4], in1=fs(fgb[hf], 0, 512),
                    op0=ALU.mult, op1=ALU.mult),
                    reads=[('ps', pb + hf), ('stt', tb), ('F', fgb[hf])], writes=[('F', o_)])
                S.dma('sp', dst[:, hf * 512:(hf + 1) * 512], fs(o_, 0, 512), reads=[('F', o_)])
        S.finish()
    return nc


_CACHE = {}


def _host_consts():
    if 'c' in _CACHE:
        return _CACHE['c']
    ident = np.eye(128, dtype=np.float32)
    halfs = (1, 2, 4, 8)
    inv_p = np.zeros((4, 256), np.float32)
    for g, hf in enumerate(halfs):
        pos = np.arange(256)
        lo = np.clip(pos - hf, 0, 256)
        hi = np.clip(pos + hf, 0, 256)
        inv_p[g] = 1.0 / (hi - lo)
    per_core = []
    for i in range(8):
        j = i % 4
        t0 = 256 * j
        te = t0 - 32 + np.arange(320)
        valid = (te >= 0) & (te < 1024)
        vmask = np.broadcast_to(valid.astype(np.float32)[None, :], (128, 320)).copy()
        invc = np.zeros((4, PUW), np.float32)
        for g, hf in enumerate(halfs):
            invc[g, 8:264] = inv_p[g]
            invc[g, 272:528] = inv_p[g]
            lo = np.clip(te - hf, 0, 1024)
            hi = np.clip(te + hf, 0, 1024)
            cnt = np.maximum(hi - lo, 1)
            invc[g, 544:864] = np.where(valid, 1.0 / cnt, 0.0)
        invc = np.broadcast_to(invc.reshape(1, 4 * PUW), (128, 4 * PUW)).copy()
        kw0 = 4 * j - 6
        qe = np.arange(320)
        hs = qe // 32 + 1
        s = hs // 2
        r = 4 * j - 1 + s
        start = np.clip(r - 4, 0, 8)
        mall = np.zeros((128, 320), np.float32)
        mall[0:14] = NEG
        for kb in range(7):
            for a in range(2):
                kr = kw0 + 2 * kb + a
                ok = (kr >= 0) & (kr < 16) & (r >= 0) & (r < 16) & (kr >= start) & (kr < start + 8)
                mall[kb * 2 + a] = np.where(ok, 0.0, NEG)
        per_core.append((vmask, invc, mall))
    indall = np.zeros((128, 7, 128), np.float32)
    for kb in range(7):
        for a in range(2):
            indall[kb * 2 + a, kb, a * 64:(a + 1) * 64] = 1.0
    indall = indall.reshape(128, 896)
    _CACHE['c'] = (ident, per_core, indall)
    return _CACHE['c']


def _t2r_table(rpb, j):
    out = np.full((128, 8, QTOT), np.float32(NEG), np.float32)
    a = (np.arange(128) // 64)[:, None]
    kcol = (np.arange(128) % 64)[:, None]
    kw0 = 4 * j - 6
    for kb in range(7):
        qa, qb = QR[kb]
        q = np.arange(qa, qb)[None, :]
        hs = q // 32 + 1
        s_ = hs // 2
        r = 4 * j - 1 + s_
        qcol = (hs % 2) * 32 + q % 32
        kr = kw0 + 2 * kb + a
        start = np.clip(r - 4, 0, 8)
        col_start = np.clip(qcol - 8, 0, 48)
        ok = ((kr >= 0) & (kr < 16) & (r >= 0) & (r < 16) & (kr >= start) & (kr < start + 8)
              & (kcol >= col_start) & (kcol < col_start + 16))
        dr = np.clip(kr - r + 7, 0, 14)
        dc = np.clip(kcol - qcol, -15, 15) + 15
        for h in range(8):
            out[:, h, QOFF[kb]:QOFF[kb] + (qb - qa)] = np.where(ok, rpb[h][dr, dc], np.float32(NEG))
    return out


def _col(v):
    v = np.asarray(v, np.float32)
    return np.ascontiguousarray(v.reshape(-1, 128).T)


def _prepare(inputs):
    x_prompt = np.asarray(inputs['x_prompt'], np.float32)
    x_sample = np.asarray(inputs['x_sample'], np.float32)
    ident, per_core, indall = _host_consts()
    t2r_j = [_t2r_table(np.asarray(inputs['rpb'], np.float32)[0], j_) for j_ in range(4)]
    shared = {
        'ident': ident,
        'fgb': np.ascontiguousarray(np.broadcast_to(np.asarray(inputs['final_g'], np.float32)[None, :], (128, 1024))),
        'w_mod': np.ascontiguousarray(inputs['w_mod'], np.float32),
        'w_in_even': np.ascontiguousarray(inputs['w_in_even'][0], np.float32),
        'w_pool': np.ascontiguousarray(inputs['w_pool'][0], np.float32),
        'w_out_even': np.ascontiguousarray(inputs['w_out_even'][0], np.float32),
        'w_in_odd': np.ascontiguousarray(inputs['w_in_odd'][0], np.float32),
        'w_out_odd': np.ascontiguousarray(inputs['w_out_odd'][0], np.float32),
    }
    c = np.asarray(inputs['c'], np.float32)
    c_ctx = np.asarray(inputs['c_ctx'], np.float32)
    norm_g = np.asarray(inputs['norm_g'], np.float32)
    b_mod = np.asarray(inputs['b_mod'], np.float32)
    ckt, cvz = [], []
    for b_ in range(2):
        ck_ = np.asarray(inputs['cache_k'][b_, 0], np.float32)
        cv_ = np.asarray(inputs['cache_v'][b_, 0], np.float32)
        ckt.append(np.ascontiguousarray(ck_.transpose(1, 0, 2).reshape(256, 512)))
        vt = cv_.transpose(1, 0, 2)
        z_ = np.zeros((2, 256, 8, 64), np.float32)
        z_[0, :, 0::2, :] = vt[:, 0::2, :]
        z_[1, :, 1::2, :] = vt[:, 1::2, :]
        cvz.append(z_.reshape(2, 256, 512))
    in_maps = []
    for i in range(8):
        b, j = i // 4, i % 4
        par = np.zeros((128, NPAR), np.float32)
        cc = np.stack([_col(c_ctx), _col(c[b])], axis=2)
        par[:, PC_COND:PC_COND + 16] = cc.reshape(128, 16)
        for l in range(2):
            g2 = np.repeat(_col(norm_g[l])[:, :, None], 2, axis=2)
            par[:, PC_G + l * 16:PC_G + (l + 1) * 16] = g2.reshape(128, 16)
            b2 = np.repeat(_col(b_mod[l])[:, :, None], 2, axis=2)
            par[:, PC_BMOD + l * 48:PC_BMOD + (l + 1) * 48] = b2.reshape(128, 48)
        par[:, PC_PSC:PC_PSC + 4] = _col(inputs['pool_scale'][0])
        par[:, PC_CC:PC_CC + 12] = np.stack([_col(inputs['conv_c'][0][t]) for t in range(3)], axis=2).reshape(128, 12)
        par[:, PC_CD:PC_CD + 124] = np.stack([_col(inputs['conv_d'][0][t]) for t in range(31)], axis=2).reshape(128, 124)
        par[:, PC_CDB:PC_CDB + 4] = _col(inputs['conv_d_b'][0])
        par[:, PC_LNG:PC_LNG + 4] = _col(inputs['ln_g'][0])
        par[:, PC_LNB:PC_LNB + 4] = _col(inputs['ln_b'][0])
        par[:, PC_FG:PC_FG + 8] = _col(inputs['final_g'])
        par[:, NPAR - 1] = EPS
        xp = np.ascontiguousarray(x_prompt[2 * i:2 * i + 2].reshape(512, 1024))
        xw = np.zeros((896, 1024), np.float32)
        kw0 = 4 * j - 6
        lo_r, hi_r = max(kw0, 0), min(kw0 + 14, 16)
        xw[(lo_r - kw0) * 64:(hi_r - kw0) * 64] = x_sample[b, lo_r * 64:hi_r * 64]
        vmask, invc, mall = per_core[i]
        m = dict(shared)
        m.update({'xpT': np.ascontiguousarray(xp.T), 'xwT': np.ascontiguousarray(xw.T),
                  'ck': ckt[b], 'cv': cvz[b],
                  'params': par, 'vmask': vmask, 'invcnt': invc, 't2r': t2r_j[j]})
        in_maps.append(m)
    return in_maps


def kernel(**inputs):
    in_maps = _prepare(inputs)
    if 'nc' not in _CACHE:
        _CACHE['nc'] = build_program()
    nc = _CACHE['nc']
    res = run_bass_kernel_spmd(nc, in_maps, core_ids=list(range(8)))
    R = res.results
    y_prompt = np.concatenate([R[i]['yp'].reshape(2, 256, 1024) for i in range(8)], axis=0)
    y_sample = np.stack([np.concatenate([R[b * 4 + j]['ys'] for j in range(4)], axis=0) for b in range(2)], axis=0)
    nk = np.concatenate([R[i]['nk'].reshape(8, 64, 2, 256).transpose(2, 0, 3, 1).reshape(2, 1, 8, 256, 64) for i in range(8)], axis=0)
    nv = np.concatenate([R[i]['nv'].reshape(2, 256, 8, 64).transpose(0, 2, 1, 3).reshape(2, 1, 8, 256, 64) for i in range(8)], axis=0)
    return (y_prompt.astype(np.float32), y_sample.astype(np.float32), nk.astype(np.float32), nv.astype(np.float32))
```
